# Optimizing a Trainium2 kernel written in Bass

```python
import jax, jax.numpy as jnp
from jax import lax
import numpy as np

D_MODEL = 1024
BATCH = 2
SEQ = 8192
DEPTH = 4
DEC_BATCH = 128
DEC_SEQ = 8
PAST_LEN = 8192
PAGE_SIZE = 128

N_MIXERS = 2
N_HEADS = 16
N_KV_HEADS = 4
HEAD_DIM = 64
Q_PER_KV = N_HEADS // N_KV_HEADS
WINDOW = 128
BLOCK = 128
ROT_DIM = HEAD_DIM // 4
ROPE_THETA = 500000.0
CHUNK = 128
SGU_GROUPS = 4
D_SGU = 2 * D_MODEL
SGU_GROUP_DIM = D_SGU // SGU_GROUPS
D_FF = 2816
CONV_W = 3
N_ATTN_LAYERS = (DEPTH + 1) // 2
N_SGU_LAYERS = DEPTH // 2
QKV_DIM = (N_HEADS + 2 * N_KV_HEADS) * HEAD_DIM
EPS = 1e-6

kernel_name = 'hybrid_swa_sink_sgu_convffn_step'


def rmsnorm(x, g):
    xf = x.astype(jnp.float32)
    y = xf * lax.rsqrt(jnp.mean(xf * xf, -1, keepdims=True) + EPS)
    return (y * g.astype(jnp.float32)).astype(x.dtype)


def layernorm(x, g, b):
    xf = x.astype(jnp.float32)
    mu = jnp.mean(xf, -1, keepdims=True)
    xc = xf - mu
    y = xc * lax.rsqrt(jnp.mean(xc * xc, -1, keepdims=True) + EPS)
    return (y * g.astype(jnp.float32) + b.astype(jnp.float32)).astype(x.dtype)


def rope(x, pos):
    half = ROT_DIM // 2
    inv = ROPE_THETA ** (-jnp.arange(0, ROT_DIM, 2, dtype=jnp.float32) / ROT_DIM)
    ang = pos.astype(jnp.float32)[:, None] * inv[None, :]
    cos = jnp.cos(ang)[:, None, :]
    sin = jnp.sin(ang)[:, None, :]
    xf = x.astype(jnp.float32)
    x1 = xf[..., :half]
    x2 = xf[..., half:ROT_DIM]
    out = jnp.concatenate([x1 * cos - x2 * sin, x2 * cos + x1 * sin, xf[..., ROT_DIM:]], -1)
    return out.astype(x.dtype)


def qkv_rope(h, w, b, pos):
    B, T = h.shape[:2]
    y = h @ w + b
    nq = N_HEADS * HEAD_DIM
    nk = N_KV_HEADS * HEAD_DIM
    q = y[..., :nq].reshape(B, T, N_HEADS, HEAD_DIM)
    k = y[..., nq:nq + nk].reshape(B, T, N_KV_HEADS, HEAD_DIM)
    v = y[..., nq + nk:].reshape(B, T, N_KV_HEADS, HEAD_DIM)
    return rope(q, pos), rope(k, pos), v


def sink_softmax(s, sink, mask):
    s = jnp.where(mask, s, -jnp.inf)
    sk = sink.astype(jnp.float32)[:, :, None, None]
    m = jnp.maximum(s.max(-1, keepdims=True), sk)
    p = jnp.exp(s - m)
    return p / (p.sum(-1, keepdims=True) + jnp.exp(sk - m))


def window_attn_prompt(q, k, v, sink):
    B, S = q.shape[:2]
    nb = S // BLOCK
    qb = q.reshape(B, nb, BLOCK, N_KV_HEADS, Q_PER_KV, HEAD_DIM)
    kb = k.reshape(B, nb, BLOCK, N_KV_HEADS, HEAD_DIM)
    vb = v.reshape(B, nb, BLOCK, N_KV_HEADS, HEAD_DIM)

    def with_prev(xb):
        prev = jnp.pad(xb, ((0, 0), (1, 0), (0, 0), (0, 0), (0, 0)))[:, :-1]
        return jnp.concatenate([prev, xb], axis=2)

    kc, vc = with_prev(kb), with_prev(vb)
    s = jnp.einsum('bnqkgd,bnskd->bnkgqs', qb, kc, preferred_element_type=jnp.float32) * (HEAD_DIM ** -0.5)
    i = jnp.arange(BLOCK)[:, None]
    j = jnp.arange(2 * BLOCK)[None, :]
    diff = BLOCK + i - j
    band = (diff >= 0) & (diff <= WINDOW)
    valid = (jnp.arange(nb) > 0)[:, None, None] | (j >= BLOCK)[None]
    mask = (band[None] & valid)[None, :, None, None]
    p = sink_softmax(s, sink.reshape(N_KV_HEADS, Q_PER_KV), mask)
    o = jnp.einsum('bnkgqs,bnskd->bnqkgd', p.astype(vc.dtype), vc)
    return o.reshape(B, S, N_HEADS * HEAD_DIM)


def window_attn_sample(q, k_new, v_new, k_cache, v_cache, sink):
    DB, T = q.shape[:2]
    W = k_cache.shape[1]
    kc = jnp.concatenate([k_cache.astype(k_new.dtype), k_new], 1)
    vc = jnp.concatenate([v_cache.astype(v_new.dtype), v_new], 1)
    qg = q.reshape(DB, T, N_KV_HEADS, Q_PER_KV, HEAD_DIM)
    s = jnp.einsum('btkgd,bskd->bkgts', qg, kc, preferred_element_type=jnp.float32) * (HEAD_DIM ** -0.5)
    t = jnp.arange(T)[:, None]
    j = jnp.arange(W + T)[None, :]
    diff = W + t - j
    mask = (diff >= 0) & (diff <= WINDOW)
    p = sink_softmax(s, sink.reshape(N_KV_HEADS, Q_PER_KV), mask)
    o = jnp.einsum('bkgts,bskd->btkgd', p.astype(vc.dtype), vc)
    return o.reshape(DB, T, N_HEADS * HEAD_DIM), kc[:, -W:], vc[:, -W:]


def sgu_mix(h, w_in, b_in, ln_g, ln_b, w_sp, b_sp, w_out):
    B, T = h.shape[:2]
    tc = min(T, CHUNK)
    z = jax.nn.gelu(h @ w_in + b_in)
    u, v = jnp.split(z, 2, axis=-1)
    v = layernorm(v, ln_g, ln_b)
    tri = jnp.tril(jnp.ones((CHUNK, CHUNK), w_sp.dtype))
    wm = (w_sp * tri)[:, :tc, :tc]
    vb = v.reshape(B, T // tc, tc, SGU_GROUPS, SGU_GROUP_DIM)
    mixed = jnp.einsum('gts,bnsgc->bntgc', wm, vb) + b_sp[:, :tc].T[None, None, :, :, None]
    out = (u * mixed.reshape(B, T, D_SGU)) @ w_out
    return out, v


def conv_ffn(h, past, w_up, conv_w, conv_b, w_down):
    T = h.shape[1]
    a = h @ w_up
    ap = jnp.concatenate([past.astype(a.dtype), a], 1)
    conv = sum(conv_w[j] * ap[:, j:j + T] for j in range(CONV_W)) + conv_b
    g, u = jnp.split(conv, 2, axis=-1)
    y = (jax.nn.silu(g) * u) @ w_down
    return y, ap[:, T:]


def trunk(x, c, start, cache_k, cache_v, state_conv, w_ada, b_ada, norm_mix, norm_ffn,
          w_qkv, b_qkv, attn_sink, w_o, w_sgu_in, b_sgu_in, sgu_ln_g, sgu_ln_b,
          w_spatial, b_spatial, w_sgu_out, w_up, conv_w, conv_b, w_down, norm_final):
    B, T = x.shape[:2]
    pos = start + jnp.arange(T)
    new_k, new_v, new_conv, new_sgu = [], [], [], []
    for l in range(DEPTH):
        mod = (jax.nn.silu(c) @ w_ada[l] + b_ada[l]).astype(x.dtype)[:, None, :]
        sh1, sc1, g1, sh2, sc2, g2 = jnp.split(mod, 6, axis=-1)
        h = rmsnorm(x, norm_mix[l]) * (1 + sc1) + sh1
        idx = l // N_MIXERS
        if l % N_MIXERS == 0:
            q, k, v = qkv_rope(h, w_qkv[idx], b_qkv[idx], pos)
            if cache_k is None:
                o = window_attn_prompt(q, k, v, attn_sink[idx])
                wkeep = min(WINDOW, T)
                nk, nv = k[:, T - wkeep:], v[:, T - wkeep:]
            else:
                o, nk, nv = window_attn_sample(q, k, v, cache_k[idx], cache_v[idx], attn_sink[idx])
            new_k.append(nk)
            new_v.append(nv)
            mix = o @ w_o[idx]
        else:
            mix, vrows = sgu_mix(h, w_sgu_in[idx], b_sgu_in[idx], sgu_ln_g[idx], sgu_ln_b[idx],
                                 w_spatial[idx], b_spatial[idx], w_sgu_out[idx])
            if cache_k is not None:
                new_sgu.append(vrows)
        x = x + g1 * mix
        h = rmsnorm(x, norm_ffn[l]) * (1 + sc2) + sh2
        past = jnp.zeros((B, CONV_W - 1, 2 * D_FF), x.dtype) if state_conv is None else state_conv[l]
        f, st = conv_ffn(h, past, w_up[l], conv_w[l], conv_b[l], w_down[l])
        new_conv.append(st)
        x = x + g2 * f
    y = rmsnorm(x, norm_final)
    return y, jnp.stack(new_k), jnp.stack(new_v), jnp.stack(new_conv), new_sgu


def setup_inputs(seed: int = 0) -> dict:
    key = jax.random.key(seed)
    ks = jax.random.split(key, 32)
    f32 = jnp.float32

    def nrm(k, shape, scale):
        return jax.random.normal(k, shape, f32) * scale

    cache_win = min(WINDOW, PAST_LEN)
    return {
        'x_prompt': nrm(ks[0], (BATCH, SEQ, D_MODEL), 1.0),
        'x_sample': nrm(ks[1], (DEC_BATCH, DEC_SEQ, D_MODEL), 1.0),
        'c_prompt': nrm(ks[2], (BATCH, D_MODEL), 1.0),
        'c_sample': nrm(ks[3], (DEC_BATCH, D_MODEL), 1.0),
        'cache_k': nrm(ks[4], (N_ATTN_LAYERS, DEC_BATCH, cache_win, N_KV_HEADS, HEAD_DIM), 1.0),
        'cache_v': nrm(ks[5], (N_ATTN_LAYERS, DEC_BATCH, cache_win, N_KV_HEADS, HEAD_DIM), 1.0),
        'state_conv': nrm(ks[6], (DEPTH, DEC_BATCH, CONV_W - 1, 2 * D_FF), 1.0),
        'w_ada': nrm(ks[7], (DEPTH, D_MODEL, 6 * D_MODEL), 0.5 * D_MODEL ** -0.5),
        'b_ada': nrm(ks[8], (DEPTH, 6 * D_MODEL), 0.02),
        'norm_mix': 1.0 + nrm(ks[9], (DEPTH, D_MODEL), 0.05),
        'norm_ffn': 1.0 + nrm(ks[10], (DEPTH, D_MODEL), 0.05),
        'w_qkv': nrm(ks[11], (N_ATTN_LAYERS, D_MODEL, QKV_DIM), D_MODEL ** -0.5),
        'b_qkv': nrm(ks[12], (N_ATTN_LAYERS, QKV_DIM), 0.02),
        'attn_sink': nrm(ks[13], (N_ATTN_LAYERS, N_HEADS), 0.5),
        'w_o': nrm(ks[14], (N_ATTN_LAYERS, N_HEADS * HEAD_DIM, D_MODEL), (N_HEADS * HEAD_DIM) ** -0.5),
        'w_sgu_in': nrm(ks[15], (N_SGU_LAYERS, D_MODEL, 2 * D_SGU), D_MODEL ** -0.5),
        'b_sgu_in': nrm(ks[16], (N_SGU_LAYERS, 2 * D_SGU), 0.02),
        'sgu_ln_g': 1.0 + nrm(ks[17], (N_SGU_LAYERS, D_SGU), 0.05),
        'sgu_ln_b': nrm(ks[18], (N_SGU_LAYERS, D_SGU), 0.02),
        'w_spatial': nrm(ks[19], (N_SGU_LAYERS, SGU_GROUPS, CHUNK, CHUNK), CHUNK ** -0.5),
        'b_spatial': 1.0 + nrm(ks[20], (N_SGU_LAYERS, SGU_GROUPS, CHUNK), 0.1),
        'w_sgu_out': nrm(ks[21], (N_SGU_LAYERS, D_SGU, D_MODEL), D_SGU ** -0.5),
        'w_up': nrm(ks[22], (DEPTH, D_MODEL, 2 * D_FF), D_MODEL ** -0.5),
        'conv_w': nrm(ks[23], (DEPTH, CONV_W, 2 * D_FF), CONV_W ** -0.5),
        'conv_b': nrm(ks[24], (DEPTH, 2 * D_FF), 0.02),
        'w_down': nrm(ks[25], (DEPTH, D_FF, D_MODEL), D_FF ** -0.5),
        'norm_final': 1.0 + nrm(ks[26], (D_MODEL,), 0.05),
    }


def reference(x_prompt, x_sample, c_prompt, c_sample, cache_k, cache_v, state_conv,
              w_ada, b_ada, norm_mix, norm_ffn, w_qkv, b_qkv, attn_sink, w_o,
              w_sgu_in, b_sgu_in, sgu_ln_g, sgu_ln_b, w_spatial, b_spatial, w_sgu_out,
              w_up, conv_w, conv_b, w_down, norm_final):
    y_prompt, k_p, v_p, conv_p, _ = trunk(
        x_prompt, c_prompt, 0, None, None, None, w_ada, b_ada, norm_mix, norm_ffn,
        w_qkv, b_qkv, attn_sink, w_o, w_sgu_in, b_sgu_in, sgu_ln_g, sgu_ln_b,
        w_spatial, b_spatial, w_sgu_out, w_up, conv_w, conv_b, w_down, norm_final)
    y_sample, k_s, v_s, conv_s, sgu_rows = trunk(
        x_sample, c_sample, PAST_LEN, cache_k, cache_v, state_conv, w_ada, b_ada, norm_mix, norm_ffn,
        w_qkv, b_qkv, attn_sink, w_o, w_sgu_in, b_sgu_in, sgu_ln_g, sgu_ln_b,
        w_spatial, b_spatial, w_sgu_out, w_up, conv_w, conv_b, w_down, norm_final)
    sgu_v_s = jnp.stack(sgu_rows)
    return (y_prompt, y_sample, k_p, v_p, conv_p, k_s, v_s, conv_s, sgu_v_s)
```

```python
import numpy as np
from contextlib import ExitStack
import concourse.bass as bass
import concourse.mybir as mybir
from concourse.bass_utils import run_bass_kernel_spmd

F32 = mybir.dt.float32
BF16 = mybir.dt.bfloat16
ALU = mybir.AluOpType
AF = mybir.ActivationFunctionType
AX = mybir.AxisListType

D = 1024
NPB = 20
SB = 4
NST = NPB // SB
NBMAX = SB + 1
DFF = 2816
NFC = 44
EPS = 1e-6
NEG = -30000.0
PROC_START = [0, 1920, 3840, 5632]
OWN_START = [0, 2560, 4480, 6400]
OWN_END = [2560, 4480, 6400, 8192]
NSLOT = 5
PREFETCH = 4


class Buf:
    __slots__ = ("name", "w", "r")

    def __init__(self, name=""):
        self.name = name
        self.w = None
        self.r = []


class Tile:
    __slots__ = ("t", "b")

    def __init__(self, t, b=None):
        self.t = t
        self.b = b if b is not None else Buf()


class Entry:
    __slots__ = ("waits", "fn", "signal", "dma_sem")

    def __init__(self, waits, fn):
        self.waits = waits
        self.fn = fn
        self.signal = False
        self.dma_sem = None


COMPUTE = ("pe", "act", "dve", "pool")
ALLENG = ("pe", "act", "dve", "pool", "sp")


class Sched:
    def __init__(self, nc, stack, dma_ring=8):
        self.nc = nc
        self.streams = {k: [] for k in ALLENG}
        self.esem = {k: stack.enter_context(nc.semaphore("prog_" + k)) for k in COMPUTE}
        self.base = {k: 0 for k in COMPUTE}
        self.dsem = {}
        self.dcount = {}
        self.dlast = {}
        for q in ("sp", "pool", "act"):
            self.dsem[q] = [stack.enter_context(nc.semaphore(f"dma_{q}_{i}")) for i in range(dma_ring)]
            self.dcount[q] = 0
            self.dlast[q] = [None] * dma_ring
        self.seen = {k: {} for k in ALLENG}
        self.ring = dma_ring
        self.phase = 0
        self.hold = []
        self.all_out = []

    def _collect(self, eng, reads, writes):
        evs = []
        for b in reads:
            if b.w is not None:
                evs.append(b.w)
        for b in writes:
            if b.w is not None:
                evs.append(b.w)
            evs.extend(b.r)
        return self._reduce(eng, evs)

    def _reduce(self, eng, evs):
        best = {}
        for ev in evs:
            if ev is None:
                continue
            if ev[0] == "e":
                _, src, idx, ph = ev
                if ph != self.phase:
                    continue
                if src == eng and eng == "pe":
                    continue
                key = ("e", src)
                if key not in best or best[key][2] < idx:
                    best[key] = ev
            else:
                _, q, slot, val = ev
                key = ("d", q, slot)
                if key not in best or best[key][3] < val:
                    best[key] = ev
        waits = []
        for key, ev in best.items():
            v = ev[2] if ev[0] == "e" else ev[3]
            pk = (self.phase,) + key if ev[0] == "e" else key
            if self.seen[eng].get(pk, -1) >= v:
                continue
            self.seen[eng][pk] = v
            if ev[0] == "e":
                self.streams[ev[1]][ev[2]].signal = True
            waits.append(ev)
        return waits

    def op(self, eng, fn, reads=(), writes=()):
        waits = self._collect(eng, reads, writes)
        st = self.streams[eng]
        st.append(Entry(waits, fn))
        ev = ("e", eng, len(st) - 1, self.phase)
        for b in reads:
            b.r.append(ev)
        for b in writes:
            b.w = ev
            b.r = []
        return ev

    def dma(self, q, fn, reads=(), writes=(), hold=True, out=False):
        evs = []
        for b in reads:
            if b.w is not None:
                evs.append(b.w)
        for b in writes:
            if b.w is not None:
                evs.append(b.w)
            evs.extend(b.r)
        i = self.dcount[q]
        self.dcount[q] += 1
        slot = i % self.ring
        val = 16 * (i // self.ring + 1)
        if self.dlast[q][slot] is not None:
            evs.append(self.dlast[q][slot])
        waits = self._reduce(q, evs)
        ent = Entry(waits, fn)
        ent.dma_sem = self.dsem[q][slot]
        self.streams[q].append(ent)
        ev = ("d", q, slot, val)
        self.dlast[q][slot] = ev
        for b in reads:
            b.r.append(ev)
        for b in writes:
            b.w = ev
            b.r = []
        if hold:
            self.hold.append(ev)
        if out:
            self.all_out.append(ev)
        return ev

    def flush(self, final=False):
        nc = self.nc
        lasts = []
        for k in COMPUTE:
            st = self.streams[k]
            idx = None
            for i in range(len(st) - 1, -1, -1):
                if st[i].fn is not None and st[i].dma_sem is None:
                    idx = i
                    break
            if idx is not None:
                lasts.append(("e", k, idx, self.phase))
        extra = list(self.hold)
        if final:
            extra += self.all_out
            for q in self.dlast:
                extra += [ev for ev in self.dlast[q] if ev is not None]
        for k in ALLENG:
            waits = self._reduce(k, lasts + extra)
            self.streams[k].append(Entry(waits, None))
        self.hold = []
        counts = {}
        for k in COMPUTE:
            c = self.base[k]
            arr = []
            for ent in self.streams[k]:
                if ent.signal and ent.dma_sem is None and ent.fn is not None:
                    c += 1
                arr.append(c)
            counts[k] = arr

        def resolve(ev):
            if ev[0] == "e":
                return self.esem[ev[1]], counts[ev[1]][ev[2]]
            return self.dsem[ev[1]][ev[2]], ev[3]

        def replay(k, e):
            for ent in self.streams[k]:
                for ev in ent.waits:
                    s, v = resolve(ev)
                    e.wait_ge(s, v)
                if ent.fn is None:
                    continue
                ins = ent.fn(e)
                if ent.dma_sem is not None:
                    ins.then_inc(ent.dma_sem, 16)
                elif ent.signal:
                    ins.then_inc(self.esem[k], 1)

        with nc.Block() as block:
            @block.tensor
            def _(e):
                replay("pe", e)

            @block.scalar
            def _(e):
                replay("act", e)

            @block.vector
            def _(e):
                replay("dve", e)

            @block.gpsimd
            def _(e):
                replay("pool", e)

            @block.sync
            def _(e):
                replay("sp", e)

        for k in COMPUTE:
            if counts[k]:
                self.base[k] = counts[k][-1]
        self.streams = {k: [] for k in ALLENG}
        self.phase += 1


_IN_SPECS = [
    ("xp", [NPB * 128, D]), ("xs", [128, D]), ("c17", [17, D]),
    ("ck", [2, 16, 128, 256]), ("cv", [2, 16, 128, 256]), ("sconv", [4, 32, 2 * DFF]),
    ("wt_ada", [4, 12, 128, 4096]), ("b_ada", [4, 6 * D]), ("rows8", [8, D]), ("normf", [1, D]),
    ("wt_qkv", [2, 3, 128, 4096]), ("b_qkv", [2, 1536]), ("sinkp", [2, 16]), ("wt_o", [2, 2, 128, 4096]),
    ("wt_in", [2, 8, 128, 4096]), ("b_in", [2, 4096]), ("lnrows", [4, 2048]),
    ("wspT", [2, 4, 128, 128]), ("wspST", [2, 4, 128, 128]), ("bspP", [128, 8]), ("bspS", [128, 8]),
    ("wt_out", [2, 4, 128, 4096]), ("wt_up", [4, 11, 128, 4096]), ("convrows", [16, 2 * DFF]), ("wt_down", [4, 6, 128, 4096]),
    ("cossin", [(NPB + 1) * 128, 16]), ("ident", [128, 128]), ("maskP", [128, 256]), ("maskP0", [128, 256]),
    ("maskS", [128, 256]), ("trilT", [128, 128]), ("bmask", [128, 16 * 128]),
]
_OUT_SPECS = [
    ("yp", [NPB * 128, D]), ("ys", [128, D]), ("kp", [2, 128, 256]), ("vp", [2, 128, 256]),
    ("convp", [4, 2, 2 * DFF]), ("ks", [2, 16, 128, 256]), ("vs", [2, 16, 128, 256]),
    ("convs", [4, 32, 2 * DFF]), ("sguv", [2, 128, 2048]),
]


def weight_sequence():
    seq = []
    for l in range(4):
        for nt in range(12):
            seq.append(("ada", l, nt))
    for st in range(NST):
        for l in range(4):
            if l % 2 == 0:
                for nt in range(3):
                    seq.append(("qkv", l // 2, nt))
                for nt in range(2):
                    seq.append(("wo", l // 2, nt))
            else:
                for nt in range(4):
                    seq.append(("win", l // 2, 4 + nt))
                for g in range(4):
                    seq.append(("win", l // 2, g))
                    seq.append(("win", l // 2, 4 + g))
                for kh in range(2):
                    for nt in range(2):
                        seq.append(("wout", l // 2, kh * 2 + nt))
            for jj in range(11):
                seq.append(("wup", l, jj))
            for kg in range(3):
                for nt in range(2):
                    seq.append(("wdown", l, kg * 2 + nt))
    return seq


class _Stop(Exception):
    pass


def build_program(dbg=None):
    nc = bass.Bass("TRN2", target_bir_lowering=False)
    I = {n: nc.dram_tensor(n, s, F32, kind="ExternalInput").ap() for n, s in _IN_SPECS}
    O = {n: nc.dram_tensor(n, s, F32, kind="ExternalOutput").ap() for n, s in _OUT_SPECS}
    if dbg is not None:
        O["dbg"] = nc.dram_tensor("dbg", [128, 16384], F32, kind="ExternalOutput").ap()
    try:
        _build_body(nc, I, O, dbg)
    except _Stop:
        pass
    return nc


def _build_body(nc, I, O, dbg):

    with ExitStack() as gs:
        S = Sched(nc, gs)

        uid = [0]

        def uname(name):
            uid[0] += 1
            return f"s{uid[0]}_{name}"

        def SBT(stack, name, shape, dt):
            return Tile(stack.enter_context(nc.sbuf_tensor(uname(name), shape, dt)), Buf(name))

        dbg_off = [0]

        def dump(ap, ncols, buf, bf=False):
            if bf:
                with ExitStack() as dst_:
                    t = SBT(dst_, "dbgt", [128, ncols], F32)
                    S.op("dve", lambda e: e.tensor_copy(out=t.t[:], in_=ap), reads=[buf], writes=[t.b])
                    o = dbg_off[0]
                    S.dma("sp", lambda e: e.dma_start(out=O["dbg"][:, o:o + ncols], in_=t.t[:]), reads=[t.b], out=True)
                    S.flush()
            else:
                o = dbg_off[0]
                S.dma("sp", lambda e: e.dma_start(out=O["dbg"][:, o:o + ncols], in_=ap), reads=[buf], out=True)
            dbg_off[0] += ncols

        def stop():
            S.flush(final=True)
            print("SEM COUNTS", S.base, S.dcount)
            raise _Stop()

        x = [SBT(gs, f"x{i}", [128, D], F32) for i in range(NBMAX)]
        hT = gs.enter_context(nc.sbuf_tensor("s_hT", [128, 8, NBMAX * 128], BF16))
        hTb = [Buf(f"hT{i}") for i in range(NBMAX)]
        slots = [SBT(gs, f"ws{i}", [128, 8, 512], BF16) for i in range(NSLOT)]
        brow = [SBT(gs, f"brow{i}", [1, 512], BF16) for i in range(NSLOT)]
        pb = [Tile(gs.enter_context(nc.psum_tensor(f"pb{i}", [128, 512], F32)), Buf(f"pb{i}")) for i in range(8)]
        identf = SBT(gs, "identf", [128, 128], F32)
        identb = SBT(gs, "identb", [128, 128], BF16)
        ones = SBT(gs, "ones", [1, 128], BF16)
        maskP = SBT(gs, "maskP", [128, 256], F32)
        maskP0 = SBT(gs, "maskP0", [128, 256], F32)
        maskS = SBT(gs, "maskS", [128, 256], F32)
        wm = SBT(gs, "wm", [128, 2, 2, 4, 128], BF16)
        bsp = SBT(gs, "bsp", [128, 2, 8], F32)
        sinkB = SBT(gs, "sinkB", [128, 2, 16], F32)
        cs = SBT(gs, "cs", [128, NPB + 1, 16], F32)
        modT = SBT(gs, "modT", [128, 4, 48, 17], F32)
        Amod = SBT(gs, "Amod", [128, 4, 2, 8, 17], F32)
        cwT = SBT(gs, "cwT", [128, NFC, 16], F32)
        stP = SBT(gs, "stP", [128, 4, NFC, 2], F32)
        stPb = [[Buf(f"stP{l_}_{c_}") for c_ in range(NFC)] for l_ in range(4)]
        stP_all = [b_ for row in stPb for b_ in row]
        normfB = SBT(gs, "normfB", [128, D], F32)
        G = [[SBT(gs, f"G{k}{w}", [128, D], F32) for w in range(2)] for k in range(2)]
        kTp = [SBT(gs, f"kTp{a}", [128, 2, 128], BF16) for a in range(2)]
        Vpp = [SBT(gs, f"Vpp{a}", [128, 4, 128], BF16) for a in range(2)]

        wseq = weight_sequence()
        wstate = {"issued": 0, "next": 0}

        def w_issue(i):
            kind, l, idx = wseq[i]
            sl = slots[i % NSLOT]
            br = brow[i % NSLOT]
            dst = sl.t

            def q(dst_ap, src_ap, tile=sl):
                S.dma("pool", lambda e, d=dst_ap, s=src_ap: e.dma_start(out=d, in_=s), writes=[tile.b], hold=False)

            def tile(name, nk=8):
                q(dst[:, 0:nk, :], I[name][l][idx][:, 0:nk * 512].rearrange("p (k n) -> p k n", n=512))

            if kind == "ada":
                tile("wt_ada")
                q(br.t[:, :], I["b_ada"][l:l + 1, idx * 512:(idx + 1) * 512], br)
            elif kind == "qkv":
                tile("wt_qkv")
                q(br.t[:, :], I["b_qkv"][l:l + 1, idx * 512:(idx + 1) * 512], br)
            elif kind == "wo":
                tile("wt_o")
            elif kind == "win":
                tile("wt_in")
                q(br.t[:, :], I["b_in"][l:l + 1, idx * 512:(idx + 1) * 512], br)
            elif kind == "wout":
                tile("wt_out")
            elif kind == "wup":
                tile("wt_up")
            elif kind == "wdown":
                tile("wt_down", 8 if idx // 2 < 2 else 6)

        def w_next(kind, l, idx):
            i = wstate["next"]
            assert wseq[i] == (kind, l, idx), (wseq[i], kind, l, idx)
            while wstate["issued"] < min(len(wseq), i + PREFETCH + 1):
                w_issue(wstate["issued"])
                wstate["issued"] += 1
            wstate["next"] += 1
            return slots[i % NSLOT], brow[i % NSLOT]

        def mm(out_ap, lhsT, rhs, start, stop, reads, out_buf):
            return S.op("pe", lambda e: e.matmul(out=out_ap, lhsT=lhsT, rhs=rhs, start=start, stop=stop),
                        reads=reads, writes=[out_buf])

        def tr(out_ap, in_ap, ident_ap, reads, out_buf):
            return S.op("pe", lambda e: e.transpose(out=out_ap, in_=in_ap, identity=ident_ap), reads=reads, writes=[out_buf])

        def bfview(p):
            return p.t[:].bitcast(BF16).rearrange("p (j t) -> p j t", t=128)

        with ExitStack() as ph:
            def ld(tile, src, q="sp"):
                S.dma(q, lambda e: e.dma_start(out=tile.t[:], in_=src), writes=[tile.b])

            ld(identf, I["ident"])
            ld(maskP, I["maskP"])
            ld(maskP0, I["maskP0"])
            ld(maskS, I["maskS"])
            S.dma("sp", lambda e: e.dma_start(out=cs.t[:], in_=I["cossin"].rearrange("(b p) c -> p b c", p=128)), writes=[cs.b])
            S.dma("sp", lambda e: e.dma_start(out=bsp.t[:, 0, :], in_=I["bspP"]), writes=[bsp.b])
            S.dma("sp", lambda e: e.dma_start(out=bsp.t[:, 1, :], in_=I["bspS"]), writes=[bsp.b])
            S.dma("sp", lambda e: e.dma_start(out=sinkB.t[:].rearrange("p a h -> p (a h)"),
                                              in_=I["sinkp"].rearrange("a h -> (a h)").partition_broadcast(128)), writes=[sinkB.b])
            S.dma("sp", lambda e: e.dma_start(out=normfB.t[:], in_=I["normf"][0].partition_broadcast(128)), writes=[normfB.b])
            S.op("dve", lambda e: e.memset(ones.t[:], 1.0), writes=[ones.b])
            S.op("dve", lambda e: e.tensor_copy(out=identb.t[:], in_=identf.t[:]), reads=[identf.b], writes=[identb.b])
            S.op("dve", lambda e: e.memset(stP.t[:], 0.0), writes=stP_all)
            for a in range(2):
                S.op("dve", lambda e, a=a: e.memset(Vpp[a].t[:], 0.0), writes=[Vpp[a].b])
                S.op("dve", lambda e, a=a: e.memset(kTp[a].t[:], 0.0), writes=[kTp[a].b])
            tril = SBT(ph, "tril", [128, 128], F32)
            ld(tril, I["trilT"])
            wsp_f = SBT(ph, "wsp_f", [128, 2, 2, 4, 128], F32)
            for kind, nm in enumerate(("wspT", "wspST")):
                S.dma("sp", lambda e, kind=kind, nm=nm: e.dma_start(out=wsp_f.t[:, kind], in_=I[nm].rearrange("l g s t -> s l g t")),
                      writes=[wsp_f.b])
            S.op("dve", lambda e: e.tensor_tensor(out=wm.t[:].rearrange("p a l g t -> p (a l g) t"),
                                                  in0=wsp_f.t[:].rearrange("p a l g t -> p (a l g) t"),
                                                  in1=tril.t[:].unsqueeze(1).to_broadcast([128, 16, 128]), op=ALU.mult),
                 reads=[wsp_f.b, tril.b], writes=[wm.b])
            if dbg == "s1":
                dump(wm.t[:].rearrange("p a l g t -> p (a l g t)"), 2048, wm.b, bf=True)
                dump(cs.t[:].rearrange("p b c -> p (b c)"), 336, cs.b)
                dump(sinkB.t[:].rearrange("p a h -> p (a h)"), 32, sinkB.b)
                stop()
            rows8 = SBT(ph, "rows8", [8, D], F32)
            ld(rows8, I["rows8"])
            nrmT = SBT(ph, "nrmT", [128, 8, 8], F32)
            for j in range(8):
                mm(pb[0].t[:, j * 8:(j + 1) * 8], rows8.t[:, j * 128:(j + 1) * 128], identf.t[0:8, 0:8], True, True, [rows8.b, identf.b], pb[0].b)
            S.op("act", lambda e: e.activation(out=nrmT.t[:].rearrange("p j r -> p (j r)"), in_=pb[0].t[:, 0:64], func=AF.Copy),
                 reads=[pb[0].b], writes=[nrmT.b])
            crow = SBT(ph, "crow", [16, 2 * DFF], F32)
            ld(crow, I["convrows"])
            for c0 in range(0, NFC, 22):
                for j in range(22):
                    mm(pb[1].t[:, j * 16:(j + 1) * 16], crow.t[:, (c0 + j) * 128:(c0 + j + 1) * 128], identf.t[0:16, 0:16], True, True,
                       [crow.b, identf.b], pb[1].b)
                S.op("act", lambda e, c0=c0: e.activation(out=cwT.t[:, c0:c0 + 22, :].rearrange("p j r -> p (j r)"),
                                                          in_=pb[1].t[:, 0:352], func=AF.Copy), reads=[pb[1].b], writes=[cwT.b])
            if dbg == "s2":
                dump(cwT.t[:].rearrange("p c r -> p (c r)"), 704, cwT.b)
                dump(nrmT.t[:].rearrange("p j r -> p (j r)"), 64, nrmT.b)
                stop()
            c17 = SBT(ph, "c17", [17, D], F32)
            ld(c17, I["c17"])
            c17b = SBT(ph, "c17b", [17, D], BF16)
            S.op("act", lambda e: e.activation(out=c17b.t[:], in_=c17.t[:], func=AF.Silu), reads=[c17.b], writes=[c17b.b])
            sT = SBT(ph, "sT", [128, 8, 17], BF16)
            pv = pb[2].t[:]
            for k in range(8):
                mm(pv[:, k * 32:k * 32 + 17], c17b.t[:, k * 128:(k + 1) * 128], identb.t[0:17, 0:17], True, True, [c17b.b, identb.b], pb[2].b)
            S.op("act", lambda e: e.activation(out=sT.t[:], in_=pv[:, 0:256].rearrange("p (k c) -> p k c", c=32)[:, :, 0:17], func=AF.Copy),
                 reads=[pb[2].b], writes=[sT.b])
            if dbg == "s3":
                dump(sT.t[:].rearrange("p k c -> p (k c)"), 136, sT.b, bf=True)
                stop()
            for l in range(4):
                for nt in range(12):
                    sl, br = w_next("ada", l, nt)
                    bank = pb[3 + (nt % 2)]
                    for fc in range(4):
                        o = bank.t[:, fc * 17:(fc + 1) * 17]
                        for k in range(8):
                            mm(o, sl.t[:, k, fc * 128:(fc + 1) * 128], sT.t[:, k, :], k == 0, False, [sl.b, sT.b], bank.b)
                        mm(o, br.t[:, fc * 128:(fc + 1) * 128], ones.t[:, 0:17], False, True, [br.b, ones.b], bank.b)
                    S.op("dve", lambda e, l=l, nt=nt, bank=bank: e.tensor_copy(
                        out=modT.t[:, l, nt * 4:(nt + 1) * 4, :].rearrange("p c s -> p (c s)"), in_=bank.t[:, 0:68]),
                        reads=[bank.b], writes=[modT.b])
            for l in range(4):
                for w in range(2):
                    S.op("dve", lambda e, l=l, w=w: e.tensor_scalar(out=Amod.t[:, l, w], in0=modT.t[:, l, 8 + 24 * w:16 + 24 * w, :],
                                                                    scalar1=1.0, scalar2=None, op0=ALU.add),
                         reads=[modT.b], writes=[Amod.b])
                    S.op("dve", lambda e, l=l, w=w: e.tensor_tensor(out=Amod.t[:, l, w], in0=Amod.t[:, l, w],
                                                                    in1=nrmT.t[:, :, 4 * w + l:4 * w + l + 1].to_broadcast([128, 8, 17]),
                                                                    op=ALU.mult),
                         reads=[nrmT.b, Amod.b], writes=[Amod.b])
            if dbg == "setup":
                dump(modT.t[:].rearrange("p l j s -> p (l j s)"), 3264, modT.b)
                dump(Amod.t[:].rearrange("p l w j s -> p (l w j s)"), 1088, Amod.b)
                dump(cwT.t[:].rearrange("p c r -> p (c r)"), 704, cwT.b)
                dump(wm.t[:].rearrange("p a l g t -> p (a l g t)"), 2048, wm.b, bf=True)
                stop()
            S.flush()

        def build_gates(ph, l, kinds):
            gbs = [SBT(ph, f"gb{i}", [128, 128], F32) for i in range(4)]
            gbc = [0]
            for kind in kinds:
                for w in range(2):
                    for half in range(2):
                        bank = pb[half]
                        for jj in range(4):
                            j = half * 4 + jj
                            gb = gbs[gbc[0] % 4]
                            gbc[0] += 1
                            col = modT.t[:, l, 16 + 24 * w + j, :]
                            if kind == 0:
                                src = col[:, 0:1].to_broadcast([128, 128])
                                dstv = gb.t[:]
                            else:
                                src = col[:, 1:17].unsqueeze(2).to_broadcast([128, 16, 8])
                                dstv = gb.t[:].rearrange("p (b t) -> p b t", t=8)
                            S.op("dve", lambda e, d=dstv, s=src: e.tensor_copy(out=d, in_=s), reads=[modT.b], writes=[gb.b])
                            mm(bank.t[:, jj * 128:(jj + 1) * 128], gb.t[:], identf.t[:], True, True, [gb.b, identf.b], bank.b)
                        S.op("act", lambda e, kind=kind, w=w, half=half, bank=bank: e.activation(
                            out=G[kind][w].t[:, half * 512:(half + 1) * 512], in_=bank.t[:], func=AF.Copy),
                            reads=[bank.b], writes=[G[kind][w].b])

        def norm_phase(ph, blocks, l, w):
            ssl = [SBT(ph, f"ss{i}", [128, 4], F32) for i in range(NBMAX)]
            junk = SBT(ph, "junk", [128, D], BF16)
            xn = [SBT(ph, f"xn{i}", [128, D], BF16) for i in range(2)]
            tmpf = [SBT(ph, f"tmpf{i}", [128, 8, 128], F32) for i in range(2)]
            for bi, kind in enumerate(blocks):
                S.op("act", lambda e, bi=bi: e.activation(out=junk.t[:], in_=x[bi].t[:], func=AF.Square, accum_out=ssl[bi].t[:, 0:1]),
                     reads=[x[bi].b], writes=[junk.b, ssl[bi].b])
                S.op("act", lambda e, bi=bi: e.activation(out=ssl[bi].t[:, 1:2], in_=ssl[bi].t[:, 0:1], func=AF.Sqrt,
                                                          scale=1.0 / D, bias=EPS), reads=[ssl[bi].b], writes=[ssl[bi].b])
                S.op("dve", lambda e, bi=bi: e.reciprocal(out=ssl[bi].t[:, 2:3], in_=ssl[bi].t[:, 1:2]),
                     reads=[ssl[bi].b], writes=[ssl[bi].b])
                xt = xn[bi % 2]
                S.op("act", lambda e, bi=bi, xt=xt: e.activation(out=xt.t[:], in_=x[bi].t[:], func=AF.Copy, scale=ssl[bi].t[:, 2:3]),
                     reads=[x[bi].b, ssl[bi].b], writes=[xt.b])
                bank = pb[6 + bi % 2]
                bv = bfview(bank)
                for j in range(8):
                    tr(bv[:, j, :], xt.t[:, j * 128:(j + 1) * 128], identb.t[:], [xt.b, identb.b], bank.b)
                tf = tmpf[bi % 2]
                hv = hT[:, :, bi * 128:(bi + 1) * 128]
                if kind == 0:
                    a_ap = Amod.t[:, l, w, :, 0:1].to_broadcast([128, 8, 128])
                    s_ap = modT.t[:, l, 24 * w:24 * w + 8, 0:1].to_broadcast([128, 8, 128])
                    S.op("dve", lambda e, bv=bv, tf=tf, a_ap=a_ap: e.tensor_tensor(out=tf.t[:], in0=bv, in1=a_ap, op=ALU.mult),
                         reads=[bank.b, Amod.b], writes=[tf.b])
                    S.op("pool", lambda e, hv=hv, tf=tf, s_ap=s_ap: e.tensor_tensor(out=hv, in0=tf.t[:], in1=s_ap, op=ALU.add),
                         reads=[tf.b, modT.b], writes=[hTb[bi]])
                else:
                    for j in range(8):
                        a_ap = Amod.t[:, l, w, j, 1:17].unsqueeze(2).to_broadcast([128, 16, 8])
                        s_ap = modT.t[:, l, 24 * w + j, 1:17].unsqueeze(2).to_broadcast([128, 16, 8])
                        S.op("dve", lambda e, j=j, bv=bv, tf=tf, a_ap=a_ap: e.tensor_tensor(
                            out=tf.t[:, j, :].rearrange("p (b t) -> p b t", t=8), in0=bv[:, j, :].rearrange("p (b t) -> p b t", t=8),
                            in1=a_ap, op=ALU.mult), reads=[bank.b, Amod.b], writes=[tf.b])
                        S.op("dve", lambda e, j=j, hv=hv, tf=tf, s_ap=s_ap: e.tensor_tensor(
                            out=hv[:, j, :].rearrange("p (b t) -> p b t", t=8), in0=tf.t[:, j, :].rearrange("p (b t) -> p b t", t=8),
                            in1=s_ap, op=ALU.add), reads=[tf.b, modT.b], writes=[hTb[bi]])

        def resid_add(bi, kind, w, nt, bank, eng2="pool"):
            xs_ = x[bi].t[:, nt * 512:(nt + 1) * 512]
            g_ = G[kind][w].t[:, nt * 512:(nt + 1) * 512]
            tmp = resid_tmp[resid_ctr[0] % 3]
            resid_ctr[0] += 1
            S.op("dve", lambda e: e.tensor_tensor(out=tmp.t[:], in0=bank.t[:], in1=g_, op=ALU.mult),
                 reads=[bank.b, G[kind][w].b], writes=[tmp.b])
            S.op(eng2, lambda e: e.tensor_tensor(out=xs_, in0=xs_, in1=tmp.t[:], op=ALU.add), reads=[tmp.b, x[bi].b], writes=[x[bi].b])

        resid_tmp = [SBT(gs, f"rtmp{i}", [128, 512], F32) for i in range(3)]
        resid_ctr = [0]
        mmctr = [0]

        def dense_tm(blocks, sl, br, kchunks, act_chunk, evac, banks=(0, 1, 2)):
            pend = []
            for bi, kind in enumerate(blocks):
                bank = pb[banks[mmctr[0] % len(banks)]]
                mmctr[0] += 1
                n = len(kchunks)
                for i, kc in enumerate(kchunks):
                    lhsT, rb = act_chunk(bi, kc)
                    mm(bank.t[:], lhsT, sl.t[:, i, :], i == 0, (i == n - 1) and br is None, [sl.b, rb], bank.b)
                if br is not None:
                    mm(bank.t[:], ones.t[:], br.t[:], False, True, [ones.b, br.b], bank.b)
                r = evac(bi, kind, bank)
                if callable(r):
                    pend.append(r)
                    if len(pend) > 2:
                        pend.pop(0)()
            while pend:
                pend.pop(0)()

        def h_chunk(bi, kc):
            return hT[:, kc, bi * 128:(bi + 1) * 128], hTb[bi]

        def attn_layer(st, blocks, a, l):
            gblk0 = st * SB
            nb = len(blocks)
            has_s = 1 in blocks
            npb = sum(1 for k in blocks if k == 0)
            if has_s:
                with ExitStack() as ph:
                    build_gates(ph, l, sorted(set(blocks)))
                    norm_phase(ph, blocks, l, 0)
                    S.flush()
            with ExitStack() as pst:
                OT = [SBT(pst, f"OT{i}", [128, 8, 128], BF16) for i in range(nb)]
                qkb = [None] * nb
                Vp = [None] * nb
                kT = [None] * nb
                if has_s:
                    qkb[npb] = SBT(pst, "qkbS", [128, 1280], BF16)
                    Vp[npb] = SBT(pst, "VpS", [128, 4, 128], BF16)
                    kT[npb] = SBT(pst, "kTS", [128, 2, 128], BF16)
                smp = {}

                def temps(ph, n, nq=None):
                    return dict(
                        qT=[SBT(ph, f"qT{i}", [128, 8, 128], BF16) for i in range(nq or n)],
                        sm=[SBT(ph, f"sm{i}", [128, 4, 256], F32) for i in range(min(n, 2))],
                        pp=[SBT(ph, f"pp{i}", [128, 4, 256], BF16) for i in range(n)],
                        pT=[SBT(ph, f"pT{i}", [128, 8, 128], BF16) for i in range(min(n, 2))],
                        stt=[SBT(ph, f"stt{i}", [128, 32], F32) for i in range(n)])

                SCORE_SETS = ((pb[5], pb[6]), (pb[1], pb[2]), (pb[3], pb[4]))
                pv_half = [Buf("pv0"), Buf("pv1")]
                nsets = [2]

                def attn_pre(bi, kind, T):
                    n = len(T["qT"])
                    bq = pb[7]
                    bqv = bfview(bq)
                    for c in range(8):
                        tr(bqv[:, c, :], qkb[bi].t[:, c * 128:(c + 1) * 128], identb.t[:], [qkb[bi].b, identb.b], bq.b)
                    qt = T["qT"][bi % n]
                    S.op("act", lambda e: e.activation(out=qt.t[:], in_=bqv, func=AF.Copy), reads=[bq.b], writes=[qt.b])
                    bk = pb[7]
                    bkv = bfview(bk)
                    for kc in range(2):
                        tr(bkv[:, kc, :], qkb[bi].t[:, 1024 + kc * 128:1024 + (kc + 1) * 128], identb.t[:], [qkb[bi].b, identb.b], bk.b)
                    S.op("act", lambda e: e.activation(out=kT[bi].t[:], in_=bkv[:, 0:2, :], func=AF.Copy), reads=[bk.b], writes=[kT[bi].b])

                def attn_sa(bi, kind, gi, k, T, part):
                    qt = T["qT"][bi % len(T["qT"])]
                    n = len(T["sm"])
                    first = (kind == 0 and gblk0 + bi == 0)
                    if kind == 0:
                        kprev = kT[bi - 1] if bi > 0 else kTp[a]
                        msk = maskP0 if first else maskP
                    else:
                        msk = maskS
                        QmT, KTs, bmk = smp["QmT"], smp["KTs"], smp["bmk"]
                    kc = gi // 2
                    bA, bB = SCORE_SETS[k % nsets[0]]
                    if kind == 1 and part == 0:
                        for ci in range(2):
                            c = 2 * gi + ci
                            S.op("dve", lambda e, ci=ci, c=c: e.tensor_tensor(
                                out=QmT[ci].t[:], in0=qt.t[:, c, :].unsqueeze(1).to_broadcast([128, 16, 128]), in1=bmk.t[:], op=ALU.mult),
                                reads=[qt.b, bmk.b], writes=[QmT[ci].b])
                    for hf, bank in ((0, bA), (1, bB)):
                        if part != 0:
                            break
                        ps = slice(64 * hf, 64 * hf + 64)
                        for ci in range(2):
                            c = 2 * gi + ci
                            o_prev = bank.t[:, ci * 256:ci * 256 + 128]
                            o_own = bank.t[:, ci * 256 + 128:ci * 256 + 256]
                            if kind == 0:
                                mm(o_prev, qt.t[ps, c, :], kprev.t[ps, kc, :], True, True, [qt.b, kprev.b], bank.b)
                            else:
                                for b in range(16):
                                    mm(o_prev, QmT[ci].t[ps, b, :], KTs.t[ps, kc, b, :], b == 0, b == 15, [QmT[ci].b, KTs.b], bank.b)
                            mm(o_own, qt.t[ps, c, :], kT[bi].t[ps, kc, :], True, True, [qt.b, kT[bi].b], bank.b)
                    if part == 0:
                        return
                    s_ = T["sm"][k % len(T["sm"])]
                    p_ = T["pp"][k % len(T["pp"])]
                    t8 = T["stt"][k % len(T["stt"])]
                    for hf, bank in ((0, bA), (1, bB)):
                        S.op("dve", lambda e, hf=hf, bank=bank: e.scalar_tensor_tensor(
                            out=s_.t[:, 2 * hf:2 * hf + 2, :], in0=bank.t[:].rearrange("p (s k) -> p s k", k=256), scalar=0.125,
                            in1=msk.t[:].unsqueeze(1).to_broadcast([128, 2, 256]), op0=ALU.mult, op1=ALU.add),
                            reads=[bank.b, msk.b], writes=[s_.b])
                    sk = sinkB.t[:, a, 4 * gi:4 * gi + 4]
                    S.op("dve", lambda e: e.tensor_reduce(out=t8.t[:, 0:4], in_=s_.t[:], axis=AX.X, op=ALU.max), reads=[s_.b], writes=[t8.b])
                    S.op("dve", lambda e: e.tensor_tensor(out=t8.t[:, 0:4], in0=t8.t[:, 0:4], in1=sk, op=ALU.max),
                         reads=[t8.b, sinkB.b], writes=[t8.b])
                    S.op("dve", lambda e: e.tensor_scalar(out=t8.t[:, 4:8], in0=t8.t[:, 0:4], scalar1=-1.0, scalar2=None, op0=ALU.mult),
                         reads=[t8.b], writes=[t8.b])
                    S.op("dve", lambda e: e.tensor_tensor(out=t8.t[:, 12:16], in0=t8.t[:, 4:8], in1=sk, op=ALU.add),
                         reads=[t8.b, sinkB.b], writes=[t8.b])
                    for h4 in range(4):
                        S.op("act", lambda e, h4=h4: e.activation(
                            out=p_.t[:, h4, :], in_=s_.t[:, h4, :], func=AF.Exp, bias=t8.t[:, 4 + h4:5 + h4], scale=1.0,
                            accum_out=t8.t[:, 8 + h4:9 + h4]), reads=[s_.b, t8.b], writes=[p_.b, t8.b])
                    S.op("act", lambda e: e.activation(out=t8.t[:, 16:20], in_=t8.t[:, 12:16], func=AF.Exp), reads=[t8.b], writes=[t8.b])

                def attn_bpv(bi, kind, gi, k, T, part):
                    p_ = T["pp"][k % len(T["pp"])]
                    t8 = T["stt"][k % len(T["stt"])]
                    kc = gi // 2
                    vprev = None
                    if kind == 0:
                        vprev = Vp[bi - 1] if bi > 0 else Vpp[a]
                    else:
                        Vc = smp["Vc"]
                    if part == 0:
                        S.op("dve", lambda e: e.tensor_tensor(out=t8.t[:, 20:24], in0=t8.t[:, 8:12], in1=t8.t[:, 16:20], op=ALU.add),
                             reads=[t8.b], writes=[t8.b])
                        S.op("dve", lambda e: e.reciprocal(out=t8.t[:, 24:28], in_=t8.t[:, 20:24]), reads=[t8.b], writes=[t8.b])
                        S.op("dve", lambda e: e.tensor_tensor(out=p_.t[:], in0=p_.t[:],
                                                              in1=t8.t[:, 24:28].unsqueeze(2).to_broadcast([128, 4, 256]), op=ALU.mult),
                             reads=[t8.b, p_.b], writes=[p_.b])
                        return
                    bt = pb[7]
                    btv = bfview(bt)
                    for h4 in range(4):
                        for part in range(2):
                            tr(btv[:, h4 * 2 + part, :], p_.t[:, h4, part * 128:(part + 1) * 128], identb.t[:], [p_.b, identb.b], bt.b)
                    pt = T["pT"][k % len(T["pT"])]
                    S.op("act", lambda e: e.activation(out=pt.t[:], in_=btv, func=AF.Copy), reads=[bt.b], writes=[pt.b])
                    bo = Tile(pb[0].t, pv_half[k % 2])
                    c0 = (k % 2) * 256
                    for ci in range(2):
                        o = bo.t[:, c0 + ci * 128:c0 + (ci + 1) * 128]
                        seqm = []
                        for hf in range(2):
                            seqm.append((Vp[bi], 2 * kc + hf, (2 * hf + ci) * 2 + 1))
                        if kind == 0:
                            for hf in range(2):
                                seqm.append((vprev, 2 * kc + hf, (2 * hf + ci) * 2))
                        nn = len(seqm)
                        for i, (vt, g, pidx) in enumerate(seqm):
                            mm(o, vt.t[:, g, :], pt.t[:, pidx, :], i == 0, (i == nn - 1) and kind == 0, [vt.b, pt.b], bo.b)
                        if kind == 1:
                            for hf in range(2):
                                g = 2 * kc + hf
                                pidx = (2 * hf + ci) * 2
                                for b in range(16):
                                    mm(o[:, 8 * b:8 * b + 8], Vc.t[:, b, g, :], pt.t[:, pidx, 8 * b:8 * b + 8], False,
                                       (hf == 1 and b == 15), [Vc.b, pt.b], bo.b)
                    S.op("act", lambda e: e.activation(
                        out=OT[bi].t[:, 2 * gi:2 * gi + 2, :], in_=bo.t[:, c0:c0 + 256].rearrange("p (c t) -> p c t", t=128), func=AF.Copy),
                        reads=[bo.b], writes=[OT[bi].b])

                def attn_pipeline(bis, kind, T):
                    items = [(bi, gi) for bi in bis for gi in range(4)]
                    for hb in pv_half:
                        hb.w = pb[0].b.w
                        hb.r = list(pb[0].b.r)

                    def front(k, part=None):
                        bi, gi = items[k]
                        if part in (None, 0):
                            if gi == 0:
                                attn_pre(bi, kind, T)
                            attn_sa(bi, kind, gi, k, T, 0)
                        if part in (None, 1):
                            attn_sa(bi, kind, gi, k, T, 1)

                    skew = 2 if kind == 0 else 1
                    nsets[0] = skew + 1
                    for k0 in range(min(skew, len(items))):
                        front(k0)
                    early_b = (skew == 2)
                    if early_b:
                        attn_bpv(items[0][0], kind, items[0][1], 0, T, 0)
                    for k, (bi, gi) in enumerate(items):
                        if k + skew < len(items):
                            front(k + skew, 0)
                        if early_b:
                            if k + 1 < len(items):
                                attn_bpv(items[k + 1][0], kind, items[k + 1][1], k + 1, T, 0)
                        else:
                            attn_bpv(bi, kind, gi, k, T, 0)
                        attn_bpv(bi, kind, gi, k, T, 1)
                        if k + skew < len(items):
                            front(k + skew, 1)
                        if kind == 0 and bi == SB - 1 and gi == 3:
                            S.op("dve", lambda e, bi=bi: e.tensor_copy(out=kTp[a].t[:], in_=kT[bi].t[:]), reads=[kT[bi].b], writes=[kTp[a].b])
                            S.op("dve", lambda e, bi=bi: e.tensor_copy(out=Vpp[a].t[:], in_=Vp[bi].t[:]), reads=[Vp[bi].b], writes=[Vpp[a].b])
                    pb[0].b.r = list(pb[0].b.r) + [ev for hb in pv_half for ev in ([hb.w] if hb.w else []) + hb.r]

                def phase_c():
                    for nt in range(2):
                        sl, _ = w_next("wo", a, nt)
                        dense_tm(blocks, sl, None, list(range(8)), lambda bi, kc: (OT[bi].t[:, kc, :], OT[bi].b),
                                 lambda bi, kind, bank, nt=nt: resid_add(bi, kind, 0, nt, bank), banks=(2, 3, 4))
                    S.flush()

                with ExitStack() as ph:
                    if not has_s:
                        build_gates(ph, l, sorted(set(blocks)))
                        norm_phase(ph, blocks, l, 0)
                    for i in range(npb):
                        qkb[i] = SBT(ph, f"qkb{i}", [128, 1280], BF16)
                        Vp[i] = SBT(ph, f"Vp{i}", [128, 4, 128], BF16)
                        kT[i] = SBT(ph, f"kT{i}", [128, 2, 128], BF16)
                    kvf = [SBT(ph, f"kvf{i}", [128, 512], F32) for i in range(2)]
                    rot = [SBT(ph, f"rot{i}", [128, 8, 16], F32) for i in range(2)]
                    rtm = [SBT(ph, f"rtm{i}", [128, 8, 8], F32) for i in range(2)]
                    TA = temps(ph, 3, 2)
                    qfs = [SBT(ph, f"qf{i}", [128, 512], F32) for i in range(2)]
                    qfc = [0]
                    for i in range(nb):
                        S.op("dve", lambda e, i=i: e.memset(Vp[i].t[:], 0.0), writes=[Vp[i].b])

                    def evac_qkv(nt):
                        def f(bi, kind, bank):
                            import os
                            ksub = int(os.environ.get("KSUB", "9"))
                            if ksub == 0:
                                S.op("act", lambda e: e.activation(out=qkb[bi].t[:, 0:512], in_=bank.t[:], func=AF.Copy),
                                     reads=[bank.b], writes=[qkb[bi].b])
                                return
                            gb = gblk0 + bi if kind == 0 else NPB
                            is_out = (kind == 1) or (gb == NPB - 1)
                            HF = 2 if nt < 2 else 1
                            W_ = HF * 256
                            pv3 = bank.t[:, 0:W_].rearrange("p (hf cl d) -> p hf cl d", hf=HF, d=64)
                            if nt < 2:
                                qv3 = qkb[bi].t[:, nt * 512:(nt + 1) * 512].rearrange("p (cl hf d) -> p hf cl d", hf=2, d=64)
                            else:
                                qv3 = qkb[bi].t[:, 1024:1280].rearrange("p (hf cl d) -> p hf cl d", hf=1, d=64)
                            S.op("act", lambda e: e.activation(out=qv3[:, :, :, 16:64], in_=pv3[:, :, :, 16:64], func=AF.Copy),
                                 reads=[bank.b], writes=[qkb[bi].b])
                            if ksub == 1:
                                return
                            nh = 4 * HF
                            r = rot[(bi + nt) % 2]
                            t_ = rtm[(bi + nt) % 2]
                            qf = qfs[qfc[0] % 2]
                            qfc[0] += 1
                            S.op("act", lambda e: e.activation(out=qf.t[:], in_=bank.t[:], func=AF.Copy), reads=[bank.b], writes=[qf.b])
                            src3 = qf.t[:, 0:nh * 64].rearrange("p (h d) -> p h d", d=64)
                            cosb = cs.t[:, gb, 0:8].unsqueeze(1).to_broadcast([128, nh, 8])
                            sinb = cs.t[:, gb, 8:16].unsqueeze(1).to_broadcast([128, nh, 8])
                            x1 = src3[:, :, 0:8]
                            x2 = src3[:, :, 8:16]
                            r1 = r.t[:, 0:nh, 0:8]
                            r2 = r.t[:, 0:nh, 8:16]
                            tv = t_.t[:, 0:nh, :]
                            rd = [qf.b, cs.b]
                            S.op("dve", lambda e: e.tensor_tensor(out=r1, in0=x1, in1=cosb, op=ALU.mult), reads=rd, writes=[r.b])
                            if ksub == 2:
                                return
                            S.op("dve", lambda e: e.tensor_tensor(out=tv, in0=x2, in1=sinb, op=ALU.mult), reads=rd, writes=[t_.b])
                            S.op("dve", lambda e: e.tensor_tensor(out=r1, in0=r1, in1=tv, op=ALU.subtract), reads=[t_.b, r.b], writes=[r.b])
                            S.op("dve", lambda e: e.tensor_tensor(out=r2, in0=x2, in1=cosb, op=ALU.mult), reads=rd + [r.b], writes=[r.b])
                            S.op("dve", lambda e: e.tensor_tensor(out=tv, in0=x1, in1=sinb, op=ALU.mult), reads=rd + [r.b], writes=[t_.b])
                            S.op("dve", lambda e: e.tensor_tensor(out=r2, in0=r2, in1=tv, op=ALU.add), reads=[t_.b, r.b], writes=[r.b])
                            if ksub == 3:
                                return
                            if nt < 2:
                                for hf in range(2):
                                    S.op("dve", lambda e, hf=hf: e.tensor_copy(out=qv3[:, hf, :, 0:16], in_=r.t[:, 4 * hf:4 * hf + 4, :]),
                                         reads=[r.b], writes=[qkb[bi].b])
                            else:
                                S.op("dve", lambda e: e.tensor_copy(out=qv3[:, 0, :, 0:16], in_=r.t[:, 0:4, :]), reads=[r.b], writes=[qkb[bi].b])
                            if ksub == 4:
                                return
                            if nt == 2:
                                vv = bank.t[:, 256:512].rearrange("p (g2 gp d) -> p g2 gp d", gp=2, d=64)
                                vd = Vp[bi].t[:].rearrange("p (g2 gp) c -> p g2 gp c", gp=2)
                                for gp in range(2):
                                    S.op("act", lambda e, gp=gp: e.activation(out=vd[:, :, gp, gp * 64:gp * 64 + 64], in_=vv[:, :, gp, :], func=AF.Copy),
                                         reads=[bank.b], writes=[Vp[bi].b])
                                if is_out:
                                    kf = kvf[kind]
                                    S.op("act", lambda e: e.activation(out=kf.t[:], in_=bank.t[:], func=AF.Copy), reads=[bank.b], writes=[kf.b])
                                    S.op("dve", lambda e: e.tensor_copy(out=kf.t[:, 0:256].rearrange("p (h d) -> p h d", d=64)[:, :, 0:16],
                                                                        in_=r.t[:, 0:4, :]), reads=[r.b, kf.b], writes=[kf.b])
                                    if kind == 0:
                                        S.dma("sp", lambda e: e.dma_start(out=O["kp"][a], in_=kf.t[:, 0:256]), reads=[kf.b], out=True)
                                        S.dma("sp", lambda e: e.dma_start(out=O["vp"][a], in_=kf.t[:, 256:512]), reads=[kf.b], out=True)
                                    else:
                                        for b in range(16):
                                            S.dma("sp", lambda e, b=b: e.dma_start(out=O["ks"][a][b, 120:128, :], in_=kf.t[8 * b:8 * b + 8, 0:256]),
                                                  reads=[kf.b], out=True)
                                            S.dma("sp", lambda e, b=b: e.dma_start(out=O["vs"][a][b, 120:128, :], in_=kf.t[8 * b:8 * b + 8, 256:512]),
                                                  reads=[kf.b], out=True)
                        return f

                    for nt in range(3):
                        sl, br = w_next("qkv", a, nt)
                        dense_tm(blocks, sl, br, list(range(8)), h_chunk, evac_qkv(nt))
                    if dbg == "attnA1":
                        dump(qkb[1].t[:], 1280, qkb[1].b, bf=True)
                        dump(Vp[1].t[:].rearrange("p g c -> p (g c)"), 512, Vp[1].b, bf=True)
                        stop()
                    attn_pipeline(list(range(npb)), 0, TA)
                    if has_s:
                        S.flush()
                    else:
                        phase_c()
                    if dbg == "attnA":
                        dump(qkb[1].t[:], 1280, qkb[1].b, bf=True)
                        dump(Vp[1].t[:].rearrange("p g c -> p (g c)"), 512, Vp[1].b, bf=True)
                        dump(kT[1].t[:].rearrange("p g c -> p (g c)"), 256, kT[1].b, bf=True)
                        for i in range(2):
                            dump(OT[i].t[:].rearrange("p c t -> p (c t)"), 1024, OT[i].b, bf=True)
                        stop()

                if has_s:
                    with ExitStack() as ph:
                        TB = temps(ph, 2, 1)
                        ckb = SBT(ph, "ckb", [128, 8, 256], BF16)
                        KTs = SBT(ph, "KTs", [128, 2, 16, 128], BF16)
                        Vc = SBT(ph, "Vc", [128, 16, 4, 128], BF16)
                        QmT = [SBT(ph, f"QmT{i}", [128, 16, 128], BF16) for i in range(2)]
                        bmk = SBT(ph, "bmk", [128, 16, 128], BF16)
                        smp.update(QmT=QmT, KTs=KTs, Vc=Vc, bmk=bmk)
                        S.dma("pool", lambda e: e.dma_start(out=bmk.t[:].rearrange("p b t -> p (b t)"), in_=I["bmask"]), writes=[bmk.b])
                        S.op("dve", lambda e: e.memset(Vc.t[:], 0.0), writes=[Vc.b])
                        cvv = I["cv"][a].rearrange("b k (g2 gp d) -> gp b k g2 d", gp=2, d=64)
                        for gp in range(2):
                            for b in range(16):
                                S.dma("pool", lambda e, gp=gp, b=b: e.dma_start(
                                    out=Vc.t[:, b].rearrange("p (g2 gp) c -> p g2 gp c", gp=2)[:, :, gp, gp * 64:gp * 64 + 64], in_=cvv[gp][b]),
                                    writes=[Vc.b])
                        for half in range(2):
                            S.dma("pool", lambda e, half=half: e.dma_start(
                                out=ckb.t[:], in_=I["ck"][a][8 * half:8 * half + 8].rearrange("b k c -> k b c")), writes=[ckb.b])
                            for b8 in range(8):
                                b = 8 * half + b8
                                bank = pb[3 + b % 2]
                                bv = bfview(bank)
                                for kc in range(2):
                                    tr(bv[:, kc, :], ckb.t[:, b8, kc * 128:(kc + 1) * 128], identb.t[:], [ckb.b, identb.b], bank.b)
                                S.op("act", lambda e, b=b, bv=bv: e.activation(out=KTs.t[:, :, b, :], in_=bv[:, 0:2, :], func=AF.Copy),
                                     reads=[bank.b], writes=[KTs.b])
                        for nm_i, nm_o in (("ck", "ks"), ("cv", "vs")):
                            S.dma("sp", lambda e, nm_i=nm_i, nm_o=nm_o: e.dma_start(out=O[nm_o][a][:, 0:120, :], in_=I[nm_i][a][:, 8:128, :]),
                                  out=True, hold=False)
                        attn_pipeline([npb], 1, TB)
                        S.flush()

                if has_s:
                    phase_c()

        def sgu_layer(st, blocks, a, l):
            if 1 in blocks:
                with ExitStack() as ph:
                    build_gates(ph, l, sorted(set(blocks)))
                    norm_phase(ph, blocks, l, 0)
                    S.flush()
            with ExitStack() as ph:
                if 1 not in blocks:
                    build_gates(ph, l, sorted(set(blocks)))
                    norm_phase(ph, blocks, l, 0)
                nb = len(blocks)
                lng = SBT(ph, "lng", [128, 2048], F32)
                lnb = SBT(ph, "lnb", [128, 2048], F32)
                S.dma("sp", lambda e: e.dma_start(out=lng.t[:], in_=I["lnrows"][a].partition_broadcast(128)), writes=[lng.b])
                S.dma("sp", lambda e: e.dma_start(out=lnb.t[:], in_=I["lnrows"][2 + a].partition_broadcast(128)), writes=[lnb.b])
                pTs = [SBT(ph, f"pTs{i}", [128, 16, 128], BF16) for i in range(nb)]
                ub = [SBT(ph, f"ub{i}", [128, 512], BF16) for i in range(2 * nb)]
                vtmp = [SBT(ph, f"vtmp{i}", [128, 512], F32) for i in range(3)]
                vnb = [SBT(ph, f"vnb{i}", [128, 512], BF16) for i in range(4)]
                pg = [SBT(ph, f"pg{i}", [128, 512], BF16) for i in range(4)]
                statsl = [SBT(ph, f"stats{i}", [128, 4, 6], F32) for i in range(nb)]
                mvl = [SBT(ph, f"mv{i}", [128, 4], F32) for i in range(nb)]
                vout = SBT(ph, "vout", [128, 2048], F32) if 1 in blocks else None
                vc = [0]

                def evac_stats(t4):
                    def f(bi, kind, bank):
                        vt = vtmp[vc[0] % 3]
                        vc[0] += 1
                        S.op("act", lambda e: e.activation(out=vt.t[:], in_=bank.t[:], func=AF.Gelu), reads=[bank.b], writes=[vt.b])
                        S.op("dve", lambda e: e.bn_stats(out=statsl[bi].t[:, t4, :], in_=vt.t[:]), reads=[vt.b], writes=[statsl[bi].b])
                        if t4 == 3:
                            S.op("dve", lambda e: e.bn_aggr(out=mvl[bi].t[:, 0:2], in_=statsl[bi].t[:]), reads=[statsl[bi].b], writes=[mvl[bi].b])
                            S.op("act", lambda e: e.activation(out=mvl[bi].t[:, 2:3], in_=mvl[bi].t[:, 1:2], func=AF.Sqrt, scale=1.0, bias=EPS),
                                 reads=[mvl[bi].b], writes=[mvl[bi].b])
                            S.op("dve", lambda e: e.reciprocal(out=mvl[bi].t[:, 3:4], in_=mvl[bi].t[:, 2:3]), reads=[mvl[bi].b], writes=[mvl[bi].b])
                    return f

                for t4 in range(4):
                    sl, br = w_next("win", a, 4 + t4)
                    dense_tm(blocks, sl, br, list(range(8)), h_chunk, evac_stats(t4))

                for g in range(4):
                    def evac_u(bi, kind, bank, g=g):
                        u_ = ub[(g % 2) * nb + bi]
                        S.op("act", lambda e: e.activation(out=u_.t[:], in_=bank.t[:], func=AF.Gelu), reads=[bank.b], writes=[u_.b])

                    def evac_v(bi, kind, bank, g=g):
                        vt = vtmp[vc[0] % 3]
                        vn = vnb[vc[0] % 4]
                        p_ = pg[vc[0] % 4]
                        vc[0] += 1
                        u_ = ub[(g % 2) * nb + bi]
                        S.op("act", lambda e: e.activation(out=vt.t[:], in_=bank.t[:], func=AF.Gelu), reads=[bank.b], writes=[vt.b])
                        S.op("dve", lambda e: e.tensor_scalar(out=vt.t[:], in0=vt.t[:], scalar1=mvl[bi].t[:, 0:1], scalar2=mvl[bi].t[:, 3:4],
                                                              op0=ALU.subtract, op1=ALU.mult), reads=[vt.b, mvl[bi].b], writes=[vt.b])
                        S.op("dve", lambda e: e.tensor_tensor(out=vt.t[:], in0=vt.t[:], in1=lng.t[:, g * 512:(g + 1) * 512], op=ALU.mult),
                             reads=[vt.b, lng.b], writes=[vt.b])
                        if kind == 1:
                            S.op("dve", lambda e: e.tensor_tensor(out=vout.t[:, g * 512:(g + 1) * 512], in0=vt.t[:],
                                                                  in1=lnb.t[:, g * 512:(g + 1) * 512], op=ALU.add),
                                 reads=[vt.b, lnb.b], writes=[vout.b])
                            S.op("dve", lambda e: e.tensor_copy(out=vn.t[:], in_=vout.t[:, g * 512:(g + 1) * 512]), reads=[vout.b], writes=[vn.b])
                        else:
                            S.op("dve", lambda e: e.tensor_tensor(out=vn.t[:], in0=vt.t[:], in1=lnb.t[:, g * 512:(g + 1) * 512], op=ALU.add),
                                 reads=[vt.b, lnb.b], writes=[vn.b])
                        vcv = vc[0]

                        def tail():
                            evac_v_tail(bi, kind, g, vn, p_, u_, vcv)
                        return tail

                    def evac_v_tail(bi, kind, g, vn, p_, u_, vcv):
                        bm = pb[3 + vcv % 2]
                        mm(bm.t[:], wm.t[:, kind, a, g, :], vn.t[:], True, True, [wm.b, vn.b], bm.b)
                        S.op("dve", lambda e: e.scalar_tensor_tensor(out=p_.t[:], in0=bm.t[:], scalar=bsp.t[:, kind, 4 * a + g:4 * a + g + 1],
                                                                     in1=u_.t[:], op0=ALU.add, op1=ALU.mult),
                             reads=[bm.b, bsp.b, u_.b], writes=[p_.b])
                        bt = pb[5 + vcv % 2]
                        btv = bfview(bt)
                        for j in range(4):
                            tr(btv[:, j, :], p_.t[:, j * 128:(j + 1) * 128], identb.t[:], [p_.b, identb.b], bt.b)
                        S.op("act", lambda e: e.activation(out=pTs[bi].t[:, 4 * g:4 * g + 4, :], in_=btv[:, 0:4, :], func=AF.Copy),
                             reads=[bt.b], writes=[pTs[bi].b])

                    sl, br = w_next("win", a, g)
                    dense_tm(blocks, sl, br, list(range(8)), h_chunk, evac_u)
                    sl, br = w_next("win", a, 4 + g)
                    dense_tm(blocks, sl, br, list(range(8)), h_chunk, evac_v)
                if vout is not None:
                    S.dma("sp", lambda e: e.dma_start(out=O["sguv"][a], in_=vout.t[:]), reads=[vout.b], out=True)
                for kh in range(2):
                    for nt in range(2):
                        sl, _ = w_next("wout", a, kh * 2 + nt)
                        dense_tm(blocks, sl, None, list(range(8)), lambda bi, kc, kh=kh: (pTs[bi].t[:, kh * 8 + kc, :], pTs[bi].b),
                                 lambda bi, kind, bank, nt=nt: resid_add(bi, kind, 0, nt, bank))
                S.flush()

        def ffn_layer(st, blocks, l):
            if 1 in blocks:
                with ExitStack() as ph:
                    norm_phase(ph, blocks, l, 1)
                    S.flush()
            with ExitStack() as ph:
                if 1 not in blocks:
                    norm_phase(ph, blocks, l, 1)
                nb = len(blocks)
                npb = sum(1 for k in blocks if k == 0)
                N = npb * 128
                has_s = 1 in blocks
                gT = ph.enter_context(nc.sbuf_tensor(uname("gT"), [128, 22, nb * 128], BF16))
                gTb = Buf("gT")
                nset = 2 if has_s else 3
                ab = [[SBT(ph, f"ab{i}{h}", [128, 2 + SB * 128], F32) for h in range(2)] for i in range(nset)]
                tt = [[SBT(ph, f"tt{i}{h}", [128, SB * 128], F32) for h in range(2)] for i in range(nset)]
                qq = [[SBT(ph, f"qq{i}{h}", [128, SB * 128], F32) for h in range(2)] for i in range(nset)]
                if has_s:
                    abS = [[SBT(ph, f"abS{i}{h}", [128, 16, 10], F32) for h in range(2)] for i in range(2)]
                    ttS = [[SBT(ph, f"ttS{i}{h}", [128, 16, 8], F32) for h in range(2)] for i in range(2)]
                    stS = SBT(ph, "stS", [128, NFC, 32], F32)
                    aLs = SBT(ph, "aLs", [128, NFC, 32], F32)
                    stg = [SBT(ph, "stg0", [32, 1408], F32)] * 2
                    for q4 in range(4):
                        sg = stg[q4 % 2]
                        S.dma("sp", lambda e, q4=q4, sg=sg: e.dma_start(out=sg.t[:], in_=I["sconv"][l][:, q4 * 1408:(q4 + 1) * 1408]),
                              writes=[sg.b])
                        bank = pb[6 + q4 % 2]
                        for j in range(11):
                            mm(bank.t[:, j * 32:(j + 1) * 32], sg.t[:, j * 128:(j + 1) * 128], identf.t[0:32, 0:32], True, True, [sg.b, identf.b], bank.b)
                        S.op("act", lambda e, q4=q4, bank=bank: e.activation(
                            out=stS.t[:, q4 * 11:(q4 + 1) * 11, :].rearrange("p j r -> p (j r)"), in_=bank.t[:, 0:352], func=AF.Copy),
                            reads=[bank.b], writes=[stS.b])
                cnt = [0]
                ffn_pend = []
                for jj in range(11):
                    sl, _ = w_next("wup", l, jj)
                    def chunk(jx):
                        j = 2 * jj + jx
                        par = cnt[0] % nset
                        cnt[0] += 1
                        banks = (pb[0 + 2 * par], pb[1 + 2 * par])
                        bS = pb[4 + par]
                        for h in range(2):
                            wcol = sl.t[:, :, h * 256 + jx * 128:h * 256 + jx * 128 + 128]
                            if npb:
                                for k in range(8):
                                    mm(banks[h].t[:, 0:N], wcol[:, k, :], hT[:, k, 0:N], k == 0, k == 7,
                                       [sl.b] + hTb[0:npb], banks[h].b)
                            if has_s:
                                for k in range(8):
                                    mm(bS.t[:, h * 128:(h + 1) * 128], wcol[:, k, :], hT[:, k, N:N + 128], k == 0, k == 7,
                                       [sl.b, hTb[npb]], bS.b)
                        for h in range(2):
                            fc = j + 22 * h
                            w0 = cwT.t[:, fc, 4 * l + 0:4 * l + 1]
                            w1 = cwT.t[:, fc, 4 * l + 1:4 * l + 2]
                            w2 = cwT.t[:, fc, 4 * l + 2:4 * l + 3]
                            cb = cwT.t[:, fc, 4 * l + 3:4 * l + 4]
                            if npb:
                                a_ = ab[par][h]
                                t_ = tt[par][h]
                                S.op("act", lambda e, a_=a_, h=h: e.activation(out=a_.t[:, 2:2 + N], in_=banks[h].t[:, 0:N], func=AF.Copy),
                                     reads=[banks[h].b], writes=[a_.b])
                                if h == 0:
                                    S.op("dve", lambda e, a_=a_, fc=fc: e.tensor_copy(out=a_.t[:, 0:2], in_=stP.t[:, l, fc, :]), reads=[stPb[l][fc]], writes=[a_.b])
                                    S.op("dve", lambda e, a_=a_, t_=t_, w2=w2, cb=cb: e.tensor_scalar(
                                        out=t_.t[:, 0:N], in0=a_.t[:, 2:2 + N], scalar1=w2, scalar2=cb, op0=ALU.mult, op1=ALU.add),
                                        reads=[a_.b, cwT.b], writes=[t_.b])
                                    S.op("dve", lambda e, a_=a_, t_=t_, w1=w1: e.scalar_tensor_tensor(
                                        out=t_.t[:, 0:N], in0=a_.t[:, 1:1 + N], scalar=w1, in1=t_.t[:, 0:N], op0=ALU.mult, op1=ALU.add),
                                        reads=[a_.b, cwT.b, t_.b], writes=[t_.b])
                                    S.op("dve", lambda e, a_=a_, t_=t_, w0=w0: e.scalar_tensor_tensor(
                                        out=t_.t[:, 0:N], in0=a_.t[:, 0:N], scalar=w0, in1=t_.t[:, 0:N], op0=ALU.mult, op1=ALU.add),
                                        reads=[a_.b, cwT.b, t_.b], writes=[t_.b])
                                    S.op("dve", lambda e, a_=a_, fc=fc: e.tensor_copy(out=stP.t[:, l, fc, :], in_=a_.t[:, N:N + 2]), reads=[a_.b], writes=[stPb[l][fc]])
                                else:
                                    q1, q0 = qq[par]
                                    S.op("pool", lambda e, a_=a_, fc=fc: e.tensor_copy(out=a_.t[:, 0:2], in_=stP.t[:, l, fc, :]), reads=[stPb[l][fc]], writes=[a_.b])
                                    S.op("act", lambda e, a_=a_, q1=q1, w1=w1: e.activation(out=q1.t[:, 0:N], in_=a_.t[:, 1:1 + N], func=AF.Copy, scale=w1),
                                         reads=[a_.b, cwT.b], writes=[q1.b])
                                    S.op("act", lambda e, a_=a_, q0=q0, w0=w0: e.activation(out=q0.t[:, 0:N], in_=a_.t[:, 0:N], func=AF.Copy, scale=w0),
                                         reads=[a_.b, cwT.b], writes=[q0.b])
                                    S.op("act", lambda e, a_=a_, t_=t_, w2=w2, cb=cb: e.activation(
                                        out=t_.t[:, 0:N], in_=a_.t[:, 2:2 + N], func=AF.Identity, scale=w2, bias=cb),
                                        reads=[a_.b, cwT.b], writes=[t_.b])
                                    S.op("pool", lambda e, t_=t_, q1=q1: e.tensor_tensor(out=t_.t[:, 0:N], in0=t_.t[:, 0:N], in1=q1.t[:, 0:N], op=ALU.add),
                                         reads=[q1.b, t_.b], writes=[t_.b])
                                    S.op("dve", lambda e, t_=t_, q0=q0: e.tensor_tensor(out=t_.t[:, 0:N], in0=t_.t[:, 0:N], in1=q0.t[:, 0:N], op=ALU.add),
                                         reads=[q0.b, t_.b], writes=[t_.b])
                                    S.op("pool", lambda e, a_=a_, fc=fc: e.tensor_copy(out=stP.t[:, l, fc, :], in_=a_.t[:, N:N + 2]), reads=[a_.b], writes=[stPb[l][fc]])
                            if has_s:
                                a_ = abS[par][h]
                                t_ = ttS[par][h]
                                S.op("act", lambda e, a_=a_, h=h: e.activation(
                                    out=a_.t[:, :, 2:10], in_=bS.t[:, h * 128:(h + 1) * 128].rearrange("p (b t) -> p b t", t=8), func=AF.Copy),
                                    reads=[bS.b], writes=[a_.b])
                                S.op("dve", lambda e, a_=a_, fc=fc: e.tensor_copy(
                                    out=a_.t[:, :, 0:2], in_=stS.t[:, fc, :].rearrange("p (b t) -> p b t", t=2)), reads=[stS.b], writes=[a_.b])
                                S.op("dve", lambda e, a_=a_, t_=t_, w2=w2, cb=cb: e.tensor_scalar(
                                    out=t_.t[:], in0=a_.t[:, :, 2:10], scalar1=w2, scalar2=cb, op0=ALU.mult, op1=ALU.add),
                                    reads=[a_.b, cwT.b], writes=[t_.b])
                                S.op("dve", lambda e, a_=a_, t_=t_, w1=w1: e.scalar_tensor_tensor(
                                    out=t_.t[:], in0=a_.t[:, :, 1:9], scalar=w1, in1=t_.t[:], op0=ALU.mult, op1=ALU.add),
                                    reads=[a_.b, cwT.b, t_.b], writes=[t_.b])
                                S.op("dve", lambda e, a_=a_, t_=t_, w0=w0: e.scalar_tensor_tensor(
                                    out=t_.t[:], in0=a_.t[:, :, 0:8], scalar=w0, in1=t_.t[:], op0=ALU.mult, op1=ALU.add),
                                    reads=[a_.b, cwT.b, t_.b], writes=[t_.b])
                                S.op("dve", lambda e, a_=a_, fc=fc: e.tensor_copy(
                                    out=aLs.t[:, fc, :].rearrange("p (b t) -> p b t", t=2), in_=a_.t[:, :, 8:10]), reads=[a_.b], writes=[aLs.b])
                        def tail():
                            if npb:
                                tg, tu = tt[par][0], tt[par][1]
                                S.op("act", lambda e: e.activation(out=tg.t[:, 0:N], in_=tg.t[:, 0:N], func=AF.Silu), reads=[tg.b], writes=[tg.b])
                                S.op("dve", lambda e: e.tensor_tensor(out=gT[:, j, 0:N], in0=tg.t[:, 0:N], in1=tu.t[:, 0:N], op=ALU.mult),
                                     reads=[tg.b, tu.b], writes=[gTb])
                            if has_s:
                                tgs, tus = ttS[par][0], ttS[par][1]
                                S.op("act", lambda e: e.activation(out=tgs.t[:], in_=tgs.t[:], func=AF.Silu), reads=[tgs.b], writes=[tgs.b])
                                S.op("dve", lambda e: e.tensor_tensor(
                                    out=gT[:, j, N:N + 128].rearrange("p (b t) -> p b t", t=8), in0=tgs.t[:], in1=tus.t[:], op=ALU.mult),
                                    reads=[tgs.b, tus.b], writes=[gTb])
                        return tail

                    for jx in range(2):
                        tl = chunk(jx)
                        if has_s:
                            tl()
                        else:
                            if ffn_pend:
                                ffn_pend.pop(0)()
                            ffn_pend.append(tl)
                while ffn_pend:
                    ffn_pend.pop(0)()
                for kg in range(3):
                    nk = 8 if kg < 2 else 6
                    for nt in range(2):
                        sl, _ = w_next("wdown", l, kg * 2 + nt)
                        dense_tm(blocks, sl, None, list(range(nk)),
                                 lambda bi, kc, kg=kg: (gT[:, kg * 8 + kc, bi * 128:(bi + 1) * 128], gTb),
                                 lambda bi, kind, bank, nt=nt: resid_add(bi, kind, 1, nt, bank, "dve"), banks=(6, 7))
                if has_s:
                    for c0 in range(0, NFC, 4):
                        bank = pb[(c0 // 4) % 2]
                        sg = stg[(c0 // 4) % 2]
                        for j in range(4):
                            mm(bank.t[0:32, j * 128:(j + 1) * 128], aLs.t[:, c0 + j, :], identf.t[:], True, True, [aLs.b, identf.b], bank.b)
                        S.op("act", lambda e, bank=bank, sg=sg: e.activation(out=sg.t[:, 0:512], in_=bank.t[0:32, :], func=AF.Copy),
                             reads=[bank.b], writes=[sg.b])
                        S.dma("sp", lambda e, sg=sg, c0=c0: e.dma_start(out=O["convs"][l][:, c0 * 128:(c0 + 4) * 128], in_=sg.t[:, 0:512]),
                              reads=[sg.b], out=True)
                S.flush()

        for st in range(NST):
            blocks = [0] * SB + ([1] if st == NST - 1 else [])
            with ExitStack() as ph:
                for bi, kind in enumerate(blocks):
                    src = I["xp"][(st * SB + bi) * 128:(st * SB + bi + 1) * 128, :] if kind == 0 else I["xs"]
                    S.dma("sp", lambda e, bi=bi, src=src: e.dma_start(out=x[bi].t[:], in_=src), writes=[x[bi].b])
                S.flush()
            for l in range(4):
                if l % 2 == 0:
                    attn_layer(st, blocks, l // 2, l)
                else:
                    sgu_layer(st, blocks, l // 2, l)
                if dbg == f"mix{l}" and st == 0:
                    for i in range(4):
                        dump(x[i].t[:], 1024, x[i].b)
                    stop()
                ffn_layer(st, blocks, l)
                if dbg == f"ffn{l}" and st == 0:
                    for i in range(4):
                        dump(x[i].t[:], 1024, x[i].b)
                    S.op("dve", lambda e: e.engine_nop(), reads=stP_all, writes=[stP.b])
                    dump(stP.t[:].rearrange("p l c t -> p (l c t)"), 352, stP.b)
                    stop()
            with ExitStack() as ph:
                ssl = [SBT(ph, f"fss{i}", [128, 4], F32) for i in range(NBMAX)]
                junk = SBT(ph, "fjunk", [128, D], BF16)
                yo = [SBT(ph, f"yo{i}", [128, D], F32) for i in range(2)]
                for bi, kind in enumerate(blocks):
                    S.op("act", lambda e, bi=bi: e.activation(out=junk.t[:], in_=x[bi].t[:], func=AF.Square, accum_out=ssl[bi].t[:, 0:1]),
                         reads=[x[bi].b], writes=[junk.b, ssl[bi].b])
                    S.op("act", lambda e, bi=bi: e.activation(out=ssl[bi].t[:, 1:2], in_=ssl[bi].t[:, 0:1], func=AF.Sqrt,
                                                              scale=1.0 / D, bias=EPS), reads=[ssl[bi].b], writes=[ssl[bi].b])
                    S.op("dve", lambda e, bi=bi: e.reciprocal(out=ssl[bi].t[:, 2:3], in_=ssl[bi].t[:, 1:2]),
                         reads=[ssl[bi].b], writes=[ssl[bi].b])
                    y_ = yo[bi % 2]
                    S.op("dve", lambda e, bi=bi, y_=y_: e.scalar_tensor_tensor(out=y_.t[:], in0=x[bi].t[:], scalar=ssl[bi].t[:, 2:3],
                                                                              in1=normfB.t[:], op0=ALU.mult, op1=ALU.mult),
                         reads=[x[bi].b, ssl[bi].b, normfB.b], writes=[y_.b])
                    dst = O["yp"][(st * SB + bi) * 128:(st * SB + bi + 1) * 128, :] if kind == 0 else O["ys"]
                    S.dma("sp", lambda e, y_=y_, dst=dst: e.dma_start(out=dst, in_=y_.t[:]), reads=[y_.b], out=True)
                S.flush()

        with ExitStack() as ph:
            stg2 = [SBT(ph, f"stg2{i}", [2, 512], F32) for i in range(2)]
            for l in range(4):
                for c0 in range(0, NFC, 4):
                    i = (l * 11 + c0 // 4) % 2
                    bank = pb[i]
                    for j in range(4):
                        mm(bank.t[0:2, j * 128:(j + 1) * 128], stP.t[:, l, c0 + j, :], identf.t[:], True, True, [stPb[l][c0 + j], identf.b], bank.b)
                    S.op("act", lambda e, bank=bank, i=i: e.activation(out=stg2[i].t[:], in_=bank.t[0:2, :], func=AF.Copy),
                         reads=[bank.b], writes=[stg2[i].b])
                    S.dma("sp", lambda e, i=i, l=l, c0=c0: e.dma_start(out=O["convp"][l][:, c0 * 128:(c0 + 4) * 128], in_=stg2[i].t[:]),
                          reads=[stg2[i].b], out=True)
            S.flush(final=True)
        assert wstate["next"] == len(wseq)
    return nc


_CACHE = {}


def _consts():
    i = np.arange(128)[:, None]
    j = np.arange(128)[None, :]
    ninf = np.float32(NEG)
    prev = np.where(j >= i, 0.0, ninf)
    own = np.where(j <= i, 0.0, ninf)
    maskP = np.concatenate([prev, own], 1).astype(np.float32)
    maskP0 = np.concatenate([np.full((128, 128), ninf), own], 1).astype(np.float32)
    t = i % 8
    cachem = np.where(j >= t, 0.0, ninf)
    newm = np.where((j // 8 == i // 8) & (j % 8 <= t), 0.0, ninf)
    maskS = np.concatenate([cachem, newm], 1).astype(np.float32)
    trilT = (i <= j).astype(np.float32)
    bm = (np.arange(128)[None, :] // 8 == np.arange(16)[:, None]).astype(np.float32).reshape(1, 16 * 128)
    bmask = np.ascontiguousarray(np.broadcast_to(bm, (128, 16 * 128)))
    return dict(ident=np.eye(128, dtype=np.float32), maskP=maskP, maskP0=maskP0, maskS=maskS, trilT=trilT, bmask=bmask)


def _cossin(pos):
    half = 8
    inv = (np.float32(500000.0) ** (-np.arange(0, 16, 2, dtype=np.float32) / np.float32(16))).astype(np.float32)
    ang = pos.astype(np.float32)[:, None] * inv[None, :]
    return np.concatenate([np.cos(ang), np.sin(ang)], 1).astype(np.float32)


def _prep(x_prompt, x_sample, c_prompt, c_sample, cache_k, cache_v, state_conv,
          w_ada, b_ada, norm_mix, norm_ffn, w_qkv, b_qkv, attn_sink, w_o,
          w_sgu_in, b_sgu_in, sgu_ln_g, sgu_ln_b, w_spatial, b_spatial, w_sgu_out,
          w_up, conv_w, conv_b, w_down, norm_final):
    f = lambda a: np.ascontiguousarray(np.asarray(a, dtype=np.float32))
    x_prompt, x_sample, c_prompt, c_sample = f(x_prompt), f(x_sample), f(c_prompt), f(c_sample)
    cache_k, cache_v, state_conv = f(cache_k), f(cache_v), f(state_conv)
    consts = _consts()
    w_spatial = f(w_spatial)
    b_spatial = f(b_spatial)
    wspT = np.ascontiguousarray(w_spatial.transpose(0, 1, 3, 2))
    wspST = np.zeros((2, 4, 128, 128), np.float32)
    for b in range(16):
        wspST[:, :, 8 * b:8 * b + 8, 8 * b:8 * b + 8] = wspT[:, :, 0:8, 0:8]
    bspP = np.ascontiguousarray(b_spatial.transpose(2, 0, 1).reshape(128, 8))
    bspS = np.ascontiguousarray(bspP[np.arange(128) % 8])
    perm = [0, 1, 4, 5, 2, 3, 6, 7, 8, 9, 12, 13, 10, 11, 14, 15]
    def tiles_k(W):
        L, K, N = W.shape
        t = W.reshape(L, K // 1024, 8, 128, N // 512, 512).transpose(0, 1, 4, 3, 2, 5)
        return np.ascontiguousarray(t.reshape(L, (K // 1024) * (N // 512), 128, 4096))

    w_o_ = f(w_o).reshape(2, 2, 2, 4, 64, D).transpose(0, 2, 4, 1, 3, 5).reshape(2, 128, 8, 2, 512)
    wt_o = np.ascontiguousarray(w_o_.transpose(0, 3, 1, 2, 4).reshape(2, 2, 128, 4096))
    w_up_ = f(w_up)
    wu = np.concatenate([w_up_[:, :, :DFF].reshape(4, D, 11, 256), w_up_[:, :, DFF:].reshape(4, D, 11, 256)], 3)
    wt_up = np.ascontiguousarray(wu.reshape(4, 8, 128, 11, 512).transpose(0, 3, 2, 1, 4).reshape(4, 11, 128, 4096))
    w_dn = np.zeros((4, 3072, D), np.float32)
    w_dn[:, :DFF] = f(w_down)
    shared = dict(
        wt_ada=tiles_k(f(w_ada)), b_ada=f(b_ada), rows8=np.concatenate([f(norm_mix), f(norm_ffn)], 0), normf=f(norm_final).reshape(1, D),
        wt_qkv=tiles_k(f(w_qkv)), b_qkv=f(b_qkv), sinkp=np.ascontiguousarray(f(attn_sink)[:, perm]), wt_o=wt_o,
        wt_in=tiles_k(f(w_sgu_in)), b_in=f(b_sgu_in), lnrows=np.concatenate([f(sgu_ln_g), f(sgu_ln_b)], 0),
        wspT=wspT, wspST=wspST, bspP=bspP, bspS=bspS, wt_out=tiles_k(f(w_sgu_out)), wt_up=wt_up,
        convrows=np.ascontiguousarray(np.concatenate([f(conv_w), f(conv_b)[:, None, :]], 1).reshape(16, 2 * DFF)),
        wt_down=tiles_k(w_dn), **consts)
    in_maps = []
    for c in range(8):
        seq, r = c // 4, c % 4
        p0 = PROC_START[r]
        pos = np.concatenate([np.arange(p0, p0 + NPB * 128), 8192 + (np.arange(128) % 8)])
        m = dict(shared)
        m.update(
            xp=np.ascontiguousarray(x_prompt[seq, p0:p0 + NPB * 128]),
            xs=np.ascontiguousarray(x_sample[16 * c:16 * c + 16].reshape(128, D)),
            c17=np.ascontiguousarray(np.concatenate([c_prompt[seq:seq + 1], c_sample[16 * c:16 * c + 16]], 0)),
            ck=np.ascontiguousarray(cache_k[:, 16 * c:16 * c + 16].reshape(2, 16, 128, 256)),
            cv=np.ascontiguousarray(cache_v[:, 16 * c:16 * c + 16].reshape(2, 16, 128, 256)),
            sconv=np.ascontiguousarray(state_conv[:, 16 * c:16 * c + 16].reshape(4, 32, 2 * DFF)),
            cossin=_cossin(pos),
        )
        in_maps.append(m)
    return in_maps


def kernel(**inputs):
    in_maps = _prep(**inputs)
    if "nc" not in _CACHE:
        _CACHE["nc"] = build_program()
    nc = _CACHE["nc"]
    res = run_bass_kernel_spmd(nc, in_maps, core_ids=list(range(8)))
    R = res.results
    y_prompt = np.zeros((2, 8192, D), np.float32)
    k_p = np.zeros((2, 2, 128, 4, 64), np.float32)
    v_p = np.zeros((2, 2, 128, 4, 64), np.float32)
    conv_p = np.zeros((4, 2, 2, 2 * DFF), np.float32)
    for c in range(8):
        seq, r = c // 4, c % 4
        p0 = PROC_START[r]
        y_prompt[seq, OWN_START[r]:OWN_END[r]] = R[c]["yp"][OWN_START[r] - p0:OWN_END[r] - p0]
        if r == 3:
            k_p[:, seq] = R[c]["kp"].reshape(2, 128, 4, 64)
            v_p[:, seq] = R[c]["vp"].reshape(2, 128, 4, 64)
            conv_p[:, seq] = R[c]["convp"]
    y_sample = np.concatenate([R[c]["ys"].reshape(16, 8, D) for c in range(8)], 0)
    k_s = np.concatenate([R[c]["ks"].reshape(2, 16, 128, 4, 64) for c in range(8)], 1)
    v_s = np.concatenate([R[c]["vs"].reshape(2, 16, 128, 4, 64) for c in range(8)], 1)
    conv_s = np.concatenate([R[c]["convs"].reshape(4, 16, 2, 2 * DFF) for c in range(8)], 1)
    sgu_v = np.concatenate([R[c]["sguv"].reshape(2, 16, 8, 2048) for c in range(8)], 1)
    return (y_prompt, y_sample, k_p, v_p, conv_p, k_s, v_s, conv_s, sgu_v)
```

```python
import numpy as np
from contextlib import ExitStack
import concourse.bass as bass
import concourse.mybir as mybir
from concourse.bass_utils import run_bass_kernel_spmd

F32 = mybir.dt.float32
BF16 = mybir.dt.bfloat16
ALU = mybir.AluOpType
AF = mybir.ActivationFunctionType
AX = mybir.AxisListType

D = 1024
NPB = 20
SB = 4
NST = NPB // SB
NBMAX = SB + 1
DFF = 2816
NFC = 44
EPS = 1e-6
NEG = -30000.0
PROC_START = [0, 1920, 3840, 5632]
OWN_START = [0, 2560, 4480, 6400]
OWN_END = [2560, 4480, 6400, 8192]
NSLOT = 5
PREFETCH = 4


class Buf:
    __slots__ = ("name", "w", "r")

    def __init__(self, name=""):
        self.name = name
        self.w = None
        self.r = []


class Tile:
    __slots__ = ("t", "b")

    def __init__(self, t, b=None):
        self.t = t
        self.b = b if b is not None else Buf()


class Entry:
    __slots__ = ("waits", "fn", "signal", "dma_sem")

    def __init__(self, waits, fn):
        self.waits = waits
        self.fn = fn
        self.signal = False
        self.dma_sem = None


COMPUTE = ("pe", "act", "dve", "pool")
ALLENG = ("pe", "act", "dve", "pool", "sp")


class Sched:
    def __init__(self, nc, stack, dma_ring=8):
        self.nc = nc
        self.streams = {k: [] for k in ALLENG}
        self.esem = {k: stack.enter_context(nc.semaphore("prog_" + k)) for k in COMPUTE}
        self.base = {k: 0 for k in COMPUTE}
        self.dsem = {}
        self.dcount = {}
        self.dlast = {}
        for q in ("sp", "pool", "act"):
            self.dsem[q] = [stack.enter_context(nc.semaphore(f"dma_{q}_{i}")) for i in range(dma_ring)]
            self.dcount[q] = 0
            self.dlast[q] = [None] * dma_ring
        self.seen = {k: {} for k in ALLENG}
        self.ring = dma_ring
        self.phase = 0
        self.hold = []
        self.all_out = []

    def _collect(self, eng, reads, writes):
        evs = []
        for b in reads:
            if b.w is not None:
                evs.append(b.w)
        for b in writes:
            if b.w is not None:
                evs.append(b.w)
            evs.extend(b.r)
        return self._reduce(eng, evs)

    def _reduce(self, eng, evs):
        best = {}
        for ev in evs:
            if ev is None:
                continue
            if ev[0] == "e":
                _, src, idx, ph = ev
                if ph != self.phase:
                    continue
                if src == eng and eng == "pe":
                    continue
                key = ("e", src)
                if key not in best or best[key][2] < idx:
                    best[key] = ev
            else:
                _, q, slot, val = ev
                key = ("d", q, slot)
                if key not in best or best[key][3] < val:
                    best[key] = ev
        waits = []
        for key, ev in best.items():
            v = ev[2] if ev[0] == "e" else ev[3]
            pk = (self.phase,) + key if ev[0] == "e" else key
            if self.seen[eng].get(pk, -1) >= v:
                continue
            self.seen[eng][pk] = v
            if ev[0] == "e":
                self.streams[ev[1]][ev[2]].signal = True
            waits.append(ev)
        return waits

    def op(self, eng, fn, reads=(), writes=()):
        waits = self._collect(eng, reads, writes)
        st = self.streams[eng]
        st.append(Entry(waits, fn))
        ev = ("e", eng, len(st) - 1, self.phase)
        for b in reads:
            b.r.append(ev)
        for b in writes:
            b.w = ev
            b.r = []
        return ev

    def dma(self, q, fn, reads=(), writes=(), hold=True, out=False):
        evs = []
        for b in reads:
            if b.w is not None:
                evs.append(b.w)
        for b in writes:
            if b.w is not None:
                evs.append(b.w)
            evs.extend(b.r)
        i = self.dcount[q]
        self.dcount[q] += 1
        slot = i % self.ring
        val = 16 * (i // self.ring + 1)
        if self.dlast[q][slot] is not None:
            evs.append(self.dlast[q][slot])
        waits = self._reduce(q, evs)
        ent = Entry(waits, fn)
        ent.dma_sem = self.dsem[q][slot]
        self.streams[q].append(ent)
        ev = ("d", q, slot, val)
        self.dlast[q][slot] = ev
        for b in reads:
            b.r.append(ev)
        for b in writes:
            b.w = ev
            b.r = []
        if hold:
            self.hold.append(ev)
        if out:
            self.all_out.append(ev)
        return ev

    def flush(self, final=False):
        nc = self.nc
        lasts = []
        for k in COMPUTE:
            st = self.streams[k]
            idx = None
            for i in range(len(st) - 1, -1, -1):
                if st[i].fn is not None and st[i].dma_sem is None:
                    idx = i
                    break
            if idx is not None:
                lasts.append(("e", k, idx, self.phase))
        extra = list(self.hold)
        if final:
            extra += self.all_out
            for q in self.dlast:
                extra += [ev for ev in self.dlast[q] if ev is not None]
        for k in ALLENG:
            waits = self._reduce(k, lasts + extra)
            self.streams[k].append(Entry(waits, None))
        self.hold = []
        counts = {}
        for k in COMPUTE:
            c = self.base[k]
            arr = []
            for ent in self.streams[k]:
                if ent.signal and ent.dma_sem is None and ent.fn is not None:
                    c += 1
                arr.append(c)
            counts[k] = arr

        def resolve(ev):
            if ev[0] == "e":
                return self.esem[ev[1]], counts[ev[1]][ev[2]]
            return self.dsem[ev[1]][ev[2]], ev[3]

        def replay(k, e):
            for ent in self.streams[k]:
                for ev in ent.waits:
                    s, v = resolve(ev)
                    e.wait_ge(s, v)
                if ent.fn is None:
                    continue
                ins = ent.fn(e)
                if ent.dma_sem is not None:
                    ins.then_inc(ent.dma_sem, 16)
                elif ent.signal:
                    ins.then_inc(self.esem[k], 1)

        with nc.Block() as block:
            @block.tensor
            def _(e):
                replay("pe", e)

            @block.scalar
            def _(e):
                replay("act", e)

            @block.vector
            def _(e):
                replay("dve", e)

            @block.gpsimd
            def _(e):
                replay("pool", e)

            @block.sync
            def _(e):
                replay("sp", e)

        for k in COMPUTE:
            if counts[k]:
                self.base[k] = counts[k][-1]
        self.streams = {k: [] for k in ALLENG}
        self.phase += 1


_IN_SPECS = [
    ("xp", [NPB * 128, D]), ("xs", [128, D]), ("c17", [17, D]),
    ("ck", [2, 16, 128, 256]), ("cv", [2, 16, 128, 256]), ("sconv", [4, 32, 2 * DFF]),
    ("wt_ada", [4, 12, 128, 4096]), ("b_ada", [4, 6 * D]), ("rows8", [8, D]), ("normf", [1, D]),
    ("wt_qkv", [2, 3, 128, 4096]), ("b_qkv", [2, 1536]), ("sinkp", [2, 16]), ("wt_o", [2, 2, 128, 4096]),
    ("wt_in", [2, 8, 128, 4096]), ("b_in", [2, 4096]), ("lnrows", [4, 2048]),
    ("wspT", [2, 4, 128, 128]), ("wspST", [2, 4, 128, 128]), ("bspP", [128, 8]), ("bspS", [128, 8]),
    ("wt_out", [2, 4, 128, 4096]), ("wt_up", [4, 11, 128, 4096]), ("convrows", [16, 2 * DFF]), ("wt_down", [4, 6, 128, 4096]),
    ("cossin", [(NPB + 1) * 128, 16]), ("ident", [128, 128]), ("maskP", [128, 256]), ("maskP0", [128, 256]),
    ("maskS", [128, 256]), ("trilT", [128, 128]), ("bmask", [128, 16 * 128]),
]
_OUT_SPECS = [
    ("yp", [NPB * 128, D]), ("ys", [128, D]), ("kp", [2, 128, 256]), ("vp", [2, 128, 256]),
    ("convp", [4, 2, 2 * DFF]), ("ks", [2, 16, 128, 256]), ("vs", [2, 16, 128, 256]),
    ("convs", [4, 32, 2 * DFF]), ("sguv", [2, 128, 2048]),
]


def weight_sequence():
    seq = []
    for l in range(4):
        for nt in range(12):
            seq.append(("ada", l, nt))
    for st in range(NST):
        for l in range(4):
            if l % 2 == 0:
                for nt in range(3):
                    seq.append(("qkv", l // 2, nt))
                for nt in range(2):
                    seq.append(("wo", l // 2, nt))
            else:
                for nt in range(4):
                    seq.append(("win", l // 2, 4 + nt))
                for g in range(4):
                    seq.append(("win", l // 2, g))
                    seq.append(("win", l // 2, 4 + g))
                for kh in range(2):
                    for nt in range(2):
                        seq.append(("wout", l // 2, kh * 2 + nt))
            for jj in range(11):
                seq.append(("wup", l, jj))
            for kg in range(3):
                for nt in range(2):
                    seq.append(("wdown", l, kg * 2 + nt))
    return seq


class _Stop(Exception):
    pass


def build_program(dbg=None):
    nc = bass.Bass("TRN2", target_bir_lowering=False)
    I = {n: nc.dram_tensor(n, s, F32, kind="ExternalInput").ap() for n, s in _IN_SPECS}
    O = {n: nc.dram_tensor(n, s, F32, kind="ExternalOutput").ap() for n, s in _OUT_SPECS}
    if dbg is not None:
        O["dbg"] = nc.dram_tensor("dbg", [128, 16384], F32, kind="ExternalOutput").ap()
    try:
        _build_body(nc, I, O, dbg)
    except _Stop:
        pass
    return nc


def _build_body(nc, I, O, dbg):

    with ExitStack() as gs:
        S = Sched(nc, gs)

        uid = [0]

        def uname(name):
            uid[0] += 1
            return f"s{uid[0]}_{name}"

        def SBT(stack, name, shape, dt):
            return Tile(stack.enter_context(nc.sbuf_tensor(uname(name), shape, dt)), Buf(name))

        dbg_off = [0]

        def dump(ap, ncols, buf, bf=False):
            if bf:
                with ExitStack() as dst_:
                    t = SBT(dst_, "dbgt", [128, ncols], F32)
                    S.op("dve", lambda e: e.tensor_copy(out=t.t[:], in_=ap), reads=[buf], writes=[t.b])
                    o = dbg_off[0]
                    S.dma("sp", lambda e: e.dma_start(out=O["dbg"][:, o:o + ncols], in_=t.t[:]), reads=[t.b], out=True)
                    S.flush()
            else:
                o = dbg_off[0]
                S.dma("sp", lambda e: e.dma_start(out=O["dbg"][:, o:o + ncols], in_=ap), reads=[buf], out=True)
            dbg_off[0] += ncols

        def stop():
            S.flush(final=True)
            print("SEM COUNTS", S.base, S.dcount)
            raise _Stop()

        x = [SBT(gs, f"x{i}", [128, D], F32) for i in range(NBMAX)]
        hT = gs.enter_context(nc.sbuf_tensor("s_hT", [128, 8, NBMAX * 128], BF16))
        hTb = [Buf(f"hT{i}") for i in range(NBMAX)]
        slots = [SBT(gs, f"ws{i}", [128, 8, 512], BF16) for i in range(NSLOT)]
        brow = [SBT(gs, f"brow{i}", [1, 512], BF16) for i in range(NSLOT)]
        pb = [Tile(gs.enter_context(nc.psum_tensor(f"pb{i}", [128, 512], F32)), Buf(f"pb{i}")) for i in range(8)]
        identf = SBT(gs, "identf", [128, 128], F32)
        identb = SBT(gs, "identb", [128, 128], BF16)
        ones = SBT(gs, "ones", [1, 128], BF16)
        maskP = SBT(gs, "maskP", [128, 256], F32)
        maskP0 = SBT(gs, "maskP0", [128, 256], F32)
        maskS = SBT(gs, "maskS", [128, 256], F32)
        wm = SBT(gs, "wm", [128, 2, 2, 4, 128], BF16)
        bsp = SBT(gs, "bsp", [128, 2, 8], F32)
        sinkB = SBT(gs, "sinkB", [128, 2, 16], F32)
        cs = SBT(gs, "cs", [128, NPB + 1, 16], F32)
        modT = SBT(gs, "modT", [128, 4, 48, 17], F32)
        Amod = SBT(gs, "Amod", [128, 4, 2, 8, 17], F32)
        cwT = SBT(gs, "cwT", [128, NFC, 16], F32)
        stP = SBT(gs, "stP", [128, 4, NFC, 2], F32)
        stPb = [[Buf(f"stP{l_}_{c_}") for c_ in range(NFC)] for l_ in range(4)]
        stP_all = [b_ for row in stPb for b_ in row]
        normfB = SBT(gs, "normfB", [128, D], F32)
        G = [[SBT(gs, f"G{k}{w}", [128, D], F32) for w in range(2)] for k in range(2)]
        kTp = [SBT(gs, f"kTp{a}", [128, 2, 128], BF16) for a in range(2)]
        Vpp = [SBT(gs, f"Vpp{a}", [128, 4, 128], BF16) for a in range(2)]

        wseq = weight_sequence()
        wstate = {"issued": 0, "next": 0}

        def w_issue(i):
            kind, l, idx = wseq[i]
            sl = slots[i % NSLOT]
            br = brow[i % NSLOT]
            dst = sl.t

            def q(dst_ap, src_ap, tile=sl):
                S.dma("pool", lambda e, d=dst_ap, s=src_ap: e.dma_start(out=d, in_=s), writes=[tile.b], hold=False)

            def tile(name, nk=8):
                q(dst[:, 0:nk, :], I[name][l][idx][:, 0:nk * 512].rearrange("p (k n) -> p k n", n=512))

            if kind == "ada":
                tile("wt_ada")
                q(br.t[:, :], I["b_ada"][l:l + 1, idx * 512:(idx + 1) * 512], br)
            elif kind == "qkv":
                tile("wt_qkv")
                q(br.t[:, :], I["b_qkv"][l:l + 1, idx * 512:(idx + 1) * 512], br)
            elif kind == "wo":
                tile("wt_o")
            elif kind == "win":
                tile("wt_in")
                q(br.t[:, :], I["b_in"][l:l + 1, idx * 512:(idx + 1) * 512], br)
            elif kind == "wout":
                tile("wt_out")
            elif kind == "wup":
                tile("wt_up")
            elif kind == "wdown":
                tile("wt_down", 8 if idx // 2 < 2 else 6)

        def w_next(kind, l, idx):
            i = wstate["next"]
            assert wseq[i] == (kind, l, idx), (wseq[i], kind, l, idx)
            while wstate["issued"] < min(len(wseq), i + PREFETCH + 1):
                w_issue(wstate["issued"])
                wstate["issued"] += 1
            wstate["next"] += 1
            return slots[i % NSLOT], brow[i % NSLOT]

        def mm(out_ap, lhsT, rhs, start, stop, reads, out_buf):
            return S.op("pe", lambda e: e.matmul(out=out_ap, lhsT=lhsT, rhs=rhs, start=start, stop=stop),
                        reads=reads, writes=[out_buf])

        def tr(out_ap, in_ap, ident_ap, reads, out_buf):
            return S.op("pe", lambda e: e.transpose(out=out_ap, in_=in_ap, identity=ident_ap), reads=reads, writes=[out_buf])

        def bfview(p):
            return p.t[:].bitcast(BF16).rearrange("p (j t) -> p j t", t=128)

        with ExitStack() as ph:
            def ld(tile, src, q="sp"):
                S.dma(q, lambda e: e.dma_start(out=tile.t[:], in_=src), writes=[tile.b])

            ld(identf, I["ident"])
            ld(maskP, I["maskP"])
            ld(maskP0, I["maskP0"])
            ld(maskS, I["maskS"])
            S.dma("sp", lambda e: e.dma_start(out=cs.t[:], in_=I["cossin"].rearrange("(b p) c -> p b c", p=128)), writes=[cs.b])
            S.dma("sp", lambda e: e.dma_start(out=bsp.t[:, 0, :], in_=I["bspP"]), writes=[bsp.b])
            S.dma("sp", lambda e: e.dma_start(out=bsp.t[:, 1, :], in_=I["bspS"]), writes=[bsp.b])
            S.dma("sp", lambda e: e.dma_start(out=sinkB.t[:].rearrange("p a h -> p (a h)"),
                                              in_=I["sinkp"].rearrange("a h -> (a h)").partition_broadcast(128)), writes=[sinkB.b])
            S.dma("sp", lambda e: e.dma_start(out=normfB.t[:], in_=I["normf"][0].partition_broadcast(128)), writes=[normfB.b])
            S.op("dve", lambda e: e.memset(ones.t[:], 1.0), writes=[ones.b])
            S.op("dve", lambda e: e.tensor_copy(out=identb.t[:], in_=identf.t[:]), reads=[identf.b], writes=[identb.b])
            S.op("dve", lambda e: e.memset(stP.t[:], 0.0), writes=stP_all)
            for a in range(2):
                S.op("dve", lambda e, a=a: e.memset(Vpp[a].t[:], 0.0), writes=[Vpp[a].b])
                S.op("dve", lambda e, a=a: e.memset(kTp[a].t[:], 0.0), writes=[kTp[a].b])
            tril = SBT(ph, "tril", [128, 128], F32)
            ld(tril, I["trilT"])
            wsp_f = SBT(ph, "wsp_f", [128, 2, 2, 4, 128], F32)
            for kind, nm in enumerate(("wspT", "wspST")):
                S.dma("sp", lambda e, kind=kind, nm=nm: e.dma_start(out=wsp_f.t[:, kind], in_=I[nm].rearrange("l g s t -> s l g t")),
                      writes=[wsp_f.b])
            S.op("dve", lambda e: e.tensor_tensor(out=wm.t[:].rearrange("p a l g t -> p (a l g) t"),
                                                  in0=wsp_f.t[:].rearrange("p a l g t -> p (a l g) t"),
                                                  in1=tril.t[:].unsqueeze(1).to_broadcast([128, 16, 128]), op=ALU.mult),
                 reads=[wsp_f.b, tril.b], writes=[wm.b])
            if dbg == "s1":
                dump(wm.t[:].rearrange("p a l g t -> p (a l g t)"), 2048, wm.b, bf=True)
                dump(cs.t[:].rearrange("p b c -> p (b c)"), 336, cs.b)
                dump(sinkB.t[:].rearrange("p a h -> p (a h)"), 32, sinkB.b)
                stop()
            rows8 = SBT(ph, "rows8", [8, D], F32)
            ld(rows8, I["rows8"])
            nrmT = SBT(ph, "nrmT", [128, 8, 8], F32)
            for j in range(8):
                mm(pb[0].t[:, j * 8:(j + 1) * 8], rows8.t[:, j * 128:(j + 1) * 128], identf.t[0:8, 0:8], True, True, [rows8.b, identf.b], pb[0].b)
            S.op("act", lambda e: e.activation(out=nrmT.t[:].rearrange("p j r -> p (j r)"), in_=pb[0].t[:, 0:64], func=AF.Copy),
                 reads=[pb[0].b], writes=[nrmT.b])
            crow = SBT(ph, "crow", [16, 2 * DFF], F32)
            ld(crow, I["convrows"])
            for c0 in range(0, NFC, 22):
                for j in range(22):
                    mm(pb[1].t[:, j * 16:(j + 1) * 16], crow.t[:, (c0 + j) * 128:(c0 + j + 1) * 128], identf.t[0:16, 0:16], True, True,
                       [crow.b, identf.b], pb[1].b)
                S.op("act", lambda e, c0=c0: e.activation(out=cwT.t[:, c0:c0 + 22, :].rearrange("p j r -> p (j r)"),
                                                          in_=pb[1].t[:, 0:352], func=AF.Copy), reads=[pb[1].b], writes=[cwT.b])
            if dbg == "s2":
                dump(cwT.t[:].rearrange("p c r -> p (c r)"), 704, cwT.b)
                dump(nrmT.t[:].rearrange("p j r -> p (j r)"), 64, nrmT.b)
                stop()
            c17 = SBT(ph, "c17", [17, D], F32)
            ld(c17, I["c17"])
            c17b = SBT(ph, "c17b", [17, D], BF16)
            S.op("act", lambda e: e.activation(out=c17b.t[:], in_=c17.t[:], func=AF.Silu), reads=[c17.b], writes=[c17b.b])
            sT = SBT(ph, "sT", [128, 8, 17], BF16)
            pv = pb[2].t[:]
            for k in range(8):
                mm(pv[:, k * 32:k * 32 + 17], c17b.t[:, k * 128:(k + 1) * 128], identb.t[0:17, 0:17], True, True, [c17b.b, identb.b], pb[2].b)
            S.op("act", lambda e: e.activation(out=sT.t[:], in_=pv[:, 0:256].rearrange("p (k c) -> p k c", c=32)[:, :, 0:17], func=AF.Copy),
                 reads=[pb[2].b], writes=[sT.b])
            if dbg == "s3":
                dump(sT.t[:].rearrange("p k c -> p (k c)"), 136, sT.b, bf=True)
                stop()
            for l in range(4):
                for nt in range(12):
                    sl, br = w_next("ada", l, nt)
                    bank = pb[3 + (nt % 2)]
                    for fc in range(4):
                        o = bank.t[:, fc * 17:(fc + 1) * 17]
                        for k in range(8):
                            mm(o, sl.t[:, k, fc * 128:(fc + 1) * 128], sT.t[:, k, :], k == 0, False, [sl.b, sT.b], bank.b)
                        mm(o, br.t[:, fc * 128:(fc + 1) * 128], ones.t[:, 0:17], False, True, [br.b, ones.b], bank.b)
                    S.op("dve", lambda e, l=l, nt=nt, bank=bank: e.tensor_copy(
                        out=modT.t[:, l, nt * 4:(nt + 1) * 4, :].rearrange("p c s -> p (c s)"), in_=bank.t[:, 0:68]),
                        reads=[bank.b], writes=[modT.b])
            for l in range(4):
                for w in range(2):
                    S.op("dve", lambda e, l=l, w=w: e.tensor_scalar(out=Amod.t[:, l, w], in0=modT.t[:, l, 8 + 24 * w:16 + 24 * w, :],
                                                                    scalar1=1.0, scalar2=None, op0=ALU.add),
                         reads=[modT.b], writes=[Amod.b])
                    S.op("dve", lambda e, l=l, w=w: e.tensor_tensor(out=Amod.t[:, l, w], in0=Amod.t[:, l, w],
                                                                    in1=nrmT.t[:, :, 4 * w + l:4 * w + l + 1].to_broadcast([128, 8, 17]),
                                                                    op=ALU.mult),
                         reads=[nrmT.b, Amod.b], writes=[Amod.b])
            if dbg == "setup":
                dump(modT.t[:].rearrange("p l j s -> p (l j s)"), 3264, modT.b)
                dump(Amod.t[:].rearrange("p l w j s -> p (l w j s)"), 1088, Amod.b)
                dump(cwT.t[:].rearrange("p c r -> p (c r)"), 704, cwT.b)
                dump(wm.t[:].rearrange("p a l g t -> p (a l g t)"), 2048, wm.b, bf=True)
                stop()
            S.flush()

        def build_gates(ph, l, kinds):
            gbs = [SBT(ph, f"gb{i}", [128, 128], F32) for i in range(4)]
            gbc = [0]
            for kind in kinds:
                for w in range(2):
                    for half in range(2):
                        bank = pb[half]
                        for jj in range(4):
                            j = half * 4 + jj
                            gb = gbs[gbc[0] % 4]
                            gbc[0] += 1
                            col = modT.t[:, l, 16 + 24 * w + j, :]
                            if kind == 0:
                                src = col[:, 0:1].to_broadcast([128, 128])
                                dstv = gb.t[:]
                            else:
                                src = col[:, 1:17].unsqueeze(2).to_broadcast([128, 16, 8])
                                dstv = gb.t[:].rearrange("p (b t) -> p b t", t=8)
                            S.op("dve", lambda e, d=dstv, s=src: e.tensor_copy(out=d, in_=s), reads=[modT.b], writes=[gb.b])
                            mm(bank.t[:, jj * 128:(jj + 1) * 128], gb.t[:], identf.t[:], True, True, [gb.b, identf.b], bank.b)
                        S.op("act", lambda e, kind=kind, w=w, half=half, bank=bank: e.activation(
                            out=G[kind][w].t[:, half * 512:(half + 1) * 512], in_=bank.t[:], func=AF.Copy),
                            reads=[bank.b], writes=[G[kind][w].b])

        def norm_phase(ph, blocks, l, w):
            ssl = [SBT(ph, f"ss{i}", [128, 4], F32) for i in range(NBMAX)]
            junk = SBT(ph, "junk", [128, D], BF16)
            xn = [SBT(ph, f"xn{i}", [128, D], BF16) for i in range(2)]
            tmpf = [SBT(ph, f"tmpf{i}", [128, 8, 128], F32) for i in range(2)]
            for bi, kind in enumerate(blocks):
                S.op("act", lambda e, bi=bi: e.activation(out=junk.t[:], in_=x[bi].t[:], func=AF.Square, accum_out=ssl[bi].t[:, 0:1]),
                     reads=[x[bi].b], writes=[junk.b, ssl[bi].b])
                S.op("act", lambda e, bi=bi: e.activation(out=ssl[bi].t[:, 1:2], in_=ssl[bi].t[:, 0:1], func=AF.Sqrt,
                                                          scale=1.0 / D, bias=EPS), reads=[ssl[bi].b], writes=[ssl[bi].b])
                S.op("dve", lambda e, bi=bi: e.reciprocal(out=ssl[bi].t[:, 2:3], in_=ssl[bi].t[:, 1:2]),
                     reads=[ssl[bi].b], writes=[ssl[bi].b])
                xt = xn[bi % 2]
                S.op("act", lambda e, bi=bi, xt=xt: e.activation(out=xt.t[:], in_=x[bi].t[:], func=AF.Copy, scale=ssl[bi].t[:, 2:3]),
                     reads=[x[bi].b, ssl[bi].b], writes=[xt.b])
                bank = pb[6 + bi % 2]
                bv = bfview(bank)
                for j in range(8):
                    tr(bv[:, j, :], xt.t[:, j * 128:(j + 1) * 128], identb.t[:], [xt.b, identb.b], bank.b)
                tf = tmpf[bi % 2]
                hv = hT[:, :, bi * 128:(bi + 1) * 128]
                if kind == 0:
                    a_ap = Amod.t[:, l, w, :, 0:1].to_broadcast([128, 8, 128])
                    s_ap = modT.t[:, l, 24 * w:24 * w + 8, 0:1].to_broadcast([128, 8, 128])
                    S.op("dve", lambda e, bv=bv, tf=tf, a_ap=a_ap: e.tensor_tensor(out=tf.t[:], in0=bv, in1=a_ap, op=ALU.mult),
                         reads=[bank.b, Amod.b], writes=[tf.b])
                    S.op("dve", lambda e, hv=hv, tf=tf, s_ap=s_ap: e.tensor_tensor(out=hv, in0=tf.t[:], in1=s_ap, op=ALU.add),
                         reads=[tf.b, modT.b], writes=[hTb[bi]])
                else:
                    for j in range(8):
                        a_ap = Amod.t[:, l, w, j, 1:17].unsqueeze(2).to_broadcast([128, 16, 8])
                        s_ap = modT.t[:, l, 24 * w + j, 1:17].unsqueeze(2).to_broadcast([128, 16, 8])
                        S.op("dve", lambda e, j=j, bv=bv, tf=tf, a_ap=a_ap: e.tensor_tensor(
                            out=tf.t[:, j, :].rearrange("p (b t) -> p b t", t=8), in0=bv[:, j, :].rearrange("p (b t) -> p b t", t=8),
                            in1=a_ap, op=ALU.mult), reads=[bank.b, Amod.b], writes=[tf.b])
                        S.op("dve", lambda e, j=j, hv=hv, tf=tf, s_ap=s_ap: e.tensor_tensor(
                            out=hv[:, j, :].rearrange("p (b t) -> p b t", t=8), in0=tf.t[:, j, :].rearrange("p (b t) -> p b t", t=8),
                            in1=s_ap, op=ALU.add), reads=[tf.b, modT.b], writes=[hTb[bi]])

        def resid_add(bi, kind, w, nt, bank, eng2="pool"):
            xs_ = x[bi].t[:, nt * 512:(nt + 1) * 512]
            g_ = G[kind][w].t[:, nt * 512:(nt + 1) * 512]
            tmp = resid_tmp[resid_ctr[0] % 3]
            resid_ctr[0] += 1
            S.op("dve", lambda e: e.tensor_tensor(out=tmp.t[:], in0=bank.t[:], in1=g_, op=ALU.mult),
                 reads=[bank.b, G[kind][w].b], writes=[tmp.b])
            S.op(eng2, lambda e: e.tensor_tensor(out=xs_, in0=xs_, in1=tmp.t[:], op=ALU.add), reads=[tmp.b, x[bi].b], writes=[x[bi].b])

        resid_tmp = [SBT(gs, f"rtmp{i}", [128, 512], F32) for i in range(3)]
        resid_ctr = [0]
        mmctr = [0]

        def dense_tm(blocks, sl, br, kchunks, act_chunk, evac, banks=(0, 1, 2)):
            pend = []
            for bi, kind in enumerate(blocks):
                bank = pb[banks[mmctr[0] % len(banks)]]
                mmctr[0] += 1
                n = len(kchunks)
                for i, kc in enumerate(kchunks):
                    lhsT, rb = act_chunk(bi, kc)
                    mm(bank.t[:], lhsT, sl.t[:, i, :], i == 0, (i == n - 1) and br is None, [sl.b, rb], bank.b)
                if br is not None:
                    mm(bank.t[:], ones.t[:], br.t[:], False, True, [ones.b, br.b], bank.b)
                r = evac(bi, kind, bank)
                if callable(r):
                    pend.append(r)
                    if len(pend) > 2:
                        pend.pop(0)()
            while pend:
                pend.pop(0)()

        def h_chunk(bi, kc):
            return hT[:, kc, bi * 128:(bi + 1) * 128], hTb[bi]

        def attn_layer(st, blocks, a, l):
            gblk0 = st * SB
            nb = len(blocks)
            has_s = 1 in blocks
            npb = sum(1 for k in blocks if k == 0)
            if has_s:
                with ExitStack() as ph:
                    build_gates(ph, l, sorted(set(blocks)))
                    norm_phase(ph, blocks, l, 0)
                    S.flush()
            with ExitStack() as pst:
                OT = [SBT(pst, f"OT{i}", [128, 8, 128], BF16) for i in range(nb)]
                qkb = [None] * nb
                Vp = [None] * nb
                kT = [None] * nb
                if has_s:
                    qkb[npb] = SBT(pst, "qkbS", [128, 1280], BF16)
                    Vp[npb] = SBT(pst, "VpS", [128, 4, 128], BF16)
                    kT[npb] = SBT(pst, "kTS", [128, 2, 128], BF16)
                smp = {}

                def temps(ph, n, nq=None):
                    return dict(
                        qT=[SBT(ph, f"qT{i}", [128, 8, 128], BF16) for i in range(nq or n)],
                        sm=[SBT(ph, f"sm{i}", [128, 4, 256], F32) for i in range(min(n, 2))],
                        pp=[SBT(ph, f"pp{i}", [128, 4, 256], BF16) for i in range(n)],
                        pT=[SBT(ph, f"pT{i}", [128, 8, 128], BF16) for i in range(min(n, 2))],
                        stt=[SBT(ph, f"stt{i}", [128, 32], F32) for i in range(n)])

                SCORE_SETS = ((pb[5], pb[6]), (pb[1], pb[2]), (pb[3], pb[4]))
                pv_half = [Buf("pv0"), Buf("pv1")]
                nsets = [2]

                def attn_pre(bi, kind, T):
                    n = len(T["qT"])
                    bq = pb[7]
                    bqv = bfview(bq)
                    for c in range(8):
                        tr(bqv[:, c, :], qkb[bi].t[:, c * 128:(c + 1) * 128], identb.t[:], [qkb[bi].b, identb.b], bq.b)
                    qt = T["qT"][bi % n]
                    S.op("act", lambda e: e.activation(out=qt.t[:], in_=bqv, func=AF.Copy), reads=[bq.b], writes=[qt.b])
                    bk = pb[7]
                    bkv = bfview(bk)
                    for kc in range(2):
                        tr(bkv[:, kc, :], qkb[bi].t[:, 1024 + kc * 128:1024 + (kc + 1) * 128], identb.t[:], [qkb[bi].b, identb.b], bk.b)
                    S.op("act", lambda e: e.activation(out=kT[bi].t[:], in_=bkv[:, 0:2, :], func=AF.Copy), reads=[bk.b], writes=[kT[bi].b])

                def attn_sa(bi, kind, gi, k, T, part):
                    qt = T["qT"][bi % len(T["qT"])]
                    n = len(T["sm"])
                    first = (kind == 0 and gblk0 + bi == 0)
                    if kind == 0:
                        kprev = kT[bi - 1] if bi > 0 else kTp[a]
                        msk = maskP0 if first else maskP
                    else:
                        msk = maskS
                        QmT, KTs, bmk = smp["QmT"], smp["KTs"], smp["bmk"]
                    kc = gi // 2
                    bA, bB = SCORE_SETS[k % nsets[0]]
                    if kind == 1 and part == 0:
                        for ci in range(2):
                            c = 2 * gi + ci
                            S.op("dve", lambda e, ci=ci, c=c: e.tensor_tensor(
                                out=QmT[ci].t[:], in0=qt.t[:, c, :].unsqueeze(1).to_broadcast([128, 16, 128]), in1=bmk.t[:], op=ALU.mult),
                                reads=[qt.b, bmk.b], writes=[QmT[ci].b])
                    for hf, bank in ((0, bA), (1, bB)):
                        if part != 0:
                            break
                        ps = slice(64 * hf, 64 * hf + 64)
                        for ci in range(2):
                            c = 2 * gi + ci
                            o_prev = bank.t[:, ci * 256:ci * 256 + 128]
                            o_own = bank.t[:, ci * 256 + 128:ci * 256 + 256]
                            if kind == 0:
                                mm(o_prev, qt.t[ps, c, :], kprev.t[ps, kc, :], True, True, [qt.b, kprev.b], bank.b)
                            else:
                                for b in range(16):
                                    mm(o_prev, QmT[ci].t[ps, b, :], KTs.t[ps, kc, b, :], b == 0, b == 15, [QmT[ci].b, KTs.b], bank.b)
                            mm(o_own, qt.t[ps, c, :], kT[bi].t[ps, kc, :], True, True, [qt.b, kT[bi].b], bank.b)
                    if part == 0:
                        return
                    s_ = T["sm"][k % len(T["sm"])]
                    p_ = T["pp"][k % len(T["pp"])]
                    t8 = T["stt"][k % len(T["stt"])]
                    for hf, bank in ((0, bA), (1, bB)):
                        S.op("dve", lambda e, hf=hf, bank=bank: e.scalar_tensor_tensor(
                            out=s_.t[:, 2 * hf:2 * hf + 2, :], in0=bank.t[:].rearrange("p (s k) -> p s k", k=256), scalar=0.125,
                            in1=msk.t[:].unsqueeze(1).to_broadcast([128, 2, 256]), op0=ALU.mult, op1=ALU.add),
                            reads=[bank.b, msk.b], writes=[s_.b])
                    sk = sinkB.t[:, a, 4 * gi:4 * gi + 4]
                    S.op("dve", lambda e: e.tensor_reduce(out=t8.t[:, 0:4], in_=s_.t[:], axis=AX.X, op=ALU.max), reads=[s_.b], writes=[t8.b])
                    S.op("dve", lambda e: e.tensor_tensor(out=t8.t[:, 0:4], in0=t8.t[:, 0:4], in1=sk, op=ALU.max),
                         reads=[t8.b, sinkB.b], writes=[t8.b])
                    S.op("dve", lambda e: e.tensor_scalar(out=t8.t[:, 4:8], in0=t8.t[:, 0:4], scalar1=-1.0, scalar2=None, op0=ALU.mult),
                         reads=[t8.b], writes=[t8.b])
                    S.op("dve", lambda e: e.tensor_tensor(out=t8.t[:, 12:16], in0=t8.t[:, 4:8], in1=sk, op=ALU.add),
                         reads=[t8.b, sinkB.b], writes=[t8.b])
                    for h4 in range(4):
                        S.op("act", lambda e, h4=h4: e.activation(
                            out=p_.t[:, h4, :], in_=s_.t[:, h4, :], func=AF.Exp, bias=t8.t[:, 4 + h4:5 + h4], scale=1.0,
                            accum_out=t8.t[:, 8 + h4:9 + h4]), reads=[s_.b, t8.b], writes=[p_.b, t8.b])
                    S.op("act", lambda e: e.activation(out=t8.t[:, 16:20], in_=t8.t[:, 12:16], func=AF.Exp), reads=[t8.b], writes=[t8.b])

                def attn_bpv(bi, kind, gi, k, T, part):
                    p_ = T["pp"][k % len(T["pp"])]
                    t8 = T["stt"][k % len(T["stt"])]
                    kc = gi // 2
                    vprev = None
                    if kind == 0:
                        vprev = Vp[bi - 1] if bi > 0 else Vpp[a]
                    else:
                        Vc = smp["Vc"]
                    if part == 0:
                        S.op("dve", lambda e: e.tensor_tensor(out=t8.t[:, 20:24], in0=t8.t[:, 8:12], in1=t8.t[:, 16:20], op=ALU.add),
                             reads=[t8.b], writes=[t8.b])
                        S.op("dve", lambda e: e.reciprocal(out=t8.t[:, 24:28], in_=t8.t[:, 20:24]), reads=[t8.b], writes=[t8.b])
                        S.op("dve", lambda e: e.tensor_tensor(out=p_.t[:], in0=p_.t[:],
                                                              in1=t8.t[:, 24:28].unsqueeze(2).to_broadcast([128, 4, 256]), op=ALU.mult),
                             reads=[t8.b, p_.b], writes=[p_.b])
                        return
                    bt = pb[7]
                    btv = bfview(bt)
                    for h4 in range(4):
                        for part in range(2):
                            tr(btv[:, h4 * 2 + part, :], p_.t[:, h4, part * 128:(part + 1) * 128], identb.t[:], [p_.b, identb.b], bt.b)
                    pt = T["pT"][k % len(T["pT"])]
                    S.op("act", lambda e: e.activation(out=pt.t[:], in_=btv, func=AF.Copy), reads=[bt.b], writes=[pt.b])
                    bo = Tile(pb[0].t, pv_half[k % 2])
                    c0 = (k % 2) * 256
                    for ci in range(2):
                        o = bo.t[:, c0 + ci * 128:c0 + (ci + 1) * 128]
                        seqm = []
                        for hf in range(2):
                            seqm.append((Vp[bi], 2 * kc + hf, (2 * hf + ci) * 2 + 1))
                        if kind == 0:
                            for hf in range(2):
                                seqm.append((vprev, 2 * kc + hf, (2 * hf + ci) * 2))
                        nn = len(seqm)
                        for i, (vt, g, pidx) in enumerate(seqm):
                            mm(o, vt.t[:, g, :], pt.t[:, pidx, :], i == 0, (i == nn - 1) and kind == 0, [vt.b, pt.b], bo.b)
                        if kind == 1:
                            for hf in range(2):
                                g = 2 * kc + hf
                                pidx = (2 * hf + ci) * 2
                                for b in range(16):
                                    mm(o[:, 8 * b:8 * b + 8], Vc.t[:, b, g, :], pt.t[:, pidx, 8 * b:8 * b + 8], False,
                                       (hf == 1 and b == 15), [Vc.b, pt.b], bo.b)
                    S.op("act", lambda e: e.activation(
                        out=OT[bi].t[:, 2 * gi:2 * gi + 2, :], in_=bo.t[:, c0:c0 + 256].rearrange("p (c t) -> p c t", t=128), func=AF.Copy),
                        reads=[bo.b], writes=[OT[bi].b])

                def attn_pipeline(bis, kind, T):
                    items = [(bi, gi) for bi in bis for gi in range(4)]
                    for hb in pv_half:
                        hb.w = pb[0].b.w
                        hb.r = list(pb[0].b.r)

                    def front(k, part=None):
                        bi, gi = items[k]
                        if part in (None, 0):
                            if gi == 0:
                                attn_pre(bi, kind, T)
                            attn_sa(bi, kind, gi, k, T, 0)
                        if part in (None, 1):
                            attn_sa(bi, kind, gi, k, T, 1)

                    skew = 2 if kind == 0 else 1
                    nsets[0] = skew + 1
                    for k0 in range(min(skew, len(items))):
                        front(k0)
                    for k, (bi, gi) in enumerate(items):
                        if k + skew < len(items):
                            front(k + skew, 0)
                        attn_bpv(bi, kind, gi, k, T, 0)
                        attn_bpv(bi, kind, gi, k, T, 1)
                        if k + skew < len(items):
                            front(k + skew, 1)
                        if kind == 0 and bi == SB - 1 and gi == 3:
                            S.op("dve", lambda e, bi=bi: e.tensor_copy(out=kTp[a].t[:], in_=kT[bi].t[:]), reads=[kT[bi].b], writes=[kTp[a].b])
                            S.op("dve", lambda e, bi=bi: e.tensor_copy(out=Vpp[a].t[:], in_=Vp[bi].t[:]), reads=[Vp[bi].b], writes=[Vpp[a].b])
                    pb[0].b.r = list(pb[0].b.r) + [ev for hb in pv_half for ev in ([hb.w] if hb.w else []) + hb.r]

                def phase_c():
                    for nt in range(2):
                        sl, _ = w_next("wo", a, nt)
                        dense_tm(blocks, sl, None, list(range(8)), lambda bi, kc: (OT[bi].t[:, kc, :], OT[bi].b),
                                 lambda bi, kind, bank, nt=nt: resid_add(bi, kind, 0, nt, bank), banks=(2, 3, 4))
                    S.flush()

                with ExitStack() as ph:
                    if not has_s:
                        build_gates(ph, l, sorted(set(blocks)))
                        norm_phase(ph, blocks, l, 0)
                    for i in range(npb):
                        qkb[i] = SBT(ph, f"qkb{i}", [128, 1280], BF16)
                        Vp[i] = SBT(ph, f"Vp{i}", [128, 4, 128], BF16)
                        kT[i] = SBT(ph, f"kT{i}", [128, 2, 128], BF16)
                    kvf = [SBT(ph, f"kvf{i}", [128, 512], F32) for i in range(2)]
                    rot = [SBT(ph, f"rot{i}", [128, 8, 16], F32) for i in range(2)]
                    rtm = [SBT(ph, f"rtm{i}", [128, 8, 8], F32) for i in range(2)]
                    TA = temps(ph, 3, 2)
                    qfs = [SBT(ph, f"qf{i}", [128, 512], F32) for i in range(2)]
                    qfc = [0]
                    for i in range(nb):
                        S.op("dve", lambda e, i=i: e.memset(Vp[i].t[:], 0.0), writes=[Vp[i].b])

                    def evac_qkv(nt):
                        def f(bi, kind, bank):
                            import os
                            ksub = int(os.environ.get("KSUB", "9"))
                            if ksub == 0:
                                S.op("act", lambda e: e.activation(out=qkb[bi].t[:, 0:512], in_=bank.t[:], func=AF.Copy),
                                     reads=[bank.b], writes=[qkb[bi].b])
                                return
                            gb = gblk0 + bi if kind == 0 else NPB
                            is_out = (kind == 1) or (gb == NPB - 1)
                            HF = 2 if nt < 2 else 1
                            W_ = HF * 256
                            pv3 = bank.t[:, 0:W_].rearrange("p (hf cl d) -> p hf cl d", hf=HF, d=64)
                            if nt < 2:
                                qv3 = qkb[bi].t[:, nt * 512:(nt + 1) * 512].rearrange("p (cl hf d) -> p hf cl d", hf=2, d=64)
                            else:
                                qv3 = qkb[bi].t[:, 1024:1280].rearrange("p (hf cl d) -> p hf cl d", hf=1, d=64)
                            S.op("act", lambda e: e.activation(out=qv3[:, :, :, 16:64], in_=pv3[:, :, :, 16:64], func=AF.Copy),
                                 reads=[bank.b], writes=[qkb[bi].b])
                            if ksub == 1:
                                return
                            nh = 4 * HF
                            r = rot[(bi + nt) % 2]
                            t_ = rtm[(bi + nt) % 2]
                            qf = qfs[qfc[0] % 2]
                            qfc[0] += 1
                            S.op("act", lambda e: e.activation(out=qf.t[:], in_=bank.t[:], func=AF.Copy), reads=[bank.b], writes=[qf.b])
                            src3 = qf.t[:, 0:nh * 64].rearrange("p (h d) -> p h d", d=64)
                            cosb = cs.t[:, gb, 0:8].unsqueeze(1).to_broadcast([128, nh, 8])
                            sinb = cs.t[:, gb, 8:16].unsqueeze(1).to_broadcast([128, nh, 8])
                            x1 = src3[:, :, 0:8]
                            x2 = src3[:, :, 8:16]
                            r1 = r.t[:, 0:nh, 0:8]
                            r2 = r.t[:, 0:nh, 8:16]
                            tv = t_.t[:, 0:nh, :]
                            rd = [qf.b, cs.b]
                            S.op("dve", lambda e: e.tensor_tensor(out=r1, in0=x1, in1=cosb, op=ALU.mult), reads=rd, writes=[r.b])
                            if ksub == 2:
                                return
                            S.op("dve", lambda e: e.tensor_tensor(out=tv, in0=x2, in1=sinb, op=ALU.mult), reads=rd, writes=[t_.b])
                            S.op("dve", lambda e: e.tensor_tensor(out=r1, in0=r1, in1=tv, op=ALU.subtract), reads=[t_.b, r.b], writes=[r.b])
                            S.op("dve", lambda e: e.tensor_tensor(out=r2, in0=x2, in1=cosb, op=ALU.mult), reads=rd + [r.b], writes=[r.b])
                            S.op("dve", lambda e: e.tensor_tensor(out=tv, in0=x1, in1=sinb, op=ALU.mult), reads=rd + [r.b], writes=[t_.b])
                            S.op("dve", lambda e: e.tensor_tensor(out=r2, in0=r2, in1=tv, op=ALU.add), reads=[t_.b, r.b], writes=[r.b])
                            if ksub == 3:
                                return
                            if nt < 2:
                                for hf in range(2):
                                    S.op("dve", lambda e, hf=hf: e.tensor_copy(out=qv3[:, hf, :, 0:16], in_=r.t[:, 4 * hf:4 * hf + 4, :]),
                                         reads=[r.b], writes=[qkb[bi].b])
                            else:
                                S.op("dve", lambda e: e.tensor_copy(out=qv3[:, 0, :, 0:16], in_=r.t[:, 0:4, :]), reads=[r.b], writes=[qkb[bi].b])
                            if ksub == 4:
                                return
                            if nt == 2:
                                vv = bank.t[:, 256:512].rearrange("p (g2 gp d) -> p g2 gp d", gp=2, d=64)
                                vd = Vp[bi].t[:].rearrange("p (g2 gp) c -> p g2 gp c", gp=2)
                                for gp in range(2):
                                    S.op("act", lambda e, gp=gp: e.activation(out=vd[:, :, gp, gp * 64:gp * 64 + 64], in_=vv[:, :, gp, :], func=AF.Copy),
                                         reads=[bank.b], writes=[Vp[bi].b])
                                if is_out:
                                    kf = kvf[kind]
                                    S.op("act", lambda e: e.activation(out=kf.t[:], in_=bank.t[:], func=AF.Copy), reads=[bank.b], writes=[kf.b])
                                    S.op("dve", lambda e: e.tensor_copy(out=kf.t[:, 0:256].rearrange("p (h d) -> p h d", d=64)[:, :, 0:16],
                                                                        in_=r.t[:, 0:4, :]), reads=[r.b, kf.b], writes=[kf.b])
                                    if kind == 0:
                                        S.dma("sp", lambda e: e.dma_start(out=O["kp"][a], in_=kf.t[:, 0:256]), reads=[kf.b], out=True)
                                        S.dma("sp", lambda e: e.dma_start(out=O["vp"][a], in_=kf.t[:, 256:512]), reads=[kf.b], out=True)
                                    else:
                                        for b in range(16):
                                            S.dma("sp", lambda e, b=b: e.dma_start(out=O["ks"][a][b, 120:128, :], in_=kf.t[8 * b:8 * b + 8, 0:256]),
                                                  reads=[kf.b], out=True)
                                            S.dma("sp", lambda e, b=b: e.dma_start(out=O["vs"][a][b, 120:128, :], in_=kf.t[8 * b:8 * b + 8, 256:512]),
                                                  reads=[kf.b], out=True)
                        return f

                    for nt in range(3):
                        sl, br = w_next("qkv", a, nt)
                        dense_tm(blocks, sl, br, list(range(8)), h_chunk, evac_qkv(nt))
                    if dbg == "attnA1":
                        dump(qkb[1].t[:], 1280, qkb[1].b, bf=True)
                        dump(Vp[1].t[:].rearrange("p g c -> p (g c)"), 512, Vp[1].b, bf=True)
                        stop()
                    attn_pipeline(list(range(npb)), 0, TA)
                    if has_s:
                        S.flush()
                    else:
                        phase_c()
                    if dbg == "attnA":
                        dump(qkb[1].t[:], 1280, qkb[1].b, bf=True)
                        dump(Vp[1].t[:].rearrange("p g c -> p (g c)"), 512, Vp[1].b, bf=True)
                        dump(kT[1].t[:].rearrange("p g c -> p (g c)"), 256, kT[1].b, bf=True)
                        for i in range(2):
                            dump(OT[i].t[:].rearrange("p c t -> p (c t)"), 1024, OT[i].b, bf=True)
                        stop()

                if has_s:
                    with ExitStack() as ph:
                        TB = temps(ph, 2, 1)
                        ckb = SBT(ph, "ckb", [128, 8, 256], BF16)
                        KTs = SBT(ph, "KTs", [128, 2, 16, 128], BF16)
                        Vc = SBT(ph, "Vc", [128, 16, 4, 128], BF16)
                        QmT = [SBT(ph, f"QmT{i}", [128, 16, 128], BF16) for i in range(2)]
                        bmk = SBT(ph, "bmk", [128, 16, 128], BF16)
                        smp.update(QmT=QmT, KTs=KTs, Vc=Vc, bmk=bmk)
                        S.dma("pool", lambda e: e.dma_start(out=bmk.t[:].rearrange("p b t -> p (b t)"), in_=I["bmask"]), writes=[bmk.b])
                        S.op("dve", lambda e: e.memset(Vc.t[:], 0.0), writes=[Vc.b])
                        cvv = I["cv"][a].rearrange("b k (g2 gp d) -> gp b k g2 d", gp=2, d=64)
                        for gp in range(2):
                            for b in range(16):
                                S.dma("pool", lambda e, gp=gp, b=b: e.dma_start(
                                    out=Vc.t[:, b].rearrange("p (g2 gp) c -> p g2 gp c", gp=2)[:, :, gp, gp * 64:gp * 64 + 64], in_=cvv[gp][b]),
                                    writes=[Vc.b])
                        for half in range(2):
                            S.dma("pool", lambda e, half=half: e.dma_start(
                                out=ckb.t[:], in_=I["ck"][a][8 * half:8 * half + 8].rearrange("b k c -> k b c")), writes=[ckb.b])
                            for b8 in range(8):
                                b = 8 * half + b8
                                bank = pb[3 + b % 2]
                                bv = bfview(bank)
                                for kc in range(2):
                                    tr(bv[:, kc, :], ckb.t[:, b8, kc * 128:(kc + 1) * 128], identb.t[:], [ckb.b, identb.b], bank.b)
                                S.op("act", lambda e, b=b, bv=bv: e.activation(out=KTs.t[:, :, b, :], in_=bv[:, 0:2, :], func=AF.Copy),
                                     reads=[bank.b], writes=[KTs.b])
                        for nm_i, nm_o in (("ck", "ks"), ("cv", "vs")):
                            S.dma("sp", lambda e, nm_i=nm_i, nm_o=nm_o: e.dma_start(out=O[nm_o][a][:, 0:120, :], in_=I[nm_i][a][:, 8:128, :]),
                                  out=True, hold=False)
                        attn_pipeline([npb], 1, TB)
                        S.flush()

                if has_s:
                    phase_c()

        def sgu_layer(st, blocks, a, l):
            if 1 in blocks:
                with ExitStack() as ph:
                    build_gates(ph, l, sorted(set(blocks)))
                    norm_phase(ph, blocks, l, 0)
                    S.flush()
            with ExitStack() as ph:
                if 1 not in blocks:
                    build_gates(ph, l, sorted(set(blocks)))
                    norm_phase(ph, blocks, l, 0)
                nb = len(blocks)
                lng = SBT(ph, "lng", [128, 2048], F32)
                lnb = SBT(ph, "lnb", [128, 2048], F32)
                S.dma("sp", lambda e: e.dma_start(out=lng.t[:], in_=I["lnrows"][a].partition_broadcast(128)), writes=[lng.b])
                S.dma("sp", lambda e: e.dma_start(out=lnb.t[:], in_=I["lnrows"][2 + a].partition_broadcast(128)), writes=[lnb.b])
                pTs = [SBT(ph, f"pTs{i}", [128, 16, 128], BF16) for i in range(nb)]
                ub = [SBT(ph, f"ub{i}", [128, 512], BF16) for i in range(2 * nb)]
                vtmp = [SBT(ph, f"vtmp{i}", [128, 512], F32) for i in range(3)]
                vnb = [SBT(ph, f"vnb{i}", [128, 512], BF16) for i in range(4)]
                pg = [SBT(ph, f"pg{i}", [128, 512], BF16) for i in range(4)]
                statsl = [SBT(ph, f"stats{i}", [128, 4, 6], F32) for i in range(nb)]
                mvl = [SBT(ph, f"mv{i}", [128, 4], F32) for i in range(nb)]
                vout = SBT(ph, "vout", [128, 2048], F32) if 1 in blocks else None
                vc = [0]

                def evac_stats(t4):
                    def f(bi, kind, bank):
                        vt = vtmp[vc[0] % 3]
                        vc[0] += 1
                        S.op("act", lambda e: e.activation(out=vt.t[:], in_=bank.t[:], func=AF.Gelu), reads=[bank.b], writes=[vt.b])
                        S.op("dve", lambda e: e.bn_stats(out=statsl[bi].t[:, t4, :], in_=vt.t[:]), reads=[vt.b], writes=[statsl[bi].b])
                        if t4 == 3:
                            S.op("dve", lambda e: e.bn_aggr(out=mvl[bi].t[:, 0:2], in_=statsl[bi].t[:]), reads=[statsl[bi].b], writes=[mvl[bi].b])
                            S.op("act", lambda e: e.activation(out=mvl[bi].t[:, 2:3], in_=mvl[bi].t[:, 1:2], func=AF.Sqrt, scale=1.0, bias=EPS),
                                 reads=[mvl[bi].b], writes=[mvl[bi].b])
                            S.op("dve", lambda e: e.reciprocal(out=mvl[bi].t[:, 3:4], in_=mvl[bi].t[:, 2:3]), reads=[mvl[bi].b], writes=[mvl[bi].b])
                    return f

                for t4 in range(4):
                    sl, br = w_next("win", a, 4 + t4)
                    dense_tm(blocks, sl, br, list(range(8)), h_chunk, evac_stats(t4))

                for g in range(4):
                    def evac_u(bi, kind, bank, g=g):
                        u_ = ub[(g % 2) * nb + bi]
                        S.op("act", lambda e: e.activation(out=u_.t[:], in_=bank.t[:], func=AF.Gelu), reads=[bank.b], writes=[u_.b])

                    def evac_v(bi, kind, bank, g=g):
                        vt = vtmp[vc[0] % 3]
                        vn = vnb[vc[0] % 4]
                        p_ = pg[vc[0] % 4]
                        vc[0] += 1
                        u_ = ub[(g % 2) * nb + bi]
                        S.op("act", lambda e: e.activation(out=vt.t[:], in_=bank.t[:], func=AF.Gelu), reads=[bank.b], writes=[vt.b])
                        S.op("dve", lambda e: e.tensor_scalar(out=vt.t[:], in0=vt.t[:], scalar1=mvl[bi].t[:, 0:1], scalar2=mvl[bi].t[:, 3:4],
                                                              op0=ALU.subtract, op1=ALU.mult), reads=[vt.b, mvl[bi].b], writes=[vt.b])
                        S.op("dve", lambda e: e.tensor_tensor(out=vt.t[:], in0=vt.t[:], in1=lng.t[:, g * 512:(g + 1) * 512], op=ALU.mult),
                             reads=[vt.b, lng.b], writes=[vt.b])
                        if kind == 1:
                            S.op("dve", lambda e: e.tensor_tensor(out=vout.t[:, g * 512:(g + 1) * 512], in0=vt.t[:],
                                                                  in1=lnb.t[:, g * 512:(g + 1) * 512], op=ALU.add),
                                 reads=[vt.b, lnb.b], writes=[vout.b])
                            S.op("dve", lambda e: e.tensor_copy(out=vn.t[:], in_=vout.t[:, g * 512:(g + 1) * 512]), reads=[vout.b], writes=[vn.b])
                        else:
                            S.op("dve", lambda e: e.tensor_tensor(out=vn.t[:], in0=vt.t[:], in1=lnb.t[:, g * 512:(g + 1) * 512], op=ALU.add),
                                 reads=[vt.b, lnb.b], writes=[vn.b])
                        vcv = vc[0]

                        def tail():
                            evac_v_tail(bi, kind, g, vn, p_, u_, vcv)
                        return tail

                    def evac_v_tail(bi, kind, g, vn, p_, u_, vcv):
                        bm = pb[3 + vcv % 2]
                        mm(bm.t[:], wm.t[:, kind, a, g, :], vn.t[:], True, True, [wm.b, vn.b], bm.b)
                        S.op("dve", lambda e: e.scalar_tensor_tensor(out=p_.t[:], in0=bm.t[:], scalar=bsp.t[:, kind, 4 * a + g:4 * a + g + 1],
                                                                     in1=u_.t[:], op0=ALU.add, op1=ALU.mult),
                             reads=[bm.b, bsp.b, u_.b], writes=[p_.b])
                        bt = pb[5 + vcv % 2]
                        btv = bfview(bt)
                        for j in range(4):
                            tr(btv[:, j, :], p_.t[:, j * 128:(j + 1) * 128], identb.t[:], [p_.b, identb.b], bt.b)
                        S.op("act", lambda e: e.activation(out=pTs[bi].t[:, 4 * g:4 * g + 4, :], in_=btv[:, 0:4, :], func=AF.Copy),
                             reads=[bt.b], writes=[pTs[bi].b])

                    sl, br = w_next("win", a, g)
                    dense_tm(blocks, sl, br, list(range(8)), h_chunk, evac_u)
                    sl, br = w_next("win", a, 4 + g)
                    dense_tm(blocks, sl, br, list(range(8)), h_chunk, evac_v)
                if vout is not None:
                    S.dma("sp", lambda e: e.dma_start(out=O["sguv"][a], in_=vout.t[:]), reads=[vout.b], out=True)
                for kh in range(2):
                    for nt in range(2):
                        sl, _ = w_next("wout", a, kh * 2 + nt)
                        dense_tm(blocks, sl, None, list(range(8)), lambda bi, kc, kh=kh: (pTs[bi].t[:, kh * 8 + kc, :], pTs[bi].b),
                                 lambda bi, kind, bank, nt=nt: resid_add(bi, kind, 0, nt, bank))
                S.flush()

        def ffn_layer(st, blocks, l):
            if 1 in blocks:
                with ExitStack() as ph:
                    norm_phase(ph, blocks, l, 1)
                    S.flush()
            with ExitStack() as ph:
                if 1 not in blocks:
                    norm_phase(ph, blocks, l, 1)
                nb = len(blocks)
                npb = sum(1 for k in blocks if k == 0)
                N = npb * 128
                has_s = 1 in blocks
                gT = ph.enter_context(nc.sbuf_tensor(uname("gT"), [128, 22, nb * 128], BF16))
                gTb = Buf("gT")
                nset = 2 if has_s else 3
                ab = [[SBT(ph, f"ab{i}{h}", [128, 2 + SB * 128], F32) for h in range(2)] for i in range(nset)]
                tt = [[SBT(ph, f"tt{i}{h}", [128, SB * 128], F32) for h in range(2)] for i in range(nset)]
                qq = [[SBT(ph, f"qq{i}{h}", [128, SB * 128], F32) for h in range(2)] for i in range(nset)]
                if has_s:
                    abS = [[SBT(ph, f"abS{i}{h}", [128, 16, 10], F32) for h in range(2)] for i in range(2)]
                    ttS = [[SBT(ph, f"ttS{i}{h}", [128, 16, 8], F32) for h in range(2)] for i in range(2)]
                    stS = SBT(ph, "stS", [128, NFC, 32], F32)
                    aLs = SBT(ph, "aLs", [128, NFC, 32], F32)
                    stg = [SBT(ph, "stg0", [32, 1408], F32)] * 2
                    for q4 in range(4):
                        sg = stg[q4 % 2]
                        S.dma("sp", lambda e, q4=q4, sg=sg: e.dma_start(out=sg.t[:], in_=I["sconv"][l][:, q4 * 1408:(q4 + 1) * 1408]),
                              writes=[sg.b])
                        bank = pb[6 + q4 % 2]
                        for j in range(11):
                            mm(bank.t[:, j * 32:(j + 1) * 32], sg.t[:, j * 128:(j + 1) * 128], identf.t[0:32, 0:32], True, True, [sg.b, identf.b], bank.b)
                        S.op("act", lambda e, q4=q4, bank=bank: e.activation(
                            out=stS.t[:, q4 * 11:(q4 + 1) * 11, :].rearrange("p j r -> p (j r)"), in_=bank.t[:, 0:352], func=AF.Copy),
                            reads=[bank.b], writes=[stS.b])
                cnt = [0]
                ffn_pend = []
                for jj in range(11):
                    sl, _ = w_next("wup", l, jj)
                    def chunk(jx):
                        j = 2 * jj + jx
                        par = cnt[0] % nset
                        cnt[0] += 1
                        banks = (pb[0 + 2 * par], pb[1 + 2 * par])
                        bS = pb[4 + par]
                        for h in range(2):
                            wcol = sl.t[:, :, h * 256 + jx * 128:h * 256 + jx * 128 + 128]
                            if npb:
                                for k in range(8):
                                    mm(banks[h].t[:, 0:N], wcol[:, k, :], hT[:, k, 0:N], k == 0, k == 7,
                                       [sl.b] + hTb[0:npb], banks[h].b)
                            if has_s:
                                for k in range(8):
                                    mm(bS.t[:, h * 128:(h + 1) * 128], wcol[:, k, :], hT[:, k, N:N + 128], k == 0, k == 7,
                                       [sl.b, hTb[npb]], bS.b)
                        for h in range(2):
                            fc = j + 22 * h
                            w0 = cwT.t[:, fc, 4 * l + 0:4 * l + 1]
                            w1 = cwT.t[:, fc, 4 * l + 1:4 * l + 2]
                            w2 = cwT.t[:, fc, 4 * l + 2:4 * l + 3]
                            cb = cwT.t[:, fc, 4 * l + 3:4 * l + 4]
                            if npb:
                                a_ = ab[par][h]
                                t_ = tt[par][h]
                                S.op("act", lambda e, a_=a_, h=h: e.activation(out=a_.t[:, 2:2 + N], in_=banks[h].t[:, 0:N], func=AF.Copy),
                                     reads=[banks[h].b], writes=[a_.b])
                                if h == 0:
                                    S.op("dve", lambda e, a_=a_, fc=fc: e.tensor_copy(out=a_.t[:, 0:2], in_=stP.t[:, l, fc, :]), reads=[stPb[l][fc]], writes=[a_.b])
                                    S.op("dve", lambda e, a_=a_, t_=t_, w2=w2, cb=cb: e.tensor_scalar(
                                        out=t_.t[:, 0:N], in0=a_.t[:, 2:2 + N], scalar1=w2, scalar2=cb, op0=ALU.mult, op1=ALU.add),
                                        reads=[a_.b, cwT.b], writes=[t_.b])
                                    S.op("dve", lambda e, a_=a_, t_=t_, w1=w1: e.scalar_tensor_tensor(
                                        out=t_.t[:, 0:N], in0=a_.t[:, 1:1 + N], scalar=w1, in1=t_.t[:, 0:N], op0=ALU.mult, op1=ALU.add),
                                        reads=[a_.b, cwT.b, t_.b], writes=[t_.b])
                                    S.op("dve", lambda e, a_=a_, t_=t_, w0=w0: e.scalar_tensor_tensor(
                                        out=t_.t[:, 0:N], in0=a_.t[:, 0:N], scalar=w0, in1=t_.t[:, 0:N], op0=ALU.mult, op1=ALU.add),
                                        reads=[a_.b, cwT.b, t_.b], writes=[t_.b])
                                    S.op("dve", lambda e, a_=a_, fc=fc: e.tensor_copy(out=stP.t[:, l, fc, :], in_=a_.t[:, N:N + 2]), reads=[a_.b], writes=[stPb[l][fc]])
                                else:
                                    q1, q0 = qq[par]
                                    S.op("pool", lambda e, a_=a_, fc=fc: e.tensor_copy(out=a_.t[:, 0:2], in_=stP.t[:, l, fc, :]), reads=[stPb[l][fc]], writes=[a_.b])
                                    S.op("act", lambda e, a_=a_, q1=q1, w1=w1: e.activation(out=q1.t[:, 0:N], in_=a_.t[:, 1:1 + N], func=AF.Copy, scale=w1),
                                         reads=[a_.b, cwT.b], writes=[q1.b])
                                    S.op("act", lambda e, a_=a_, q0=q0, w0=w0: e.activation(out=q0.t[:, 0:N], in_=a_.t[:, 0:N], func=AF.Copy, scale=w0),
                                         reads=[a_.b, cwT.b], writes=[q0.b])
                                    S.op("act", lambda e, a_=a_, t_=t_, w2=w2, cb=cb: e.activation(
                                        out=t_.t[:, 0:N], in_=a_.t[:, 2:2 + N], func=AF.Identity, scale=w2, bias=cb),
                                        reads=[a_.b, cwT.b], writes=[t_.b])
                                    S.op("pool", lambda e, t_=t_, q1=q1: e.tensor_tensor(out=t_.t[:, 0:N], in0=t_.t[:, 0:N], in1=q1.t[:, 0:N], op=ALU.add),
                                         reads=[q1.b, t_.b], writes=[t_.b])
                                    S.op("dve", lambda e, t_=t_, q0=q0: e.tensor_tensor(out=t_.t[:, 0:N], in0=t_.t[:, 0:N], in1=q0.t[:, 0:N], op=ALU.add),
                                         reads=[q0.b, t_.b], writes=[t_.b])
                                    S.op("pool", lambda e, a_=a_, fc=fc: e.tensor_copy(out=stP.t[:, l, fc, :], in_=a_.t[:, N:N + 2]), reads=[a_.b], writes=[stPb[l][fc]])
                            if has_s:
                                a_ = abS[par][h]
                                t_ = ttS[par][h]
                                S.op("act", lambda e, a_=a_, h=h: e.activation(
                                    out=a_.t[:, :, 2:10], in_=bS.t[:, h * 128:(h + 1) * 128].rearrange("p (b t) -> p b t", t=8), func=AF.Copy),
                                    reads=[bS.b], writes=[a_.b])
                                S.op("dve", lambda e, a_=a_, fc=fc: e.tensor_copy(
                                    out=a_.t[:, :, 0:2], in_=stS.t[:, fc, :].rearrange("p (b t) -> p b t", t=2)), reads=[stS.b], writes=[a_.b])
                                S.op("dve", lambda e, a_=a_, t_=t_, w2=w2, cb=cb: e.tensor_scalar(
                                    out=t_.t[:], in0=a_.t[:, :, 2:10], scalar1=w2, scalar2=cb, op0=ALU.mult, op1=ALU.add),
                                    reads=[a_.b, cwT.b], writes=[t_.b])
                                S.op("dve", lambda e, a_=a_, t_=t_, w1=w1: e.scalar_tensor_tensor(
                                    out=t_.t[:], in0=a_.t[:, :, 1:9], scalar=w1, in1=t_.t[:], op0=ALU.mult, op1=ALU.add),
                                    reads=[a_.b, cwT.b, t_.b], writes=[t_.b])
                                S.op("dve", lambda e, a_=a_, t_=t_, w0=w0: e.scalar_tensor_tensor(
                                    out=t_.t[:], in0=a_.t[:, :, 0:8], scalar=w0, in1=t_.t[:], op0=ALU.mult, op1=ALU.add),
                                    reads=[a_.b, cwT.b, t_.b], writes=[t_.b])
                                S.op("dve", lambda e, a_=a_, fc=fc: e.tensor_copy(
                                    out=aLs.t[:, fc, :].rearrange("p (b t) -> p b t", t=2), in_=a_.t[:, :, 8:10]), reads=[a_.b], writes=[aLs.b])
                        def tail():
                            if npb:
                                tg, tu = tt[par][0], tt[par][1]
                                S.op("act", lambda e: e.activation(out=tg.t[:, 0:N], in_=tg.t[:, 0:N], func=AF.Silu), reads=[tg.b], writes=[tg.b])
                                S.op("dve", lambda e: e.tensor_tensor(out=gT[:, j, 0:N], in0=tg.t[:, 0:N], in1=tu.t[:, 0:N], op=ALU.mult),
                                     reads=[tg.b, tu.b], writes=[gTb])
                            if has_s:
                                tgs, tus = ttS[par][0], ttS[par][1]
                                S.op("act", lambda e: e.activation(out=tgs.t[:], in_=tgs.t[:], func=AF.Silu), reads=[tgs.b], writes=[tgs.b])
                                S.op("dve", lambda e: e.tensor_tensor(
                                    out=gT[:, j, N:N + 128].rearrange("p (b t) -> p b t", t=8), in0=tgs.t[:], in1=tus.t[:], op=ALU.mult),
                                    reads=[tgs.b, tus.b], writes=[gTb])
                        return tail

                    for jx in range(2):
                        tl = chunk(jx)
                        if has_s:
                            tl()
                        else:
                            if ffn_pend:
                                ffn_pend.pop(0)()
                            ffn_pend.append(tl)
                while ffn_pend:
                    ffn_pend.pop(0)()
                for kg in range(3):
                    nk = 8 if kg < 2 else 6
                    for nt in range(2):
                        sl, _ = w_next("wdown", l, kg * 2 + nt)
                        dense_tm(blocks, sl, None, list(range(nk)),
                                 lambda bi, kc, kg=kg: (gT[:, kg * 8 + kc, bi * 128:(bi + 1) * 128], gTb),
                                 lambda bi, kind, bank, nt=nt: resid_add(bi, kind, 1, nt, bank, "dve"), banks=(6, 7))
                if has_s:
                    for c0 in range(0, NFC, 4):
                        bank = pb[(c0 // 4) % 2]
                        sg = stg[(c0 // 4) % 2]
                        for j in range(4):
                            mm(bank.t[0:32, j * 128:(j + 1) * 128], aLs.t[:, c0 + j, :], identf.t[:], True, True, [aLs.b, identf.b], bank.b)
                        S.op("act", lambda e, bank=bank, sg=sg: e.activation(out=sg.t[:, 0:512], in_=bank.t[0:32, :], func=AF.Copy),
                             reads=[bank.b], writes=[sg.b])
                        S.dma("sp", lambda e, sg=sg, c0=c0: e.dma_start(out=O["convs"][l][:, c0 * 128:(c0 + 4) * 128], in_=sg.t[:, 0:512]),
                              reads=[sg.b], out=True)
                S.flush()

        for st in range(NST):
            blocks = [0] * SB + ([1] if st == NST - 1 else [])
            with ExitStack() as ph:
                for bi, kind in enumerate(blocks):
                    src = I["xp"][(st * SB + bi) * 128:(st * SB + bi + 1) * 128, :] if kind == 0 else I["xs"]
                    S.dma("sp", lambda e, bi=bi, src=src: e.dma_start(out=x[bi].t[:], in_=src), writes=[x[bi].b])
                S.flush()
            for l in range(4):
                if l % 2 == 0:
                    attn_layer(st, blocks, l // 2, l)
                else:
                    sgu_layer(st, blocks, l // 2, l)
                if dbg == f"mix{l}" and st == 0:
                    for i in range(4):
                        dump(x[i].t[:], 1024, x[i].b)
                    stop()
                ffn_layer(st, blocks, l)
                if dbg == f"ffn{l}" and st == 0:
                    for i in range(4):
                        dump(x[i].t[:], 1024, x[i].b)
                    S.op("dve", lambda e: e.engine_nop(), reads=stP_all, writes=[stP.b])
                    dump(stP.t[:].rearrange("p l c t -> p (l c t)"), 352, stP.b)
                    stop()
            with ExitStack() as ph:
                ssl = [SBT(ph, f"fss{i}", [128, 4], F32) for i in range(NBMAX)]
                junk = SBT(ph, "fjunk", [128, D], BF16)
                yo = [SBT(ph, f"yo{i}", [128, D], F32) for i in range(2)]
                for bi, kind in enumerate(blocks):
                    S.op("act", lambda e, bi=bi: e.activation(out=junk.t[:], in_=x[bi].t[:], func=AF.Square, accum_out=ssl[bi].t[:, 0:1]),
                         reads=[x[bi].b], writes=[junk.b, ssl[bi].b])
                    S.op("act", lambda e, bi=bi: e.activation(out=ssl[bi].t[:, 1:2], in_=ssl[bi].t[:, 0:1], func=AF.Sqrt,
                                                              scale=1.0 / D, bias=EPS), reads=[ssl[bi].b], writes=[ssl[bi].b])
                    S.op("dve", lambda e, bi=bi: e.reciprocal(out=ssl[bi].t[:, 2:3], in_=ssl[bi].t[:, 1:2]),
                         reads=[ssl[bi].b], writes=[ssl[bi].b])
                    y_ = yo[bi % 2]
                    S.op("dve", lambda e, bi=bi, y_=y_: e.scalar_tensor_tensor(out=y_.t[:], in0=x[bi].t[:], scalar=ssl[bi].t[:, 2:3],
                                                                              in1=normfB.t[:], op0=ALU.mult, op1=ALU.mult),
                         reads=[x[bi].b, ssl[bi].b, normfB.b], writes=[y_.b])
                    dst = O["yp"][(st * SB + bi) * 128:(st * SB + bi + 1) * 128, :] if kind == 0 else O["ys"]
                    S.dma("sp", lambda e, y_=y_, dst=dst: e.dma_start(out=dst, in_=y_.t[:]), reads=[y_.b], out=True)
                S.flush()

        with ExitStack() as ph:
            stg2 = [SBT(ph, f"stg2{i}", [2, 512], F32) for i in range(2)]
            for l in range(4):
                for c0 in range(0, NFC, 4):
                    i = (l * 11 + c0 // 4) % 2
                    bank = pb[i]
                    for j in range(4):
                        mm(bank.t[0:2, j * 128:(j + 1) * 128], stP.t[:, l, c0 + j, :], identf.t[:], True, True, [stPb[l][c0 + j], identf.b], bank.b)
                    S.op("act", lambda e, bank=bank, i=i: e.activation(out=stg2[i].t[:], in_=bank.t[0:2, :], func=AF.Copy),
                         reads=[bank.b], writes=[stg2[i].b])
                    S.dma("sp", lambda e, i=i, l=l, c0=c0: e.dma_start(out=O["convp"][l][:, c0 * 128:(c0 + 4) * 128], in_=stg2[i].t[:]),
                          reads=[stg2[i].b], out=True)
            S.flush(final=True)
        assert wstate["next"] == len(wseq)
    return nc


_CACHE = {}


def _consts():
    i = np.arange(128)[:, None]
    j = np.arange(128)[None, :]
    ninf = np.float32(NEG)
    prev = np.where(j >= i, 0.0, ninf)
    own = np.where(j <= i, 0.0, ninf)
    maskP = np.concatenate([prev, own], 1).astype(np.float32)
    maskP0 = np.concatenate([np.full((128, 128), ninf), own], 1).astype(np.float32)
    t = i % 8
    cachem = np.where(j >= t, 0.0, ninf)
    newm = np.where((j // 8 == i // 8) & (j % 8 <= t), 0.0, ninf)
    maskS = np.concatenate([cachem, newm], 1).astype(np.float32)
    trilT = (i <= j).astype(np.float32)
    bm = (np.arange(128)[None, :] // 8 == np.arange(16)[:, None]).astype(np.float32).reshape(1, 16 * 128)
    bmask = np.ascontiguousarray(np.broadcast_to(bm, (128, 16 * 128)))
    return dict(ident=np.eye(128, dtype=np.float32), maskP=maskP, maskP0=maskP0, maskS=maskS, trilT=trilT, bmask=bmask)


def _cossin(pos):
    half = 8
    inv = (np.float32(500000.0) ** (-np.arange(0, 16, 2, dtype=np.float32) / np.float32(16))).astype(np.float32)
    ang = pos.astype(np.float32)[:, None] * inv[None, :]
    return np.concatenate([np.cos(ang), np.sin(ang)], 1).astype(np.float32)


def _prep(x_prompt, x_sample, c_prompt, c_sample, cache_k, cache_v, state_conv,
          w_ada, b_ada, norm_mix, norm_ffn, w_qkv, b_qkv, attn_sink, w_o,
          w_sgu_in, b_sgu_in, sgu_ln_g, sgu_ln_b, w_spatial, b_spatial, w_sgu_out,
          w_up, conv_w, conv_b, w_down, norm_final):
    f = lambda a: np.ascontiguousarray(np.asarray(a, dtype=np.float32))
    x_prompt, x_sample, c_prompt, c_sample = f(x_prompt), f(x_sample), f(c_prompt), f(c_sample)
    cache_k, cache_v, state_conv = f(cache_k), f(cache_v), f(state_conv)
    consts = _consts()
    w_spatial = f(w_spatial)
    b_spatial = f(b_spatial)
    wspT = np.ascontiguousarray(w_spatial.transpose(0, 1, 3, 2))
    wspST = np.zeros((2, 4, 128, 128), np.float32)
    for b in range(16):
        wspST[:, :, 8 * b:8 * b + 8, 8 * b:8 * b + 8] = wspT[:, :, 0:8, 0:8]
    bspP = np.ascontiguousarray(b_spatial.transpose(2, 0, 1).reshape(128, 8))
    bspS = np.ascontiguousarray(bspP[np.arange(128) % 8])
    perm = [0, 1, 4, 5, 2, 3, 6, 7, 8, 9, 12, 13, 10, 11, 14, 15]
    def tiles_k(W):
        L, K, N = W.shape
        t = W.reshape(L, K // 1024, 8, 128, N // 512, 512).transpose(0, 1, 4, 3, 2, 5)
        return np.ascontiguousarray(t.reshape(L, (K // 1024) * (N // 512), 128, 4096))

    w_o_ = f(w_o).reshape(2, 2, 2, 4, 64, D).transpose(0, 2, 4, 1, 3, 5).reshape(2, 128, 8, 2, 512)
    wt_o = np.ascontiguousarray(w_o_.transpose(0, 3, 1, 2, 4).reshape(2, 2, 128, 4096))
    w_up_ = f(w_up)
    wu = np.concatenate([w_up_[:, :, :DFF].reshape(4, D, 11, 256), w_up_[:, :, DFF:].reshape(4, D, 11, 256)], 3)
    wt_up = np.ascontiguousarray(wu.reshape(4, 8, 128, 11, 512).transpose(0, 3, 2, 1, 4).reshape(4, 11, 128, 4096))
    w_dn = np.zeros((4, 3072, D), np.float32)
    w_dn[:, :DFF] = f(w_down)
    shared = dict(
        wt_ada=tiles_k(f(w_ada)), b_ada=f(b_ada), rows8=np.concatenate([f(norm_mix), f(norm_ffn)], 0), normf=f(norm_final).reshape(1, D),
        wt_qkv=tiles_k(f(w_qkv)), b_qkv=f(b_qkv), sinkp=np.ascontiguousarray(f(attn_sink)[:, perm]), wt_o=wt_o,
        wt_in=tiles_k(f(w_sgu_in)), b_in=f(b_sgu_in), lnrows=np.concatenate([f(sgu_ln_g), f(sgu_ln_b)], 0),
        wspT=wspT, wspST=wspST, bspP=bspP, bspS=bspS, wt_out=tiles_k(f(w_sgu_out)), wt_up=wt_up,
        convrows=np.ascontiguousarray(np.concatenate([f(conv_w), f(conv_b)[:, None, :]], 1).reshape(16, 2 * DFF)),
        wt_down=tiles_k(w_dn), **consts)
    in_maps = []
    for c in range(8):
        seq, r = c // 4, c % 4
        p0 = PROC_START[r]
        pos = np.concatenate([np.arange(p0, p0 + NPB * 128), 8192 + (np.arange(128) % 8)])
        m = dict(shared)
        m.update(
            xp=np.ascontiguousarray(x_prompt[seq, p0:p0 + NPB * 128]),
            xs=np.ascontiguousarray(x_sample[16 * c:16 * c + 16].reshape(128, D)),
            c17=np.ascontiguousarray(np.concatenate([c_prompt[seq:seq + 1], c_sample[16 * c:16 * c + 16]], 0)),
            ck=np.ascontiguousarray(cache_k[:, 16 * c:16 * c + 16].reshape(2, 16, 128, 256)),
            cv=np.ascontiguousarray(cache_v[:, 16 * c:16 * c + 16].reshape(2, 16, 128, 256)),
            sconv=np.ascontiguousarray(state_conv[:, 16 * c:16 * c + 16].reshape(4, 32, 2 * DFF)),
            cossin=_cossin(pos),
        )
        in_maps.append(m)
    return in_maps


def kernel(**inputs):
    in_maps = _prep(**inputs)
    if "nc" not in _CACHE:
        _CACHE["nc"] = build_program()
    nc = _CACHE["nc"]
    res = run_bass_kernel_spmd(nc, in_maps, core_ids=list(range(8)))
    R = res.results
    y_prompt = np.zeros((2, 8192, D), np.float32)
    k_p = np.zeros((2, 2, 128, 4, 64), np.float32)
    v_p = np.zeros((2, 2, 128, 4, 64), np.float32)
    conv_p = np.zeros((4, 2, 2, 2 * DFF), np.float32)
    for c in range(8):
        seq, r = c // 4, c % 4
        p0 = PROC_START[r]
        y_prompt[seq, OWN_START[r]:OWN_END[r]] = R[c]["yp"][OWN_START[r] - p0:OWN_END[r] - p0]
        if r == 3:
            k_p[:, seq] = R[c]["kp"].reshape(2, 128, 4, 64)
            v_p[:, seq] = R[c]["vp"].reshape(2, 128, 4, 64)
            conv_p[:, seq] = R[c]["convp"]
    y_sample = np.concatenate([R[c]["ys"].reshape(16, 8, D) for c in range(8)], 0)
    k_s = np.concatenate([R[c]["ks"].reshape(2, 16, 128, 4, 64) for c in range(8)], 1)
    v_s = np.concatenate([R[c]["vs"].reshape(2, 16, 128, 4, 64) for c in range(8)], 1)
    conv_s = np.concatenate([R[c]["convs"].reshape(4, 16, 2, 2 * DFF) for c in range(8)], 1)
    sgu_v = np.concatenate([R[c]["sguv"].reshape(2, 16, 8, 2048) for c in range(8)], 1)
    return (y_prompt, y_sample, k_p, v_p, conv_p, k_s, v_s, conv_s, sgu_v)
```

```python
import numpy as np
from contextlib import ExitStack
import concourse.bass as bass
import concourse.mybir as mybir
from concourse.bass_utils import run_bass_kernel_spmd

F32 = mybir.dt.float32
BF16 = mybir.dt.bfloat16
ALU = mybir.AluOpType
AF = mybir.ActivationFunctionType
AX = mybir.AxisListType

D = 1024
NPB = 20
SB = 4
NST = NPB // SB
NBMAX = SB + 1
DFF = 2816
NFC = 44
EPS = 1e-6
NEG = -30000.0
PROC_START = [0, 1920, 3840, 5632]
OWN_START = [0, 2560, 4480, 6400]
OWN_END = [2560, 4480, 6400, 8192]
NSLOT = 5
PREFETCH = 4


class Buf:
    __slots__ = ("name", "w", "r")

    def __init__(self, name=""):
        self.name = name
        self.w = None
        self.r = []


class Tile:
    __slots__ = ("t", "b")

    def __init__(self, t, b=None):
        self.t = t
        self.b = b if b is not None else Buf()


class Entry:
    __slots__ = ("waits", "fn", "signal", "dma_sem")

    def __init__(self, waits, fn):
        self.waits = waits
        self.fn = fn
        self.signal = False
        self.dma_sem = None


COMPUTE = ("pe", "act", "dve", "pool")
ALLENG = ("pe", "act", "dve", "pool", "sp")


class Sched:
    def __init__(self, nc, stack, dma_ring=8):
        self.nc = nc
        self.streams = {k: [] for k in ALLENG}
        self.esem = {k: stack.enter_context(nc.semaphore("prog_" + k)) for k in COMPUTE}
        self.base = {k: 0 for k in COMPUTE}
        self.dsem = {}
        self.dcount = {}
        self.dlast = {}
        for q in ("sp", "pool", "act"):
            self.dsem[q] = [stack.enter_context(nc.semaphore(f"dma_{q}_{i}")) for i in range(dma_ring)]
            self.dcount[q] = 0
            self.dlast[q] = [None] * dma_ring
        self.seen = {k: {} for k in ALLENG}
        self.ring = dma_ring
        self.phase = 0
        self.hold = []
        self.all_out = []

    def _collect(self, eng, reads, writes):
        evs = []
        for b in reads:
            if b.w is not None:
                evs.append(b.w)
        for b in writes:
            if b.w is not None:
                evs.append(b.w)
            evs.extend(b.r)
        return self._reduce(eng, evs)

    def _reduce(self, eng, evs):
        best = {}
        for ev in evs:
            if ev is None:
                continue
            if ev[0] == "e":
                _, src, idx, ph = ev
                if ph != self.phase:
                    continue
                if src == eng and eng == "pe":
                    continue
                key = ("e", src)
                if key not in best or best[key][2] < idx:
                    best[key] = ev
            else:
                _, q, slot, val = ev
                key = ("d", q, slot)
                if key not in best or best[key][3] < val:
                    best[key] = ev
        waits = []
        for key, ev in best.items():
            v = ev[2] if ev[0] == "e" else ev[3]
            pk = (self.phase,) + key if ev[0] == "e" else key
            if self.seen[eng].get(pk, -1) >= v:
                continue
            self.seen[eng][pk] = v
            if ev[0] == "e":
                self.streams[ev[1]][ev[2]].signal = True
            waits.append(ev)
        return waits

    def op(self, eng, fn, reads=(), writes=()):
        waits = self._collect(eng, reads, writes)
        st = self.streams[eng]
        st.append(Entry(waits, fn))
        ev = ("e", eng, len(st) - 1, self.phase)
        for b in reads:
            b.r.append(ev)
        for b in writes:
            b.w = ev
            b.r = []
        return ev

    def dma(self, q, fn, reads=(), writes=(), hold=True, out=False):
        evs = []
        for b in reads:
            if b.w is not None:
                evs.append(b.w)
        for b in writes:
            if b.w is not None:
                evs.append(b.w)
            evs.extend(b.r)
        i = self.dcount[q]
        self.dcount[q] += 1
        slot = i % self.ring
        val = 16 * (i // self.ring + 1)
        if self.dlast[q][slot] is not None:
            evs.append(self.dlast[q][slot])
        waits = self._reduce(q, evs)
        ent = Entry(waits, fn)
        ent.dma_sem = self.dsem[q][slot]
        self.streams[q].append(ent)
        ev = ("d", q, slot, val)
        self.dlast[q][slot] = ev
        for b in reads:
            b.r.append(ev)
        for b in writes:
            b.w = ev
            b.r = []
        if hold:
            self.hold.append(ev)
        if out:
            self.all_out.append(ev)
        return ev

    def flush(self, final=False):
        nc = self.nc
        lasts = []
        for k in COMPUTE:
            st = self.streams[k]
            idx = None
            for i in range(len(st) - 1, -1, -1):
                if st[i].fn is not None and st[i].dma_sem is None:
                    idx = i
                    break
            if idx is not None:
                lasts.append(("e", k, idx, self.phase))
        extra = list(self.hold)
        if final:
            extra += self.all_out
            for q in self.dlast:
                extra += [ev for ev in self.dlast[q] if ev is not None]
        for k in ALLENG:
            waits = self._reduce(k, lasts + extra)
            self.streams[k].append(Entry(waits, None))
        self.hold = []
        counts = {}
        for k in COMPUTE:
            c = self.base[k]
            arr = []
            for ent in self.streams[k]:
                if ent.signal and ent.dma_sem is None and ent.fn is not None:
                    c += 1
                arr.append(c)
            counts[k] = arr

        def resolve(ev):
            if ev[0] == "e":
                return self.esem[ev[1]], counts[ev[1]][ev[2]]
            return self.dsem[ev[1]][ev[2]], ev[3]

        def replay(k, e):
            for ent in self.streams[k]:
                for ev in ent.waits:
                    s, v = resolve(ev)
                    e.wait_ge(s, v)
                if ent.fn is None:
                    continue
                ins = ent.fn(e)
                if ent.dma_sem is not None:
                    ins.then_inc(ent.dma_sem, 16)
                elif ent.signal:
                    ins.then_inc(self.esem[k], 1)

        with nc.Block() as block:
            @block.tensor
            def _(e):
                replay("pe", e)

            @block.scalar
            def _(e):
                replay("act", e)

            @block.vector
            def _(e):
                replay("dve", e)

            @block.gpsimd
            def _(e):
                replay("pool", e)

            @block.sync
            def _(e):
                replay("sp", e)

        for k in COMPUTE:
            if counts[k]:
                self.base[k] = counts[k][-1]
        self.streams = {k: [] for k in ALLENG}
        self.phase += 1


_IN_SPECS = [
    ("xp", [NPB * 128, D]), ("xs", [128, D]), ("c17", [17, D]),
    ("ck", [2, 16, 128, 256]), ("cv", [2, 16, 128, 256]), ("sconv", [4, 32, 2 * DFF]),
    ("wt_ada", [4, 12, 128, 4096]), ("b_ada", [4, 6 * D]), ("rows8", [8, D]), ("normf", [1, D]),
    ("wt_qkv", [2, 3, 128, 4096]), ("b_qkv", [2, 1536]), ("sinkp", [2, 16]), ("wt_o", [2, 2, 128, 4096]),
    ("wt_in", [2, 8, 128, 4096]), ("b_in", [2, 4096]), ("lnrows", [4, 2048]),
    ("wspT", [2, 4, 128, 128]), ("wspST", [2, 4, 128, 128]), ("bspP", [128, 8]), ("bspS", [128, 8]),
    ("wt_out", [2, 4, 128, 4096]), ("wt_up", [4, 11, 128, 4096]), ("convrows", [16, 2 * DFF]), ("wt_down", [4, 6, 128, 4096]),
    ("cossin", [(NPB + 1) * 128, 16]), ("ident", [128, 128]), ("maskP", [128, 256]), ("maskP0", [128, 256]),
    ("maskS", [128, 256]), ("trilT", [128, 128]), ("bmask", [128, 16 * 128]),
]
_OUT_SPECS = [
    ("yp", [NPB * 128, D]), ("ys", [128, D]), ("kp", [2, 128, 256]), ("vp", [2, 128, 256]),
    ("convp", [4, 2, 2 * DFF]), ("ks", [2, 16, 128, 256]), ("vs", [2, 16, 128, 256]),
    ("convs", [4, 32, 2 * DFF]), ("sguv", [2, 128, 2048]),
]


def weight_sequence():
    seq = []
    for l in range(4):
        for nt in range(12):
            seq.append(("ada", l, nt))
    for st in range(NST):
        for l in range(4):
            if l % 2 == 0:
                for nt in range(3):
                    seq.append(("qkv", l // 2, nt))
                for nt in range(2):
                    seq.append(("wo", l // 2, nt))
            else:
                for nt in range(4):
                    seq.append(("win", l // 2, 4 + nt))
                for g in range(4):
                    seq.append(("win", l // 2, g))
                    seq.append(("win", l // 2, 4 + g))
                for kh in range(2):
                    for nt in range(2):
                        seq.append(("wout", l // 2, kh * 2 + nt))
            for jj in range(11):
                seq.append(("wup", l, jj))
            for kg in range(3):
                for nt in range(2):
                    seq.append(("wdown", l, kg * 2 + nt))
    return seq


class _Stop(Exception):
    pass


def build_program(dbg=None):
    nc = bass.Bass("TRN2", target_bir_lowering=False)
    I = {n: nc.dram_tensor(n, s, F32, kind="ExternalInput").ap() for n, s in _IN_SPECS}
    O = {n: nc.dram_tensor(n, s, F32, kind="ExternalOutput").ap() for n, s in _OUT_SPECS}
    if dbg is not None:
        O["dbg"] = nc.dram_tensor("dbg", [128, 16384], F32, kind="ExternalOutput").ap()
    try:
        _build_body(nc, I, O, dbg)
    except _Stop:
        pass
    return nc


def _build_body(nc, I, O, dbg):

    with ExitStack() as gs:
        S = Sched(nc, gs)

        uid = [0]

        def uname(name):
            uid[0] += 1
            return f"s{uid[0]}_{name}"

        def SBT(stack, name, shape, dt):
            return Tile(stack.enter_context(nc.sbuf_tensor(uname(name), shape, dt)), Buf(name))

        dbg_off = [0]

        def dump(ap, ncols, buf, bf=False):
            if bf:
                with ExitStack() as dst_:
                    t = SBT(dst_, "dbgt", [128, ncols], F32)
                    S.op("dve", lambda e: e.tensor_copy(out=t.t[:], in_=ap), reads=[buf], writes=[t.b])
                    o = dbg_off[0]
                    S.dma("sp", lambda e: e.dma_start(out=O["dbg"][:, o:o + ncols], in_=t.t[:]), reads=[t.b], out=True)
                    S.flush()
            else:
                o = dbg_off[0]
                S.dma("sp", lambda e: e.dma_start(out=O["dbg"][:, o:o + ncols], in_=ap), reads=[buf], out=True)
            dbg_off[0] += ncols

        def stop():
            S.flush(final=True)
            print("SEM COUNTS", S.base, S.dcount)
            raise _Stop()

        x = [SBT(gs, f"x{i}", [128, D], F32) for i in range(NBMAX)]
        hT = gs.enter_context(nc.sbuf_tensor("s_hT", [128, 8, NBMAX * 128], BF16))
        hTb = [Buf(f"hT{i}") for i in range(NBMAX)]
        slots = [SBT(gs, f"ws{i}", [128, 8, 512], BF16) for i in range(NSLOT)]
        brow = [SBT(gs, f"brow{i}", [1, 512], BF16) for i in range(NSLOT)]
        pb = [Tile(gs.enter_context(nc.psum_tensor(f"pb{i}", [128, 512], F32)), Buf(f"pb{i}")) for i in range(8)]
        identf = SBT(gs, "identf", [128, 128], F32)
        identb = SBT(gs, "identb", [128, 128], BF16)
        ones = SBT(gs, "ones", [1, 128], BF16)
        maskP = SBT(gs, "maskP", [128, 256], F32)
        maskP0 = SBT(gs, "maskP0", [128, 256], F32)
        maskS = SBT(gs, "maskS", [128, 256], F32)
        wm = SBT(gs, "wm", [128, 2, 2, 4, 128], BF16)
        bsp = SBT(gs, "bsp", [128, 2, 8], F32)
        sinkB = SBT(gs, "sinkB", [128, 2, 16], F32)
        cs = SBT(gs, "cs", [128, NPB + 1, 16], F32)
        modT = SBT(gs, "modT", [128, 4, 48, 17], F32)
        Amod = SBT(gs, "Amod", [128, 4, 2, 8, 17], F32)
        cwT = SBT(gs, "cwT", [128, NFC, 16], F32)
        stP = SBT(gs, "stP", [128, 4, NFC, 2], F32)
        stPb = [[Buf(f"stP{l_}_{c_}") for c_ in range(NFC)] for l_ in range(4)]
        stP_all = [b_ for row in stPb for b_ in row]
        normfB = SBT(gs, "normfB", [128, D], F32)
        G = [[SBT(gs, f"G{k}{w}", [128, D], F32) for w in range(2)] for k in range(2)]
        kTp = [SBT(gs, f"kTp{a}", [128, 2, 128], BF16) for a in range(2)]
        Vpp = [SBT(gs, f"Vpp{a}", [128, 4, 128], BF16) for a in range(2)]

        wseq = weight_sequence()
        wstate = {"issued": 0, "next": 0}

        def w_issue(i):
            kind, l, idx = wseq[i]
            sl = slots[i % NSLOT]
            br = brow[i % NSLOT]
            dst = sl.t

            def q(dst_ap, src_ap, tile=sl):
                S.dma("pool", lambda e, d=dst_ap, s=src_ap: e.dma_start(out=d, in_=s), writes=[tile.b], hold=False)

            def tile(name, nk=8):
                q(dst[:, 0:nk, :], I[name][l][idx][:, 0:nk * 512].rearrange("p (k n) -> p k n", n=512))

            if kind == "ada":
                tile("wt_ada")
                q(br.t[:, :], I["b_ada"][l:l + 1, idx * 512:(idx + 1) * 512], br)
            elif kind == "qkv":
                tile("wt_qkv")
                q(br.t[:, :], I["b_qkv"][l:l + 1, idx * 512:(idx + 1) * 512], br)
            elif kind == "wo":
                tile("wt_o")
            elif kind == "win":
                tile("wt_in")
                q(br.t[:, :], I["b_in"][l:l + 1, idx * 512:(idx + 1) * 512], br)
            elif kind == "wout":
                tile("wt_out")
            elif kind == "wup":
                tile("wt_up")
            elif kind == "wdown":
                tile("wt_down", 8 if idx // 2 < 2 else 6)

        def w_next(kind, l, idx):
            i = wstate["next"]
            assert wseq[i] == (kind, l, idx), (wseq[i], kind, l, idx)
            while wstate["issued"] < min(len(wseq), i + PREFETCH + 1):
                w_issue(wstate["issued"])
                wstate["issued"] += 1
            wstate["next"] += 1
            return slots[i % NSLOT], brow[i % NSLOT]

        def mm(out_ap, lhsT, rhs, start, stop, reads, out_buf):
            return S.op("pe", lambda e: e.matmul(out=out_ap, lhsT=lhsT, rhs=rhs, start=start, stop=stop),
                        reads=reads, writes=[out_buf])

        def tr(out_ap, in_ap, ident_ap, reads, out_buf):
            return S.op("pe", lambda e: e.transpose(out=out_ap, in_=in_ap, identity=ident_ap), reads=reads, writes=[out_buf])

        def bfview(p):
            return p.t[:].bitcast(BF16).rearrange("p (j t) -> p j t", t=128)

        with ExitStack() as ph:
            def ld(tile, src, q="sp"):
                S.dma(q, lambda e: e.dma_start(out=tile.t[:], in_=src), writes=[tile.b])

            ld(identf, I["ident"])
            ld(maskP, I["maskP"])
            ld(maskP0, I["maskP0"])
            ld(maskS, I["maskS"])
            S.dma("sp", lambda e: e.dma_start(out=cs.t[:], in_=I["cossin"].rearrange("(b p) c -> p b c", p=128)), writes=[cs.b])
            S.dma("sp", lambda e: e.dma_start(out=bsp.t[:, 0, :], in_=I["bspP"]), writes=[bsp.b])
            S.dma("sp", lambda e: e.dma_start(out=bsp.t[:, 1, :], in_=I["bspS"]), writes=[bsp.b])
            S.dma("sp", lambda e: e.dma_start(out=sinkB.t[:].rearrange("p a h -> p (a h)"),
                                              in_=I["sinkp"].rearrange("a h -> (a h)").partition_broadcast(128)), writes=[sinkB.b])
            S.dma("sp", lambda e: e.dma_start(out=normfB.t[:], in_=I["normf"][0].partition_broadcast(128)), writes=[normfB.b])
            S.op("dve", lambda e: e.memset(ones.t[:], 1.0), writes=[ones.b])
            S.op("dve", lambda e: e.tensor_copy(out=identb.t[:], in_=identf.t[:]), reads=[identf.b], writes=[identb.b])
            S.op("dve", lambda e: e.memset(stP.t[:], 0.0), writes=stP_all)
            for a in range(2):
                S.op("dve", lambda e, a=a: e.memset(Vpp[a].t[:], 0.0), writes=[Vpp[a].b])
                S.op("dve", lambda e, a=a: e.memset(kTp[a].t[:], 0.0), writes=[kTp[a].b])
            tril = SBT(ph, "tril", [128, 128], F32)
            ld(tril, I["trilT"])
            wsp_f = SBT(ph, "wsp_f", [128, 2, 2, 4, 128], F32)
            for kind, nm in enumerate(("wspT", "wspST")):
                S.dma("sp", lambda e, kind=kind, nm=nm: e.dma_start(out=wsp_f.t[:, kind], in_=I[nm].rearrange("l g s t -> s l g t")),
                      writes=[wsp_f.b])
            S.op("dve", lambda e: e.tensor_tensor(out=wm.t[:].rearrange("p a l g t -> p (a l g) t"),
                                                  in0=wsp_f.t[:].rearrange("p a l g t -> p (a l g) t"),
                                                  in1=tril.t[:].unsqueeze(1).to_broadcast([128, 16, 128]), op=ALU.mult),
                 reads=[wsp_f.b, tril.b], writes=[wm.b])
            if dbg == "s1":
                dump(wm.t[:].rearrange("p a l g t -> p (a l g t)"), 2048, wm.b, bf=True)
                dump(cs.t[:].rearrange("p b c -> p (b c)"), 336, cs.b)
                dump(sinkB.t[:].rearrange("p a h -> p (a h)"), 32, sinkB.b)
                stop()
            rows8 = SBT(ph, "rows8", [8, D], F32)
            ld(rows8, I["rows8"])
            nrmT = SBT(ph, "nrmT", [128, 8, 8], F32)
            for j in range(8):
                mm(pb[0].t[:, j * 8:(j + 1) * 8], rows8.t[:, j * 128:(j + 1) * 128], identf.t[0:8, 0:8], True, True, [rows8.b, identf.b], pb[0].b)
            S.op("act", lambda e: e.activation(out=nrmT.t[:].rearrange("p j r -> p (j r)"), in_=pb[0].t[:, 0:64], func=AF.Copy),
                 reads=[pb[0].b], writes=[nrmT.b])
            crow = SBT(ph, "crow", [16, 2 * DFF], F32)
            ld(crow, I["convrows"])
            for c0 in range(0, NFC, 22):
                for j in range(22):
                    mm(pb[1].t[:, j * 16:(j + 1) * 16], crow.t[:, (c0 + j) * 128:(c0 + j + 1) * 128], identf.t[0:16, 0:16], True, True,
                       [crow.b, identf.b], pb[1].b)
                S.op("act", lambda e, c0=c0: e.activation(out=cwT.t[:, c0:c0 + 22, :].rearrange("p j r -> p (j r)"),
                                                          in_=pb[1].t[:, 0:352], func=AF.Copy), reads=[pb[1].b], writes=[cwT.b])
            if dbg == "s2":
                dump(cwT.t[:].rearrange("p c r -> p (c r)"), 704, cwT.b)
                dump(nrmT.t[:].rearrange("p j r -> p (j r)"), 64, nrmT.b)
                stop()
            c17 = SBT(ph, "c17", [17, D], F32)
            ld(c17, I["c17"])
            c17b = SBT(ph, "c17b", [17, D], BF16)
            S.op("act", lambda e: e.activation(out=c17b.t[:], in_=c17.t[:], func=AF.Silu), reads=[c17.b], writes=[c17b.b])
            sT = SBT(ph, "sT", [128, 8, 17], BF16)
            pv = pb[2].t[:]
            for k in range(8):
                mm(pv[:, k * 32:k * 32 + 17], c17b.t[:, k * 128:(k + 1) * 128], identb.t[0:17, 0:17], True, True, [c17b.b, identb.b], pb[2].b)
            S.op("act", lambda e: e.activation(out=sT.t[:], in_=pv[:, 0:256].rearrange("p (k c) -> p k c", c=32)[:, :, 0:17], func=AF.Copy),
                 reads=[pb[2].b], writes=[sT.b])
            if dbg == "s3":
                dump(sT.t[:].rearrange("p k c -> p (k c)"), 136, sT.b, bf=True)
                stop()
            for l in range(4):
                for nt in range(12):
                    sl, br = w_next("ada", l, nt)
                    bank = pb[3 + (nt % 2)]
                    for fc in range(4):
                        o = bank.t[:, fc * 17:(fc + 1) * 17]
                        for k in range(8):
                            mm(o, sl.t[:, k, fc * 128:(fc + 1) * 128], sT.t[:, k, :], k == 0, False, [sl.b, sT.b], bank.b)
                        mm(o, br.t[:, fc * 128:(fc + 1) * 128], ones.t[:, 0:17], False, True, [br.b, ones.b], bank.b)
                    S.op("dve", lambda e, l=l, nt=nt, bank=bank: e.tensor_copy(
                        out=modT.t[:, l, nt * 4:(nt + 1) * 4, :].rearrange("p c s -> p (c s)"), in_=bank.t[:, 0:68]),
                        reads=[bank.b], writes=[modT.b])
            for l in range(4):
                for w in range(2):
                    S.op("dve", lambda e, l=l, w=w: e.tensor_scalar(out=Amod.t[:, l, w], in0=modT.t[:, l, 8 + 24 * w:16 + 24 * w, :],
                                                                    scalar1=1.0, scalar2=None, op0=ALU.add),
                         reads=[modT.b], writes=[Amod.b])
                    S.op("dve", lambda e, l=l, w=w: e.tensor_tensor(out=Amod.t[:, l, w], in0=Amod.t[:, l, w],
                                                                    in1=nrmT.t[:, :, 4 * w + l:4 * w + l + 1].to_broadcast([128, 8, 17]),
                                                                    op=ALU.mult),
                         reads=[nrmT.b, Amod.b], writes=[Amod.b])
            if dbg == "setup":
                dump(modT.t[:].rearrange("p l j s -> p (l j s)"), 3264, modT.b)
                dump(Amod.t[:].rearrange("p l w j s -> p (l w j s)"), 1088, Amod.b)
                dump(cwT.t[:].rearrange("p c r -> p (c r)"), 704, cwT.b)
                dump(wm.t[:].rearrange("p a l g t -> p (a l g t)"), 2048, wm.b, bf=True)
                stop()
            S.flush()

        def build_gates(ph, l, kinds):
            gbs = [SBT(ph, f"gb{i}", [128, 128], F32) for i in range(4)]
            gbc = [0]
            for kind in kinds:
                for w in range(2):
                    for half in range(2):
                        bank = pb[half]
                        for jj in range(4):
                            j = half * 4 + jj
                            gb = gbs[gbc[0] % 4]
                            gbc[0] += 1
                            col = modT.t[:, l, 16 + 24 * w + j, :]
                            if kind == 0:
                                src = col[:, 0:1].to_broadcast([128, 128])
                                dstv = gb.t[:]
                            else:
                                src = col[:, 1:17].unsqueeze(2).to_broadcast([128, 16, 8])
                                dstv = gb.t[:].rearrange("p (b t) -> p b t", t=8)
                            S.op("dve", lambda e, d=dstv, s=src: e.tensor_copy(out=d, in_=s), reads=[modT.b], writes=[gb.b])
                            mm(bank.t[:, jj * 128:(jj + 1) * 128], gb.t[:], identf.t[:], True, True, [gb.b, identf.b], bank.b)
                        S.op("act", lambda e, kind=kind, w=w, half=half, bank=bank: e.activation(
                            out=G[kind][w].t[:, half * 512:(half + 1) * 512], in_=bank.t[:], func=AF.Copy),
                            reads=[bank.b], writes=[G[kind][w].b])

        def norm_phase(ph, blocks, l, w):
            ssl = [SBT(ph, f"ss{i}", [128, 4], F32) for i in range(NBMAX)]
            junk = SBT(ph, "junk", [128, D], BF16)
            xn = [SBT(ph, f"xn{i}", [128, D], BF16) for i in range(2)]
            tmpf = [SBT(ph, f"tmpf{i}", [128, 8, 128], F32) for i in range(2)]
            for bi, kind in enumerate(blocks):
                S.op("act", lambda e, bi=bi: e.activation(out=junk.t[:], in_=x[bi].t[:], func=AF.Square, accum_out=ssl[bi].t[:, 0:1]),
                     reads=[x[bi].b], writes=[junk.b, ssl[bi].b])
                S.op("act", lambda e, bi=bi: e.activation(out=ssl[bi].t[:, 1:2], in_=ssl[bi].t[:, 0:1], func=AF.Sqrt,
                                                          scale=1.0 / D, bias=EPS), reads=[ssl[bi].b], writes=[ssl[bi].b])
                S.op("dve", lambda e, bi=bi: e.reciprocal(out=ssl[bi].t[:, 2:3], in_=ssl[bi].t[:, 1:2]),
                     reads=[ssl[bi].b], writes=[ssl[bi].b])
                xt = xn[bi % 2]
                S.op("act", lambda e, bi=bi, xt=xt: e.activation(out=xt.t[:], in_=x[bi].t[:], func=AF.Copy, scale=ssl[bi].t[:, 2:3]),
                     reads=[x[bi].b, ssl[bi].b], writes=[xt.b])
                bank = pb[6 + bi % 2]
                bv = bfview(bank)
                for j in range(8):
                    tr(bv[:, j, :], xt.t[:, j * 128:(j + 1) * 128], identb.t[:], [xt.b, identb.b], bank.b)
                tf = tmpf[bi % 2]
                hv = hT[:, :, bi * 128:(bi + 1) * 128]
                if kind == 0:
                    a_ap = Amod.t[:, l, w, :, 0:1].to_broadcast([128, 8, 128])
                    s_ap = modT.t[:, l, 24 * w:24 * w + 8, 0:1].to_broadcast([128, 8, 128])
                    S.op("dve", lambda e, bv=bv, tf=tf, a_ap=a_ap: e.tensor_tensor(out=tf.t[:], in0=bv, in1=a_ap, op=ALU.mult),
                         reads=[bank.b, Amod.b], writes=[tf.b])
                    S.op("pool", lambda e, hv=hv, tf=tf, s_ap=s_ap: e.tensor_tensor(out=hv, in0=tf.t[:], in1=s_ap, op=ALU.add),
                         reads=[tf.b, modT.b], writes=[hTb[bi]])
                else:
                    for j in range(8):
                        a_ap = Amod.t[:, l, w, j, 1:17].unsqueeze(2).to_broadcast([128, 16, 8])
                        s_ap = modT.t[:, l, 24 * w + j, 1:17].unsqueeze(2).to_broadcast([128, 16, 8])
                        S.op("dve", lambda e, j=j, bv=bv, tf=tf, a_ap=a_ap: e.tensor_tensor(
                            out=tf.t[:, j, :].rearrange("p (b t) -> p b t", t=8), in0=bv[:, j, :].rearrange("p (b t) -> p b t", t=8),
                            in1=a_ap, op=ALU.mult), reads=[bank.b, Amod.b], writes=[tf.b])
                        S.op("dve", lambda e, j=j, hv=hv, tf=tf, s_ap=s_ap: e.tensor_tensor(
                            out=hv[:, j, :].rearrange("p (b t) -> p b t", t=8), in0=tf.t[:, j, :].rearrange("p (b t) -> p b t", t=8),
                            in1=s_ap, op=ALU.add), reads=[tf.b, modT.b], writes=[hTb[bi]])

        def resid_add(bi, kind, w, nt, bank, eng2="dve"):
            xs_ = x[bi].t[:, nt * 512:(nt + 1) * 512]
            g_ = G[kind][w].t[:, nt * 512:(nt + 1) * 512]
            tmp = resid_tmp[resid_ctr[0] % 3]
            resid_ctr[0] += 1
            S.op("dve", lambda e: e.tensor_tensor(out=tmp.t[:], in0=bank.t[:], in1=g_, op=ALU.mult),
                 reads=[bank.b, G[kind][w].b], writes=[tmp.b])
            S.op(eng2, lambda e: e.tensor_tensor(out=xs_, in0=xs_, in1=tmp.t[:], op=ALU.add), reads=[tmp.b, x[bi].b], writes=[x[bi].b])

        resid_tmp = [SBT(gs, f"rtmp{i}", [128, 512], F32) for i in range(3)]
        resid_ctr = [0]
        mmctr = [0]

        def dense_tm(blocks, sl, br, kchunks, act_chunk, evac, banks=(0, 1, 2)):
            pend = []
            for bi, kind in enumerate(blocks):
                bank = pb[banks[mmctr[0] % len(banks)]]
                mmctr[0] += 1
                n = len(kchunks)
                for i, kc in enumerate(kchunks):
                    lhsT, rb = act_chunk(bi, kc)
                    mm(bank.t[:], lhsT, sl.t[:, i, :], i == 0, (i == n - 1) and br is None, [sl.b, rb], bank.b)
                if br is not None:
                    mm(bank.t[:], ones.t[:], br.t[:], False, True, [ones.b, br.b], bank.b)
                r = evac(bi, kind, bank)
                if callable(r):
                    pend.append(r)
                    if len(pend) > 2:
                        pend.pop(0)()
            while pend:
                pend.pop(0)()

        def h_chunk(bi, kc):
            return hT[:, kc, bi * 128:(bi + 1) * 128], hTb[bi]

        def attn_layer(st, blocks, a, l):
            gblk0 = st * SB
            nb = len(blocks)
            has_s = 1 in blocks
            npb = sum(1 for k in blocks if k == 0)
            if has_s:
                with ExitStack() as ph:
                    build_gates(ph, l, sorted(set(blocks)))
                    norm_phase(ph, blocks, l, 0)
                    S.flush()
            with ExitStack() as pst:
                OT = [SBT(pst, f"OT{i}", [128, 8, 128], BF16) for i in range(nb)]
                qkb = [None] * nb
                Vp = [None] * nb
                kT = [None] * nb
                if has_s:
                    qkb[npb] = SBT(pst, "qkbS", [128, 1280], BF16)
                    Vp[npb] = SBT(pst, "VpS", [128, 4, 128], BF16)
                    kT[npb] = SBT(pst, "kTS", [128, 2, 128], BF16)
                smp = {}

                def temps(ph, n, nq=None):
                    return dict(
                        qT=[SBT(ph, f"qT{i}", [128, 8, 128], BF16) for i in range(nq or n)],
                        sm=[SBT(ph, f"sm{i}", [128, 4, 256], F32) for i in range(min(n, 2))],
                        pp=[SBT(ph, f"pp{i}", [128, 4, 256], BF16) for i in range(n)],
                        pT=[SBT(ph, f"pT{i}", [128, 8, 128], BF16) for i in range(min(n, 2))],
                        stt=[SBT(ph, f"stt{i}", [128, 32], F32) for i in range(n)])

                SCORE_SETS = ((pb[5], pb[6]), (pb[1], pb[2]), (pb[3], pb[4]))
                pv_half = [Buf("pv0"), Buf("pv1")]
                nsets = [2]

                def attn_pre(bi, kind, T):
                    n = len(T["qT"])
                    bq = pb[7]
                    bqv = bfview(bq)
                    for c in range(8):
                        tr(bqv[:, c, :], qkb[bi].t[:, c * 128:(c + 1) * 128], identb.t[:], [qkb[bi].b, identb.b], bq.b)
                    qt = T["qT"][bi % n]
                    S.op("act", lambda e: e.activation(out=qt.t[:], in_=bqv, func=AF.Copy), reads=[bq.b], writes=[qt.b])
                    bk = pb[7]
                    bkv = bfview(bk)
                    for kc in range(2):
                        tr(bkv[:, kc, :], qkb[bi].t[:, 1024 + kc * 128:1024 + (kc + 1) * 128], identb.t[:], [qkb[bi].b, identb.b], bk.b)
                    S.op("act", lambda e: e.activation(out=kT[bi].t[:], in_=bkv[:, 0:2, :], func=AF.Copy), reads=[bk.b], writes=[kT[bi].b])

                def attn_sa(bi, kind, gi, k, T, part):
                    qt = T["qT"][bi % len(T["qT"])]
                    n = len(T["sm"])
                    first = (kind == 0 and gblk0 + bi == 0)
                    if kind == 0:
                        kprev = kT[bi - 1] if bi > 0 else kTp[a]
                        msk = maskP0 if first else maskP
                    else:
                        msk = maskS
                        QmT, KTs, bmk = smp["QmT"], smp["KTs"], smp["bmk"]
                    kc = gi // 2
                    bA, bB = SCORE_SETS[k % nsets[0]]
                    if kind == 1 and part == 0:
                        for ci in range(2):
                            c = 2 * gi + ci
                            S.op("dve", lambda e, ci=ci, c=c: e.tensor_tensor(
                                out=QmT[ci].t[:], in0=qt.t[:, c, :].unsqueeze(1).to_broadcast([128, 16, 128]), in1=bmk.t[:], op=ALU.mult),
                                reads=[qt.b, bmk.b], writes=[QmT[ci].b])
                    for hf, bank in ((0, bA), (1, bB)):
                        if part != 0:
                            break
                        ps = slice(64 * hf, 64 * hf + 64)
                        for ci in range(2):
                            c = 2 * gi + ci
                            o_prev = bank.t[:, ci * 256:ci * 256 + 128]
                            o_own = bank.t[:, ci * 256 + 128:ci * 256 + 256]
                            if kind == 0:
                                mm(o_prev, qt.t[ps, c, :], kprev.t[ps, kc, :], True, True, [qt.b, kprev.b], bank.b)
                            else:
                                for b in range(16):
                                    mm(o_prev, QmT[ci].t[ps, b, :], KTs.t[ps, kc, b, :], b == 0, b == 15, [QmT[ci].b, KTs.b], bank.b)
                            mm(o_own, qt.t[ps, c, :], kT[bi].t[ps, kc, :], True, True, [qt.b, kT[bi].b], bank.b)
                    if part == 0:
                        return
                    s_ = T["sm"][k % len(T["sm"])]
                    p_ = T["pp"][k % len(T["pp"])]
                    t8 = T["stt"][k % len(T["stt"])]
                    for hf, bank in ((0, bA), (1, bB)):
                        S.op("dve", lambda e, hf=hf, bank=bank: e.scalar_tensor_tensor(
                            out=s_.t[:, 2 * hf:2 * hf + 2, :], in0=bank.t[:].rearrange("p (s k) -> p s k", k=256), scalar=0.125,
                            in1=msk.t[:].unsqueeze(1).to_broadcast([128, 2, 256]), op0=ALU.mult, op1=ALU.add),
                            reads=[bank.b, msk.b], writes=[s_.b])
                    sk = sinkB.t[:, a, 4 * gi:4 * gi + 4]
                    S.op("dve", lambda e: e.tensor_reduce(out=t8.t[:, 0:4], in_=s_.t[:], axis=AX.X, op=ALU.max), reads=[s_.b], writes=[t8.b])
                    S.op("dve", lambda e: e.tensor_tensor(out=t8.t[:, 0:4], in0=t8.t[:, 0:4], in1=sk, op=ALU.max),
                         reads=[t8.b, sinkB.b], writes=[t8.b])
                    S.op("dve", lambda e: e.tensor_scalar(out=t8.t[:, 4:8], in0=t8.t[:, 0:4], scalar1=-1.0, scalar2=None, op0=ALU.mult),
                         reads=[t8.b], writes=[t8.b])
                    S.op("dve", lambda e: e.tensor_tensor(out=t8.t[:, 12:16], in0=t8.t[:, 4:8], in1=sk, op=ALU.add),
                         reads=[t8.b, sinkB.b], writes=[t8.b])
                    for h4 in range(4):
                        S.op("act", lambda e, h4=h4: e.activation(
                            out=p_.t[:, h4, :], in_=s_.t[:, h4, :], func=AF.Exp, bias=t8.t[:, 4 + h4:5 + h4], scale=1.0,
                            accum_out=t8.t[:, 8 + h4:9 + h4]), reads=[s_.b, t8.b], writes=[p_.b, t8.b])
                    S.op("act", lambda e: e.activation(out=t8.t[:, 16:20], in_=t8.t[:, 12:16], func=AF.Exp), reads=[t8.b], writes=[t8.b])

                def attn_bpv(bi, kind, gi, k, T, part):
                    p_ = T["pp"][k % len(T["pp"])]
                    t8 = T["stt"][k % len(T["stt"])]
                    kc = gi // 2
                    vprev = None
                    if kind == 0:
                        vprev = Vp[bi - 1] if bi > 0 else Vpp[a]
                    else:
                        Vc = smp["Vc"]
                    if part == 0:
                        S.op("dve", lambda e: e.tensor_tensor(out=t8.t[:, 20:24], in0=t8.t[:, 8:12], in1=t8.t[:, 16:20], op=ALU.add),
                             reads=[t8.b], writes=[t8.b])
                        S.op("dve", lambda e: e.reciprocal(out=t8.t[:, 24:28], in_=t8.t[:, 20:24]), reads=[t8.b], writes=[t8.b])
                        S.op("dve", lambda e: e.tensor_tensor(out=p_.t[:], in0=p_.t[:],
                                                              in1=t8.t[:, 24:28].unsqueeze(2).to_broadcast([128, 4, 256]), op=ALU.mult),
                             reads=[t8.b, p_.b], writes=[p_.b])
                        return
                    bt = pb[7]
                    btv = bfview(bt)
                    for h4 in range(4):
                        for part in range(2):
                            tr(btv[:, h4 * 2 + part, :], p_.t[:, h4, part * 128:(part + 1) * 128], identb.t[:], [p_.b, identb.b], bt.b)
                    pt = T["pT"][k % len(T["pT"])]
                    S.op("act", lambda e: e.activation(out=pt.t[:], in_=btv, func=AF.Copy), reads=[bt.b], writes=[pt.b])
                    bo = Tile(pb[0].t, pv_half[k % 2])
                    c0 = (k % 2) * 256
                    for ci in range(2):
                        o = bo.t[:, c0 + ci * 128:c0 + (ci + 1) * 128]
                        seqm = []
                        for hf in range(2):
                            seqm.append((Vp[bi], 2 * kc + hf, (2 * hf + ci) * 2 + 1))
                        if kind == 0:
                            for hf in range(2):
                                seqm.append((vprev, 2 * kc + hf, (2 * hf + ci) * 2))
                        nn = len(seqm)
                        for i, (vt, g, pidx) in enumerate(seqm):
                            mm(o, vt.t[:, g, :], pt.t[:, pidx, :], i == 0, (i == nn - 1) and kind == 0, [vt.b, pt.b], bo.b)
                        if kind == 1:
                            for hf in range(2):
                                g = 2 * kc + hf
                                pidx = (2 * hf + ci) * 2
                                for b in range(16):
                                    mm(o[:, 8 * b:8 * b + 8], Vc.t[:, b, g, :], pt.t[:, pidx, 8 * b:8 * b + 8], False,
                                       (hf == 1 and b == 15), [Vc.b, pt.b], bo.b)
                    S.op("act", lambda e: e.activation(
                        out=OT[bi].t[:, 2 * gi:2 * gi + 2, :], in_=bo.t[:, c0:c0 + 256].rearrange("p (c t) -> p c t", t=128), func=AF.Copy),
                        reads=[bo.b], writes=[OT[bi].b])

                def attn_pipeline(bis, kind, T):
                    items = [(bi, gi) for bi in bis for gi in range(4)]
                    for hb in pv_half:
                        hb.w = pb[0].b.w
                        hb.r = list(pb[0].b.r)

                    def front(k, part=None):
                        bi, gi = items[k]
                        if part in (None, 0):
                            if gi == 0:
                                attn_pre(bi, kind, T)
                            attn_sa(bi, kind, gi, k, T, 0)
                        if part in (None, 1):
                            attn_sa(bi, kind, gi, k, T, 1)

                    skew = 2 if kind == 0 else 1
                    nsets[0] = skew + 1
                    for k0 in range(min(skew, len(items))):
                        front(k0)
                    for k, (bi, gi) in enumerate(items):
                        if k + skew < len(items):
                            front(k + skew, 0)
                        attn_bpv(bi, kind, gi, k, T, 0)
                        attn_bpv(bi, kind, gi, k, T, 1)
                        if k + skew < len(items):
                            front(k + skew, 1)
                        if kind == 0 and bi == SB - 1 and gi == 3:
                            S.op("dve", lambda e, bi=bi: e.tensor_copy(out=kTp[a].t[:], in_=kT[bi].t[:]), reads=[kT[bi].b], writes=[kTp[a].b])
                            S.op("dve", lambda e, bi=bi: e.tensor_copy(out=Vpp[a].t[:], in_=Vp[bi].t[:]), reads=[Vp[bi].b], writes=[Vpp[a].b])
                    pb[0].b.r = list(pb[0].b.r) + [ev for hb in pv_half for ev in ([hb.w] if hb.w else []) + hb.r]

                def phase_c():
                    for nt in range(2):
                        sl, _ = w_next("wo", a, nt)
                        dense_tm(blocks, sl, None, list(range(8)), lambda bi, kc: (OT[bi].t[:, kc, :], OT[bi].b),
                                 lambda bi, kind, bank, nt=nt: resid_add(bi, kind, 0, nt, bank), banks=(2, 3, 4))
                    S.flush()

                with ExitStack() as ph:
                    if not has_s:
                        build_gates(ph, l, sorted(set(blocks)))
                        norm_phase(ph, blocks, l, 0)
                    for i in range(npb):
                        qkb[i] = SBT(ph, f"qkb{i}", [128, 1280], BF16)
                        Vp[i] = SBT(ph, f"Vp{i}", [128, 4, 128], BF16)
                        kT[i] = SBT(ph, f"kT{i}", [128, 2, 128], BF16)
                    kvf = [SBT(ph, f"kvf{i}", [128, 512], F32) for i in range(2)]
                    rot = [SBT(ph, f"rot{i}", [128, 8, 16], F32) for i in range(2)]
                    rtm = [SBT(ph, f"rtm{i}", [128, 8, 8], F32) for i in range(2)]
                    TA = temps(ph, 3, 2)
                    qfs = [SBT(ph, f"qf{i}", [128, 512], F32) for i in range(2)]
                    qfc = [0]
                    for i in range(nb):
                        S.op("dve", lambda e, i=i: e.memset(Vp[i].t[:], 0.0), writes=[Vp[i].b])

                    def evac_qkv(nt):
                        def f(bi, kind, bank):
                            import os
                            ksub = int(os.environ.get("KSUB", "9"))
                            if ksub == 0:
                                S.op("act", lambda e: e.activation(out=qkb[bi].t[:, 0:512], in_=bank.t[:], func=AF.Copy),
                                     reads=[bank.b], writes=[qkb[bi].b])
                                return
                            gb = gblk0 + bi if kind == 0 else NPB
                            is_out = (kind == 1) or (gb == NPB - 1)
                            HF = 2 if nt < 2 else 1
                            W_ = HF * 256
                            pv3 = bank.t[:, 0:W_].rearrange("p (hf cl d) -> p hf cl d", hf=HF, d=64)
                            if nt < 2:
                                qv3 = qkb[bi].t[:, nt * 512:(nt + 1) * 512].rearrange("p (cl hf d) -> p hf cl d", hf=2, d=64)
                            else:
                                qv3 = qkb[bi].t[:, 1024:1280].rearrange("p (hf cl d) -> p hf cl d", hf=1, d=64)
                            S.op("act", lambda e: e.activation(out=qv3[:, :, :, 16:64], in_=pv3[:, :, :, 16:64], func=AF.Copy),
                                 reads=[bank.b], writes=[qkb[bi].b])
                            if ksub == 1:
                                return
                            nh = 4 * HF
                            r = rot[(bi + nt) % 2]
                            t_ = rtm[(bi + nt) % 2]
                            qf = qfs[qfc[0] % 2]
                            qfc[0] += 1
                            S.op("act", lambda e: e.activation(out=qf.t[:], in_=bank.t[:], func=AF.Copy), reads=[bank.b], writes=[qf.b])
                            src3 = qf.t[:, 0:nh * 64].rearrange("p (h d) -> p h d", d=64)
                            cosb = cs.t[:, gb, 0:8].unsqueeze(1).to_broadcast([128, nh, 8])
                            sinb = cs.t[:, gb, 8:16].unsqueeze(1).to_broadcast([128, nh, 8])
                            x1 = src3[:, :, 0:8]
                            x2 = src3[:, :, 8:16]
                            r1 = r.t[:, 0:nh, 0:8]
                            r2 = r.t[:, 0:nh, 8:16]
                            tv = t_.t[:, 0:nh, :]
                            rd = [qf.b, cs.b]
                            S.op("dve", lambda e: e.tensor_tensor(out=r1, in0=x1, in1=cosb, op=ALU.mult), reads=rd, writes=[r.b])
                            if ksub == 2:
                                return
                            S.op("dve", lambda e: e.tensor_tensor(out=tv, in0=x2, in1=sinb, op=ALU.mult), reads=rd, writes=[t_.b])
                            S.op("dve", lambda e: e.tensor_tensor(out=r1, in0=r1, in1=tv, op=ALU.subtract), reads=[t_.b, r.b], writes=[r.b])
                            S.op("dve", lambda e: e.tensor_tensor(out=r2, in0=x2, in1=cosb, op=ALU.mult), reads=rd + [r.b], writes=[r.b])
                            S.op("dve", lambda e: e.tensor_tensor(out=tv, in0=x1, in1=sinb, op=ALU.mult), reads=rd + [r.b], writes=[t_.b])
                            S.op("dve", lambda e: e.tensor_tensor(out=r2, in0=r2, in1=tv, op=ALU.add), reads=[t_.b, r.b], writes=[r.b])
                            if ksub == 3:
                                return
                            if nt < 2:
                                for hf in range(2):
                                    S.op("dve", lambda e, hf=hf: e.tensor_copy(out=qv3[:, hf, :, 0:16], in_=r.t[:, 4 * hf:4 * hf + 4, :]),
                                         reads=[r.b], writes=[qkb[bi].b])
                            else:
                                S.op("dve", lambda e: e.tensor_copy(out=qv3[:, 0, :, 0:16], in_=r.t[:, 0:4, :]), reads=[r.b], writes=[qkb[bi].b])
                            if ksub == 4:
                                return
                            if nt == 2:
                                vv = bank.t[:, 256:512].rearrange("p (g2 gp d) -> p g2 gp d", gp=2, d=64)
                                vd = Vp[bi].t[:].rearrange("p (g2 gp) c -> p g2 gp c", gp=2)
                                for gp in range(2):
                                    S.op("act", lambda e, gp=gp: e.activation(out=vd[:, :, gp, gp * 64:gp * 64 + 64], in_=vv[:, :, gp, :], func=AF.Copy),
                                         reads=[bank.b], writes=[Vp[bi].b])
                                if is_out:
                                    kf = kvf[kind]
                                    S.op("act", lambda e: e.activation(out=kf.t[:], in_=bank.t[:], func=AF.Copy), reads=[bank.b], writes=[kf.b])
                                    S.op("dve", lambda e: e.tensor_copy(out=kf.t[:, 0:256].rearrange("p (h d) -> p h d", d=64)[:, :, 0:16],
                                                                        in_=r.t[:, 0:4, :]), reads=[r.b, kf.b], writes=[kf.b])
                                    if kind == 0:
                                        S.dma("sp", lambda e: e.dma_start(out=O["kp"][a], in_=kf.t[:, 0:256]), reads=[kf.b], out=True)
                                        S.dma("sp", lambda e: e.dma_start(out=O["vp"][a], in_=kf.t[:, 256:512]), reads=[kf.b], out=True)
                                    else:
                                        for b in range(16):
                                            S.dma("sp", lambda e, b=b: e.dma_start(out=O["ks"][a][b, 120:128, :], in_=kf.t[8 * b:8 * b + 8, 0:256]),
                                                  reads=[kf.b], out=True)
                                            S.dma("sp", lambda e, b=b: e.dma_start(out=O["vs"][a][b, 120:128, :], in_=kf.t[8 * b:8 * b + 8, 256:512]),
                                                  reads=[kf.b], out=True)
                        return f

                    for nt in range(3):
                        sl, br = w_next("qkv", a, nt)
                        dense_tm(blocks, sl, br, list(range(8)), h_chunk, evac_qkv(nt))
                    if dbg == "attnA1":
                        dump(qkb[1].t[:], 1280, qkb[1].b, bf=True)
                        dump(Vp[1].t[:].rearrange("p g c -> p (g c)"), 512, Vp[1].b, bf=True)
                        stop()
                    attn_pipeline(list(range(npb)), 0, TA)
                    if has_s:
                        S.flush()
                    else:
                        phase_c()
                    if dbg == "attnA":
                        dump(qkb[1].t[:], 1280, qkb[1].b, bf=True)
                        dump(Vp[1].t[:].rearrange("p g c -> p (g c)"), 512, Vp[1].b, bf=True)
                        dump(kT[1].t[:].rearrange("p g c -> p (g c)"), 256, kT[1].b, bf=True)
                        for i in range(2):
                            dump(OT[i].t[:].rearrange("p c t -> p (c t)"), 1024, OT[i].b, bf=True)
                        stop()

                if has_s:
                    with ExitStack() as ph:
                        TB = temps(ph, 2, 1)
                        ckb = SBT(ph, "ckb", [128, 8, 256], BF16)
                        KTs = SBT(ph, "KTs", [128, 2, 16, 128], BF16)
                        Vc = SBT(ph, "Vc", [128, 16, 4, 128], BF16)
                        QmT = [SBT(ph, f"QmT{i}", [128, 16, 128], BF16) for i in range(2)]
                        bmk = SBT(ph, "bmk", [128, 16, 128], BF16)
                        smp.update(QmT=QmT, KTs=KTs, Vc=Vc, bmk=bmk)
                        S.dma("pool", lambda e: e.dma_start(out=bmk.t[:].rearrange("p b t -> p (b t)"), in_=I["bmask"]), writes=[bmk.b])
                        S.op("dve", lambda e: e.memset(Vc.t[:], 0.0), writes=[Vc.b])
                        cvv = I["cv"][a].rearrange("b k (g2 gp d) -> gp b k g2 d", gp=2, d=64)
                        for gp in range(2):
                            for b in range(16):
                                S.dma("pool", lambda e, gp=gp, b=b: e.dma_start(
                                    out=Vc.t[:, b].rearrange("p (g2 gp) c -> p g2 gp c", gp=2)[:, :, gp, gp * 64:gp * 64 + 64], in_=cvv[gp][b]),
                                    writes=[Vc.b])
                        for half in range(2):
                            S.dma("pool", lambda e, half=half: e.dma_start(
                                out=ckb.t[:], in_=I["ck"][a][8 * half:8 * half + 8].rearrange("b k c -> k b c")), writes=[ckb.b])
                            for b8 in range(8):
                                b = 8 * half + b8
                                bank = pb[3 + b % 2]
                                bv = bfview(bank)
                                for kc in range(2):
                                    tr(bv[:, kc, :], ckb.t[:, b8, kc * 128:(kc + 1) * 128], identb.t[:], [ckb.b, identb.b], bank.b)
                                S.op("act", lambda e, b=b, bv=bv: e.activation(out=KTs.t[:, :, b, :], in_=bv[:, 0:2, :], func=AF.Copy),
                                     reads=[bank.b], writes=[KTs.b])
                        for nm_i, nm_o in (("ck", "ks"), ("cv", "vs")):
                            S.dma("sp", lambda e, nm_i=nm_i, nm_o=nm_o: e.dma_start(out=O[nm_o][a][:, 0:120, :], in_=I[nm_i][a][:, 8:128, :]),
                                  out=True, hold=False)
                        attn_pipeline([npb], 1, TB)
                        S.flush()

                if has_s:
                    phase_c()

        def sgu_layer(st, blocks, a, l):
            if 1 in blocks:
                with ExitStack() as ph:
                    build_gates(ph, l, sorted(set(blocks)))
                    norm_phase(ph, blocks, l, 0)
                    S.flush()
            with ExitStack() as ph:
                if 1 not in blocks:
                    build_gates(ph, l, sorted(set(blocks)))
                    norm_phase(ph, blocks, l, 0)
                nb = len(blocks)
                lng = SBT(ph, "lng", [128, 2048], F32)
                lnb = SBT(ph, "lnb", [128, 2048], F32)
                S.dma("sp", lambda e: e.dma_start(out=lng.t[:], in_=I["lnrows"][a].partition_broadcast(128)), writes=[lng.b])
                S.dma("sp", lambda e: e.dma_start(out=lnb.t[:], in_=I["lnrows"][2 + a].partition_broadcast(128)), writes=[lnb.b])
                pTs = [SBT(ph, f"pTs{i}", [128, 16, 128], BF16) for i in range(nb)]
                ub = [SBT(ph, f"ub{i}", [128, 512], BF16) for i in range(2 * nb)]
                vtmp = [SBT(ph, f"vtmp{i}", [128, 512], F32) for i in range(3)]
                vnb = [SBT(ph, f"vnb{i}", [128, 512], BF16) for i in range(4)]
                pg = [SBT(ph, f"pg{i}", [128, 512], BF16) for i in range(4)]
                statsl = [SBT(ph, f"stats{i}", [128, 4, 6], F32) for i in range(nb)]
                mvl = [SBT(ph, f"mv{i}", [128, 4], F32) for i in range(nb)]
                vout = SBT(ph, "vout", [128, 2048], F32) if 1 in blocks else None
                vc = [0]

                def evac_stats(t4):
                    def f(bi, kind, bank):
                        vt = vtmp[vc[0] % 3]
                        vc[0] += 1
                        S.op("act", lambda e: e.activation(out=vt.t[:], in_=bank.t[:], func=AF.Gelu), reads=[bank.b], writes=[vt.b])
                        S.op("dve", lambda e: e.bn_stats(out=statsl[bi].t[:, t4, :], in_=vt.t[:]), reads=[vt.b], writes=[statsl[bi].b])
                        if t4 == 3:
                            S.op("dve", lambda e: e.bn_aggr(out=mvl[bi].t[:, 0:2], in_=statsl[bi].t[:]), reads=[statsl[bi].b], writes=[mvl[bi].b])
                            S.op("act", lambda e: e.activation(out=mvl[bi].t[:, 2:3], in_=mvl[bi].t[:, 1:2], func=AF.Sqrt, scale=1.0, bias=EPS),
                                 reads=[mvl[bi].b], writes=[mvl[bi].b])
                            S.op("dve", lambda e: e.reciprocal(out=mvl[bi].t[:, 3:4], in_=mvl[bi].t[:, 2:3]), reads=[mvl[bi].b], writes=[mvl[bi].b])
                    return f

                for t4 in range(4):
                    sl, br = w_next("win", a, 4 + t4)
                    dense_tm(blocks, sl, br, list(range(8)), h_chunk, evac_stats(t4))

                for g in range(4):
                    def evac_u(bi, kind, bank, g=g):
                        u_ = ub[(g % 2) * nb + bi]
                        S.op("act", lambda e: e.activation(out=u_.t[:], in_=bank.t[:], func=AF.Gelu), reads=[bank.b], writes=[u_.b])

                    def evac_v(bi, kind, bank, g=g):
                        vt = vtmp[vc[0] % 3]
                        vn = vnb[vc[0] % 4]
                        p_ = pg[vc[0] % 4]
                        vc[0] += 1
                        u_ = ub[(g % 2) * nb + bi]
                        S.op("act", lambda e: e.activation(out=vt.t[:], in_=bank.t[:], func=AF.Gelu), reads=[bank.b], writes=[vt.b])
                        S.op("dve", lambda e: e.tensor_scalar(out=vt.t[:], in0=vt.t[:], scalar1=mvl[bi].t[:, 0:1], scalar2=mvl[bi].t[:, 3:4],
                                                              op0=ALU.subtract, op1=ALU.mult), reads=[vt.b, mvl[bi].b], writes=[vt.b])
                        S.op("dve", lambda e: e.tensor_tensor(out=vt.t[:], in0=vt.t[:], in1=lng.t[:, g * 512:(g + 1) * 512], op=ALU.mult),
                             reads=[vt.b, lng.b], writes=[vt.b])
                        if kind == 1:
                            S.op("dve", lambda e: e.tensor_tensor(out=vout.t[:, g * 512:(g + 1) * 512], in0=vt.t[:],
                                                                  in1=lnb.t[:, g * 512:(g + 1) * 512], op=ALU.add),
                                 reads=[vt.b, lnb.b], writes=[vout.b])
                            S.op("dve", lambda e: e.tensor_copy(out=vn.t[:], in_=vout.t[:, g * 512:(g + 1) * 512]), reads=[vout.b], writes=[vn.b])
                        else:
                            S.op("dve", lambda e: e.tensor_tensor(out=vn.t[:], in0=vt.t[:], in1=lnb.t[:, g * 512:(g + 1) * 512], op=ALU.add),
                                 reads=[vt.b, lnb.b], writes=[vn.b])
                        vcv = vc[0]

                        def tail():
                            evac_v_tail(bi, kind, g, vn, p_, u_, vcv)
                        return tail

                    def evac_v_tail(bi, kind, g, vn, p_, u_, vcv):
                        bm = pb[3 + vcv % 2]
                        mm(bm.t[:], wm.t[:, kind, a, g, :], vn.t[:], True, True, [wm.b, vn.b], bm.b)
                        S.op("dve", lambda e: e.scalar_tensor_tensor(out=p_.t[:], in0=bm.t[:], scalar=bsp.t[:, kind, 4 * a + g:4 * a + g + 1],
                                                                     in1=u_.t[:], op0=ALU.add, op1=ALU.mult),
                             reads=[bm.b, bsp.b, u_.b], writes=[p_.b])
                        bt = pb[5 + vcv % 2]
                        btv = bfview(bt)
                        for j in range(4):
                            tr(btv[:, j, :], p_.t[:, j * 128:(j + 1) * 128], identb.t[:], [p_.b, identb.b], bt.b)
                        S.op("act", lambda e: e.activation(out=pTs[bi].t[:, 4 * g:4 * g + 4, :], in_=btv[:, 0:4, :], func=AF.Copy),
                             reads=[bt.b], writes=[pTs[bi].b])

                    sl, br = w_next("win", a, g)
                    dense_tm(blocks, sl, br, list(range(8)), h_chunk, evac_u)
                    sl, br = w_next("win", a, 4 + g)
                    dense_tm(blocks, sl, br, list(range(8)), h_chunk, evac_v)
                if vout is not None:
                    S.dma("sp", lambda e: e.dma_start(out=O["sguv"][a], in_=vout.t[:]), reads=[vout.b], out=True)
                for kh in range(2):
                    for nt in range(2):
                        sl, _ = w_next("wout", a, kh * 2 + nt)
                        dense_tm(blocks, sl, None, list(range(8)), lambda bi, kc, kh=kh: (pTs[bi].t[:, kh * 8 + kc, :], pTs[bi].b),
                                 lambda bi, kind, bank, nt=nt: resid_add(bi, kind, 0, nt, bank))
                S.flush()

        def ffn_layer(st, blocks, l):
            if 1 in blocks:
                with ExitStack() as ph:
                    norm_phase(ph, blocks, l, 1)
                    S.flush()
            with ExitStack() as ph:
                if 1 not in blocks:
                    norm_phase(ph, blocks, l, 1)
                nb = len(blocks)
                npb = sum(1 for k in blocks if k == 0)
                N = npb * 128
                has_s = 1 in blocks
                gT = ph.enter_context(nc.sbuf_tensor(uname("gT"), [128, 22, nb * 128], BF16))
                gTb = Buf("gT")
                nset = 2 if has_s else 3
                ab = [[SBT(ph, f"ab{i}{h}", [128, 2 + SB * 128], F32) for h in range(2)] for i in range(nset)]
                tt = [[SBT(ph, f"tt{i}{h}", [128, SB * 128], F32) for h in range(2)] for i in range(nset)]
                qq = [[SBT(ph, f"qq{i}{h}", [128, SB * 128], F32) for h in range(2)] for i in range(nset)]
                if has_s:
                    abS = [[SBT(ph, f"abS{i}{h}", [128, 16, 10], F32) for h in range(2)] for i in range(2)]
                    ttS = [[SBT(ph, f"ttS{i}{h}", [128, 16, 8], F32) for h in range(2)] for i in range(2)]
                    stS = SBT(ph, "stS", [128, NFC, 32], F32)
                    aLs = SBT(ph, "aLs", [128, NFC, 32], F32)
                    stg = [SBT(ph, "stg0", [32, 1408], F32)] * 2
                    for q4 in range(4):
                        sg = stg[q4 % 2]
                        S.dma("sp", lambda e, q4=q4, sg=sg: e.dma_start(out=sg.t[:], in_=I["sconv"][l][:, q4 * 1408:(q4 + 1) * 1408]),
                              writes=[sg.b])
                        bank = pb[6 + q4 % 2]
                        for j in range(11):
                            mm(bank.t[:, j * 32:(j + 1) * 32], sg.t[:, j * 128:(j + 1) * 128], identf.t[0:32, 0:32], True, True, [sg.b, identf.b], bank.b)
                        S.op("act", lambda e, q4=q4, bank=bank: e.activation(
                            out=stS.t[:, q4 * 11:(q4 + 1) * 11, :].rearrange("p j r -> p (j r)"), in_=bank.t[:, 0:352], func=AF.Copy),
                            reads=[bank.b], writes=[stS.b])
                cnt = [0]
                ffn_pend = []
                for jj in range(11):
                    sl, _ = w_next("wup", l, jj)
                    def chunk(jx):
                        j = 2 * jj + jx
                        par = cnt[0] % nset
                        cnt[0] += 1
                        banks = (pb[0 + 2 * par], pb[1 + 2 * par])
                        bS = pb[4 + par]
                        for h in range(2):
                            wcol = sl.t[:, :, h * 256 + jx * 128:h * 256 + jx * 128 + 128]
                            if npb:
                                for k in range(8):
                                    mm(banks[h].t[:, 0:N], wcol[:, k, :], hT[:, k, 0:N], k == 0, k == 7,
                                       [sl.b] + hTb[0:npb], banks[h].b)
                            if has_s:
                                for k in range(8):
                                    mm(bS.t[:, h * 128:(h + 1) * 128], wcol[:, k, :], hT[:, k, N:N + 128], k == 0, k == 7,
                                       [sl.b, hTb[npb]], bS.b)
                        for h in range(2):
                            fc = j + 22 * h
                            w0 = cwT.t[:, fc, 4 * l + 0:4 * l + 1]
                            w1 = cwT.t[:, fc, 4 * l + 1:4 * l + 2]
                            w2 = cwT.t[:, fc, 4 * l + 2:4 * l + 3]
                            cb = cwT.t[:, fc, 4 * l + 3:4 * l + 4]
                            if npb:
                                a_ = ab[par][h]
                                t_ = tt[par][h]
                                S.op("act", lambda e, a_=a_, h=h: e.activation(out=a_.t[:, 2:2 + N], in_=banks[h].t[:, 0:N], func=AF.Copy),
                                     reads=[banks[h].b], writes=[a_.b])
                                if h == 0:
                                    S.op("dve", lambda e, a_=a_, fc=fc: e.tensor_copy(out=a_.t[:, 0:2], in_=stP.t[:, l, fc, :]), reads=[stPb[l][fc]], writes=[a_.b])
                                    S.op("dve", lambda e, a_=a_, t_=t_, w2=w2, cb=cb: e.tensor_scalar(
                                        out=t_.t[:, 0:N], in0=a_.t[:, 2:2 + N], scalar1=w2, scalar2=cb, op0=ALU.mult, op1=ALU.add),
                                        reads=[a_.b, cwT.b], writes=[t_.b])
                                    S.op("dve", lambda e, a_=a_, t_=t_, w1=w1: e.scalar_tensor_tensor(
                                        out=t_.t[:, 0:N], in0=a_.t[:, 1:1 + N], scalar=w1, in1=t_.t[:, 0:N], op0=ALU.mult, op1=ALU.add),
                                        reads=[a_.b, cwT.b, t_.b], writes=[t_.b])
                                    S.op("dve", lambda e, a_=a_, t_=t_, w0=w0: e.scalar_tensor_tensor(
                                        out=t_.t[:, 0:N], in0=a_.t[:, 0:N], scalar=w0, in1=t_.t[:, 0:N], op0=ALU.mult, op1=ALU.add),
                                        reads=[a_.b, cwT.b, t_.b], writes=[t_.b])
                                    S.op("dve", lambda e, a_=a_, fc=fc: e.tensor_copy(out=stP.t[:, l, fc, :], in_=a_.t[:, N:N + 2]), reads=[a_.b], writes=[stPb[l][fc]])
                                else:
                                    q1, q0 = qq[par]
                                    S.op("pool", lambda e, a_=a_, fc=fc: e.tensor_copy(out=a_.t[:, 0:2], in_=stP.t[:, l, fc, :]), reads=[stPb[l][fc]], writes=[a_.b])
                                    S.op("act", lambda e, a_=a_, q1=q1, w1=w1: e.activation(out=q1.t[:, 0:N], in_=a_.t[:, 1:1 + N], func=AF.Copy, scale=w1),
                                         reads=[a_.b, cwT.b], writes=[q1.b])
                                    S.op("act", lambda e, a_=a_, q0=q0, w0=w0: e.activation(out=q0.t[:, 0:N], in_=a_.t[:, 0:N], func=AF.Copy, scale=w0),
                                         reads=[a_.b, cwT.b], writes=[q0.b])
                                    S.op("act", lambda e, a_=a_, t_=t_, w2=w2, cb=cb: e.activation(
                                        out=t_.t[:, 0:N], in_=a_.t[:, 2:2 + N], func=AF.Identity, scale=w2, bias=cb),
                                        reads=[a_.b, cwT.b], writes=[t_.b])
                                    S.op("pool", lambda e, t_=t_, q1=q1: e.tensor_tensor(out=t_.t[:, 0:N], in0=t_.t[:, 0:N], in1=q1.t[:, 0:N], op=ALU.add),
                                         reads=[q1.b, t_.b], writes=[t_.b])
                                    S.op("dve", lambda e, t_=t_, q0=q0: e.tensor_tensor(out=t_.t[:, 0:N], in0=t_.t[:, 0:N], in1=q0.t[:, 0:N], op=ALU.add),
                                         reads=[q0.b, t_.b], writes=[t_.b])
                                    S.op("pool", lambda e, a_=a_, fc=fc: e.tensor_copy(out=stP.t[:, l, fc, :], in_=a_.t[:, N:N + 2]), reads=[a_.b], writes=[stPb[l][fc]])
                            if has_s:
                                a_ = abS[par][h]
                                t_ = ttS[par][h]
                                S.op("act", lambda e, a_=a_, h=h: e.activation(
                                    out=a_.t[:, :, 2:10], in_=bS.t[:, h * 128:(h + 1) * 128].rearrange("p (b t) -> p b t", t=8), func=AF.Copy),
                                    reads=[bS.b], writes=[a_.b])
                                S.op("dve", lambda e, a_=a_, fc=fc: e.tensor_copy(
                                    out=a_.t[:, :, 0:2], in_=stS.t[:, fc, :].rearrange("p (b t) -> p b t", t=2)), reads=[stS.b], writes=[a_.b])
                                S.op("dve", lambda e, a_=a_, t_=t_, w2=w2, cb=cb: e.tensor_scalar(
                                    out=t_.t[:], in0=a_.t[:, :, 2:10], scalar1=w2, scalar2=cb, op0=ALU.mult, op1=ALU.add),
                                    reads=[a_.b, cwT.b], writes=[t_.b])
                                S.op("dve", lambda e, a_=a_, t_=t_, w1=w1: e.scalar_tensor_tensor(
                                    out=t_.t[:], in0=a_.t[:, :, 1:9], scalar=w1, in1=t_.t[:], op0=ALU.mult, op1=ALU.add),
                                    reads=[a_.b, cwT.b, t_.b], writes=[t_.b])
                                S.op("dve", lambda e, a_=a_, t_=t_, w0=w0: e.scalar_tensor_tensor(
                                    out=t_.t[:], in0=a_.t[:, :, 0:8], scalar=w0, in1=t_.t[:], op0=ALU.mult, op1=ALU.add),
                                    reads=[a_.b, cwT.b, t_.b], writes=[t_.b])
                                S.op("dve", lambda e, a_=a_, fc=fc: e.tensor_copy(
                                    out=aLs.t[:, fc, :].rearrange("p (b t) -> p b t", t=2), in_=a_.t[:, :, 8:10]), reads=[a_.b], writes=[aLs.b])
                        def tail():
                            if npb:
                                tg, tu = tt[par][0], tt[par][1]
                                S.op("act", lambda e: e.activation(out=tg.t[:, 0:N], in_=tg.t[:, 0:N], func=AF.Silu), reads=[tg.b], writes=[tg.b])
                                S.op("dve", lambda e: e.tensor_tensor(out=gT[:, j, 0:N], in0=tg.t[:, 0:N], in1=tu.t[:, 0:N], op=ALU.mult),
                                     reads=[tg.b, tu.b], writes=[gTb])
                            if has_s:
                                tgs, tus = ttS[par][0], ttS[par][1]
                                S.op("act", lambda e: e.activation(out=tgs.t[:], in_=tgs.t[:], func=AF.Silu), reads=[tgs.b], writes=[tgs.b])
                                S.op("dve", lambda e: e.tensor_tensor(
                                    out=gT[:, j, N:N + 128].rearrange("p (b t) -> p b t", t=8), in0=tgs.t[:], in1=tus.t[:], op=ALU.mult),
                                    reads=[tgs.b, tus.b], writes=[gTb])
                        return tail

                    for jx in range(2):
                        tl = chunk(jx)
                        if has_s:
                            tl()
                        else:
                            if ffn_pend:
                                ffn_pend.pop(0)()
                            ffn_pend.append(tl)
                while ffn_pend:
                    ffn_pend.pop(0)()
                for kg in range(3):
                    nk = 8 if kg < 2 else 6
                    for nt in range(2):
                        sl, _ = w_next("wdown", l, kg * 2 + nt)
                        dense_tm(blocks, sl, None, list(range(nk)),
                                 lambda bi, kc, kg=kg: (gT[:, kg * 8 + kc, bi * 128:(bi + 1) * 128], gTb),
                                 lambda bi, kind, bank, nt=nt: resid_add(bi, kind, 1, nt, bank, "dve"), banks=(6, 7))
                if has_s:
                    for c0 in range(0, NFC, 4):
                        bank = pb[(c0 // 4) % 2]
                        sg = stg[(c0 // 4) % 2]
                        for j in range(4):
                            mm(bank.t[0:32, j * 128:(j + 1) * 128], aLs.t[:, c0 + j, :], identf.t[:], True, True, [aLs.b, identf.b], bank.b)
                        S.op("act", lambda e, bank=bank, sg=sg: e.activation(out=sg.t[:, 0:512], in_=bank.t[0:32, :], func=AF.Copy),
                             reads=[bank.b], writes=[sg.b])
                        S.dma("sp", lambda e, sg=sg, c0=c0: e.dma_start(out=O["convs"][l][:, c0 * 128:(c0 + 4) * 128], in_=sg.t[:, 0:512]),
                              reads=[sg.b], out=True)
                S.flush()

        for st in range(NST):
            blocks = [0] * SB + ([1] if st == NST - 1 else [])
            with ExitStack() as ph:
                for bi, kind in enumerate(blocks):
                    src = I["xp"][(st * SB + bi) * 128:(st * SB + bi + 1) * 128, :] if kind == 0 else I["xs"]
                    S.dma("sp", lambda e, bi=bi, src=src: e.dma_start(out=x[bi].t[:], in_=src), writes=[x[bi].b])
                S.flush()
            for l in range(4):
                if l % 2 == 0:
                    attn_layer(st, blocks, l // 2, l)
                else:
                    sgu_layer(st, blocks, l // 2, l)
                if dbg == f"mix{l}" and st == 0:
                    for i in range(4):
                        dump(x[i].t[:], 1024, x[i].b)
                    stop()
                ffn_layer(st, blocks, l)
                if dbg == f"ffn{l}" and st == 0:
                    for i in range(4):
                        dump(x[i].t[:], 1024, x[i].b)
                    S.op("dve", lambda e: e.engine_nop(), reads=stP_all, writes=[stP.b])
                    dump(stP.t[:].rearrange("p l c t -> p (l c t)"), 352, stP.b)
                    stop()
            with ExitStack() as ph:
                ssl = [SBT(ph, f"fss{i}", [128, 4], F32) for i in range(NBMAX)]
                junk = SBT(ph, "fjunk", [128, D], BF16)
                yo = [SBT(ph, f"yo{i}", [128, D], F32) for i in range(2)]
                for bi, kind in enumerate(blocks):
                    S.op("act", lambda e, bi=bi: e.activation(out=junk.t[:], in_=x[bi].t[:], func=AF.Square, accum_out=ssl[bi].t[:, 0:1]),
                         reads=[x[bi].b], writes=[junk.b, ssl[bi].b])
                    S.op("act", lambda e, bi=bi: e.activation(out=ssl[bi].t[:, 1:2], in_=ssl[bi].t[:, 0:1], func=AF.Sqrt,
                                                              scale=1.0 / D, bias=EPS), reads=[ssl[bi].b], writes=[ssl[bi].b])
                    S.op("dve", lambda e, bi=bi: e.reciprocal(out=ssl[bi].t[:, 2:3], in_=ssl[bi].t[:, 1:2]),
                         reads=[ssl[bi].b], writes=[ssl[bi].b])
                    y_ = yo[bi % 2]
                    S.op("dve", lambda e, bi=bi, y_=y_: e.scalar_tensor_tensor(out=y_.t[:], in0=x[bi].t[:], scalar=ssl[bi].t[:, 2:3],
                                                                              in1=normfB.t[:], op0=ALU.mult, op1=ALU.mult),
                         reads=[x[bi].b, ssl[bi].b, normfB.b], writes=[y_.b])
                    dst = O["yp"][(st * SB + bi) * 128:(st * SB + bi + 1) * 128, :] if kind == 0 else O["ys"]
                    S.dma("sp", lambda e, y_=y_, dst=dst: e.dma_start(out=dst, in_=y_.t[:]), reads=[y_.b], out=True)
                S.flush()

        with ExitStack() as ph:
            stg2 = [SBT(ph, f"stg2{i}", [2, 512], F32) for i in range(2)]
            for l in range(4):
                for c0 in range(0, NFC, 4):
                    i = (l * 11 + c0 // 4) % 2
                    bank = pb[i]
                    for j in range(4):
                        mm(bank.t[0:2, j * 128:(j + 1) * 128], stP.t[:, l, c0 + j, :], identf.t[:], True, True, [stPb[l][c0 + j], identf.b], bank.b)
                    S.op("act", lambda e, bank=bank, i=i: e.activation(out=stg2[i].t[:], in_=bank.t[0:2, :], func=AF.Copy),
                         reads=[bank.b], writes=[stg2[i].b])
                    S.dma("sp", lambda e, i=i, l=l, c0=c0: e.dma_start(out=O["convp"][l][:, c0 * 128:(c0 + 4) * 128], in_=stg2[i].t[:]),
                          reads=[stg2[i].b], out=True)
            S.flush(final=True)
        assert wstate["next"] == len(wseq)
    return nc


_CACHE = {}


def _consts():
    i = np.arange(128)[:, None]
    j = np.arange(128)[None, :]
    ninf = np.float32(NEG)
    prev = np.where(j >= i, 0.0, ninf)
    own = np.where(j <= i, 0.0, ninf)
    maskP = np.concatenate([prev, own], 1).astype(np.float32)
    maskP0 = np.concatenate([np.full((128, 128), ninf), own], 1).astype(np.float32)
    t = i % 8
    cachem = np.where(j >= t, 0.0, ninf)
    newm = np.where((j // 8 == i // 8) & (j % 8 <= t), 0.0, ninf)
    maskS = np.concatenate([cachem, newm], 1).astype(np.float32)
    trilT = (i <= j).astype(np.float32)
    bm = (np.arange(128)[None, :] // 8 == np.arange(16)[:, None]).astype(np.float32).reshape(1, 16 * 128)
    bmask = np.ascontiguousarray(np.broadcast_to(bm, (128, 16 * 128)))
    return dict(ident=np.eye(128, dtype=np.float32), maskP=maskP, maskP0=maskP0, maskS=maskS, trilT=trilT, bmask=bmask)


def _cossin(pos):
    half = 8
    inv = (np.float32(500000.0) ** (-np.arange(0, 16, 2, dtype=np.float32) / np.float32(16))).astype(np.float32)
    ang = pos.astype(np.float32)[:, None] * inv[None, :]
    return np.concatenate([np.cos(ang), np.sin(ang)], 1).astype(np.float32)


def _prep(x_prompt, x_sample, c_prompt, c_sample, cache_k, cache_v, state_conv,
          w_ada, b_ada, norm_mix, norm_ffn, w_qkv, b_qkv, attn_sink, w_o,
          w_sgu_in, b_sgu_in, sgu_ln_g, sgu_ln_b, w_spatial, b_spatial, w_sgu_out,
          w_up, conv_w, conv_b, w_down, norm_final):
    f = lambda a: np.ascontiguousarray(np.asarray(a, dtype=np.float32))
    x_prompt, x_sample, c_prompt, c_sample = f(x_prompt), f(x_sample), f(c_prompt), f(c_sample)
    cache_k, cache_v, state_conv = f(cache_k), f(cache_v), f(state_conv)
    consts = _consts()
    w_spatial = f(w_spatial)
    b_spatial = f(b_spatial)
    wspT = np.ascontiguousarray(w_spatial.transpose(0, 1, 3, 2))
    wspST = np.zeros((2, 4, 128, 128), np.float32)
    for b in range(16):
        wspST[:, :, 8 * b:8 * b + 8, 8 * b:8 * b + 8] = wspT[:, :, 0:8, 0:8]
    bspP = np.ascontiguousarray(b_spatial.transpose(2, 0, 1).reshape(128, 8))
    bspS = np.ascontiguousarray(bspP[np.arange(128) % 8])
    perm = [0, 1, 4, 5, 2, 3, 6, 7, 8, 9, 12, 13, 10, 11, 14, 15]
    def tiles_k(W):
        L, K, N = W.shape
        t = W.reshape(L, K // 1024, 8, 128, N // 512, 512).transpose(0, 1, 4, 3, 2, 5)
        return np.ascontiguousarray(t.reshape(L, (K // 1024) * (N // 512), 128, 4096))

    w_o_ = f(w_o).reshape(2, 2, 2, 4, 64, D).transpose(0, 2, 4, 1, 3, 5).reshape(2, 128, 8, 2, 512)
    wt_o = np.ascontiguousarray(w_o_.transpose(0, 3, 1, 2, 4).reshape(2, 2, 128, 4096))
    w_up_ = f(w_up)
    wu = np.concatenate([w_up_[:, :, :DFF].reshape(4, D, 11, 256), w_up_[:, :, DFF:].reshape(4, D, 11, 256)], 3)
    wt_up = np.ascontiguousarray(wu.reshape(4, 8, 128, 11, 512).transpose(0, 3, 2, 1, 4).reshape(4, 11, 128, 4096))
    w_dn = np.zeros((4, 3072, D), np.float32)
    w_dn[:, :DFF] = f(w_down)
    shared = dict(
        wt_ada=tiles_k(f(w_ada)), b_ada=f(b_ada), rows8=np.concatenate([f(norm_mix), f(norm_ffn)], 0), normf=f(norm_final).reshape(1, D),
        wt_qkv=tiles_k(f(w_qkv)), b_qkv=f(b_qkv), sinkp=np.ascontiguousarray(f(attn_sink)[:, perm]), wt_o=wt_o,
        wt_in=tiles_k(f(w_sgu_in)), b_in=f(b_sgu_in), lnrows=np.concatenate([f(sgu_ln_g), f(sgu_ln_b)], 0),
        wspT=wspT, wspST=wspST, bspP=bspP, bspS=bspS, wt_out=tiles_k(f(w_sgu_out)), wt_up=wt_up,
        convrows=np.ascontiguousarray(np.concatenate([f(conv_w), f(conv_b)[:, None, :]], 1).reshape(16, 2 * DFF)),
        wt_down=tiles_k(w_dn), **consts)
    in_maps = []
    for c in range(8):
        seq, r = c // 4, c % 4
        p0 = PROC_START[r]
        pos = np.concatenate([np.arange(p0, p0 + NPB * 128), 8192 + (np.arange(128) % 8)])
        m = dict(shared)
        m.update(
            xp=np.ascontiguousarray(x_prompt[seq, p0:p0 + NPB * 128]),
            xs=np.ascontiguousarray(x_sample[16 * c:16 * c + 16].reshape(128, D)),
            c17=np.ascontiguousarray(np.concatenate([c_prompt[seq:seq + 1], c_sample[16 * c:16 * c + 16]], 0)),
            ck=np.ascontiguousarray(cache_k[:, 16 * c:16 * c + 16].reshape(2, 16, 128, 256)),
            cv=np.ascontiguousarray(cache_v[:, 16 * c:16 * c + 16].reshape(2, 16, 128, 256)),
            sconv=np.ascontiguousarray(state_conv[:, 16 * c:16 * c + 16].reshape(4, 32, 2 * DFF)),
            cossin=_cossin(pos),
        )
        in_maps.append(m)
    return in_maps


def kernel(**inputs):
    in_maps = _prep(**inputs)
    if "nc" not in _CACHE:
        _CACHE["nc"] = build_program()
    nc = _CACHE["nc"]
    res = run_bass_kernel_spmd(nc, in_maps, core_ids=list(range(8)))
    R = res.results
    y_prompt = np.zeros((2, 8192, D), np.float32)
    k_p = np.zeros((2, 2, 128, 4, 64), np.float32)
    v_p = np.zeros((2, 2, 128, 4, 64), np.float32)
    conv_p = np.zeros((4, 2, 2, 2 * DFF), np.float32)
    for c in range(8):
        seq, r = c // 4, c % 4
        p0 = PROC_START[r]
        y_prompt[seq, OWN_START[r]:OWN_END[r]] = R[c]["yp"][OWN_START[r] - p0:OWN_END[r] - p0]
        if r == 3:
            k_p[:, seq] = R[c]["kp"].reshape(2, 128, 4, 64)
            v_p[:, seq] = R[c]["vp"].reshape(2, 128, 4, 64)
            conv_p[:, seq] = R[c]["convp"]
    y_sample = np.concatenate([R[c]["ys"].reshape(16, 8, D) for c in range(8)], 0)
    k_s = np.concatenate([R[c]["ks"].reshape(2, 16, 128, 4, 64) for c in range(8)], 1)
    v_s = np.concatenate([R[c]["vs"].reshape(2, 16, 128, 4, 64) for c in range(8)], 1)
    conv_s = np.concatenate([R[c]["convs"].reshape(4, 16, 2, 2 * DFF) for c in range(8)], 1)
    sgu_v = np.concatenate([R[c]["sguv"].reshape(2, 16, 8, 2048) for c in range(8)], 1)
    return (y_prompt, y_sample, k_p, v_p, conv_p, k_s, v_s, conv_s, sgu_v)
```

```python
import numpy as np
from contextlib import ExitStack
import concourse.bass as bass
import concourse.mybir as mybir
from concourse.bass_utils import run_bass_kernel_spmd

F32 = mybir.dt.float32
BF16 = mybir.dt.bfloat16
ALU = mybir.AluOpType
AF = mybir.ActivationFunctionType
AX = mybir.AxisListType

D = 1024
NPB = 20
SB = 4
NST = NPB // SB
NBMAX = SB + 1
DFF = 2816
NFC = 44
EPS = 1e-6
NEG = -30000.0
PROC_START = [0, 1920, 3840, 5632]
OWN_START = [0, 2560, 4480, 6400]
OWN_END = [2560, 4480, 6400, 8192]
NSLOT = 5
PREFETCH = 4


class Buf:
    __slots__ = ("name", "w", "r")

    def __init__(self, name=""):
        self.name = name
        self.w = None
        self.r = []


class Tile:
    __slots__ = ("t", "b")

    def __init__(self, t, b=None):
        self.t = t
        self.b = b if b is not None else Buf()


class Entry:
    __slots__ = ("waits", "fn", "signal", "dma_sem")

    def __init__(self, waits, fn):
        self.waits = waits
        self.fn = fn
        self.signal = False
        self.dma_sem = None


COMPUTE = ("pe", "act", "dve", "pool")
ALLENG = ("pe", "act", "dve", "pool", "sp")


class Sched:
    def __init__(self, nc, stack, dma_ring=8):
        self.nc = nc
        self.streams = {k: [] for k in ALLENG}
        self.esem = {k: stack.enter_context(nc.semaphore("prog_" + k)) for k in COMPUTE}
        self.base = {k: 0 for k in COMPUTE}
        self.dsem = {}
        self.dcount = {}
        self.dlast = {}
        for q in ("sp", "pool", "act"):
            self.dsem[q] = [stack.enter_context(nc.semaphore(f"dma_{q}_{i}")) for i in range(dma_ring)]
            self.dcount[q] = 0
            self.dlast[q] = [None] * dma_ring
        self.seen = {k: {} for k in ALLENG}
        self.ring = dma_ring
        self.phase = 0
        self.hold = []
        self.all_out = []

    def _collect(self, eng, reads, writes):
        evs = []
        for b in reads:
            if b.w is not None:
                evs.append(b.w)
        for b in writes:
            if b.w is not None:
                evs.append(b.w)
            evs.extend(b.r)
        return self._reduce(eng, evs)

    def _reduce(self, eng, evs):
        best = {}
        for ev in evs:
            if ev is None:
                continue
            if ev[0] == "e":
                _, src, idx, ph = ev
                if ph != self.phase:
                    continue
                if src == eng and eng == "pe":
                    continue
                key = ("e", src)
                if key not in best or best[key][2] < idx:
                    best[key] = ev
            else:
                _, q, slot, val = ev
                key = ("d", q, slot)
                if key not in best or best[key][3] < val:
                    best[key] = ev
        waits = []
        for key, ev in best.items():
            v = ev[2] if ev[0] == "e" else ev[3]
            pk = (self.phase,) + key if ev[0] == "e" else key
            if self.seen[eng].get(pk, -1) >= v:
                continue
            self.seen[eng][pk] = v
            if ev[0] == "e":
                self.streams[ev[1]][ev[2]].signal = True
            waits.append(ev)
        return waits

    def op(self, eng, fn, reads=(), writes=()):
        waits = self._collect(eng, reads, writes)
        st = self.streams[eng]
        st.append(Entry(waits, fn))
        ev = ("e", eng, len(st) - 1, self.phase)
        for b in reads:
            b.r.append(ev)
        for b in writes:
            b.w = ev
            b.r = []
        return ev

    def dma(self, q, fn, reads=(), writes=(), hold=True, out=False):
        evs = []
        for b in reads:
            if b.w is not None:
                evs.append(b.w)
        for b in writes:
            if b.w is not None:
                evs.append(b.w)
            evs.extend(b.r)
        i = self.dcount[q]
        self.dcount[q] += 1
        slot = i % self.ring
        val = 16 * (i // self.ring + 1)
        if self.dlast[q][slot] is not None:
            evs.append(self.dlast[q][slot])
        waits = self._reduce(q, evs)
        ent = Entry(waits, fn)
        ent.dma_sem = self.dsem[q][slot]
        self.streams[q].append(ent)
        ev = ("d", q, slot, val)
        self.dlast[q][slot] = ev
        for b in reads:
            b.r.append(ev)
        for b in writes:
            b.w = ev
            b.r = []
        if hold:
            self.hold.append(ev)
        if out:
            self.all_out.append(ev)
        return ev

    def flush(self, final=False):
        nc = self.nc
        lasts = []
        for k in COMPUTE:
            st = self.streams[k]
            idx = None
            for i in range(len(st) - 1, -1, -1):
                if st[i].fn is not None and st[i].dma_sem is None:
                    idx = i
                    break
            if idx is not None:
                lasts.append(("e", k, idx, self.phase))
        extra = list(self.hold)
        if final:
            extra += self.all_out
            for q in self.dlast:
                extra += [ev for ev in self.dlast[q] if ev is not None]
        for k in ALLENG:
            waits = self._reduce(k, lasts + extra)
            self.streams[k].append(Entry(waits, None))
        self.hold = []
        counts = {}
        for k in COMPUTE:
            c = self.base[k]
            arr = []
            for ent in self.streams[k]:
                if ent.signal and ent.dma_sem is None and ent.fn is not None:
                    c += 1
                arr.append(c)
            counts[k] = arr

        def resolve(ev):
            if ev[0] == "e":
                return self.esem[ev[1]], counts[ev[1]][ev[2]]
            return self.dsem[ev[1]][ev[2]], ev[3]

        def replay(k, e):
            for ent in self.streams[k]:
                for ev in ent.waits:
                    s, v = resolve(ev)
                    e.wait_ge(s, v)
                if ent.fn is None:
                    continue
                ins = ent.fn(e)
                if ent.dma_sem is not None:
                    ins.then_inc(ent.dma_sem, 16)
                elif ent.signal:
                    ins.then_inc(self.esem[k], 1)

        with nc.Block() as block:
            @block.tensor
            def _(e):
                replay("pe", e)

            @block.scalar
            def _(e):
                replay("act", e)

            @block.vector
            def _(e):
                replay("dve", e)

            @block.gpsimd
            def _(e):
                replay("pool", e)

            @block.sync
            def _(e):
                replay("sp", e)

        for k in COMPUTE:
            if counts[k]:
                self.base[k] = counts[k][-1]
        self.streams = {k: [] for k in ALLENG}
        self.phase += 1


_IN_SPECS = [
    ("xp", [NPB * 128, D]), ("xs", [128, D]), ("c17", [17, D]),
    ("ck", [2, 16, 128, 256]), ("cv", [2, 16, 128, 256]), ("sconv", [4, 32, 2 * DFF]),
    ("wt_ada", [4, 12, 128, 4096]), ("b_ada", [4, 6 * D]), ("rows8", [8, D]), ("normf", [1, D]),
    ("wt_qkv", [2, 3, 128, 4096]), ("b_qkv", [2, 1536]), ("sinkp", [2, 16]), ("wt_o", [2, 2, 128, 4096]),
    ("wt_in", [2, 8, 128, 4096]), ("b_in", [2, 4096]), ("lnrows", [4, 2048]),
    ("wspT", [2, 4, 128, 128]), ("wspST", [2, 4, 128, 128]), ("bspP", [128, 8]), ("bspS", [128, 8]),
    ("wt_out", [2, 4, 128, 4096]), ("wt_up", [4, 11, 128, 4096]), ("convrows", [16, 2 * DFF]), ("wt_down", [4, 6, 128, 4096]),
    ("cossin", [(NPB + 1) * 128, 16]), ("ident", [128, 128]), ("maskP", [128, 256]), ("maskP0", [128, 256]),
    ("maskS", [128, 256]), ("trilT", [128, 128]), ("bmask", [128, 16 * 128]),
]
_OUT_SPECS = [
    ("yp", [NPB * 128, D]), ("ys", [128, D]), ("kp", [2, 128, 256]), ("vp", [2, 128, 256]),
    ("convp", [4, 2, 2 * DFF]), ("ks", [2, 16, 128, 256]), ("vs", [2, 16, 128, 256]),
    ("convs", [4, 32, 2 * DFF]), ("sguv", [2, 128, 2048]),
]


def weight_sequence():
    seq = []
    for l in range(4):
        for nt in range(12):
            seq.append(("ada", l, nt))
    for st in range(NST):
        for l in range(4):
            if l % 2 == 0:
                for nt in range(3):
                    seq.append(("qkv", l // 2, nt))
                for nt in range(2):
                    seq.append(("wo", l // 2, nt))
            else:
                for nt in range(4):
                    seq.append(("win", l // 2, 4 + nt))
                for g in range(4):
                    seq.append(("win", l // 2, g))
                    seq.append(("win", l // 2, 4 + g))
                for kh in range(2):
                    for nt in range(2):
                        seq.append(("wout", l // 2, kh * 2 + nt))
            for jj in range(11):
                seq.append(("wup", l, jj))
            for kg in range(3):
                for nt in range(2):
                    seq.append(("wdown", l, kg * 2 + nt))
    return seq


class _Stop(Exception):
    pass


def build_program(dbg=None):
    nc = bass.Bass("TRN2", target_bir_lowering=False)
    I = {n: nc.dram_tensor(n, s, F32, kind="ExternalInput").ap() for n, s in _IN_SPECS}
    O = {n: nc.dram_tensor(n, s, F32, kind="ExternalOutput").ap() for n, s in _OUT_SPECS}
    if dbg is not None:
        O["dbg"] = nc.dram_tensor("dbg", [128, 16384], F32, kind="ExternalOutput").ap()
    try:
        _build_body(nc, I, O, dbg)
    except _Stop:
        pass
    return nc


def _build_body(nc, I, O, dbg):

    with ExitStack() as gs:
        S = Sched(nc, gs)

        uid = [0]

        def uname(name):
            uid[0] += 1
            return f"s{uid[0]}_{name}"

        def SBT(stack, name, shape, dt):
            return Tile(stack.enter_context(nc.sbuf_tensor(uname(name), shape, dt)), Buf(name))

        dbg_off = [0]

        def dump(ap, ncols, buf, bf=False):
            if bf:
                with ExitStack() as dst_:
                    t = SBT(dst_, "dbgt", [128, ncols], F32)
                    S.op("dve", lambda e: e.tensor_copy(out=t.t[:], in_=ap), reads=[buf], writes=[t.b])
                    o = dbg_off[0]
                    S.dma("sp", lambda e: e.dma_start(out=O["dbg"][:, o:o + ncols], in_=t.t[:]), reads=[t.b], out=True)
                    S.flush()
            else:
                o = dbg_off[0]
                S.dma("sp", lambda e: e.dma_start(out=O["dbg"][:, o:o + ncols], in_=ap), reads=[buf], out=True)
            dbg_off[0] += ncols

        def stop():
            S.flush(final=True)
            print("SEM COUNTS", S.base, S.dcount)
            raise _Stop()

        x = [SBT(gs, f"x{i}", [128, D], F32) for i in range(NBMAX)]
        hT = gs.enter_context(nc.sbuf_tensor("s_hT", [128, 8, NBMAX * 128], BF16))
        hTb = [Buf(f"hT{i}") for i in range(NBMAX)]
        slots = [SBT(gs, f"ws{i}", [128, 8, 512], BF16) for i in range(NSLOT)]
        brow = [SBT(gs, f"brow{i}", [1, 512], BF16) for i in range(NSLOT)]
        pb = [Tile(gs.enter_context(nc.psum_tensor(f"pb{i}", [128, 512], F32)), Buf(f"pb{i}")) for i in range(8)]
        identf = SBT(gs, "identf", [128, 128], F32)
        identb = SBT(gs, "identb", [128, 128], BF16)
        ones = SBT(gs, "ones", [1, 128], BF16)
        maskP = SBT(gs, "maskP", [128, 256], F32)
        maskP0 = SBT(gs, "maskP0", [128, 256], F32)
        maskS = SBT(gs, "maskS", [128, 256], F32)
        wm = SBT(gs, "wm", [128, 2, 2, 4, 128], BF16)
        bsp = SBT(gs, "bsp", [128, 2, 8], F32)
        sinkB = SBT(gs, "sinkB", [128, 2, 16], F32)
        cs = SBT(gs, "cs", [128, NPB + 1, 16], F32)
        modT = SBT(gs, "modT", [128, 4, 48, 17], F32)
        Amod = SBT(gs, "Amod", [128, 4, 2, 8, 17], F32)
        cwT = SBT(gs, "cwT", [128, NFC, 16], F32)
        stP = SBT(gs, "stP", [128, 4, NFC, 2], F32)
        stPb = [[Buf(f"stP{l_}_{c_}") for c_ in range(NFC)] for l_ in range(4)]
        stP_all = [b_ for row in stPb for b_ in row]
        normfB = SBT(gs, "normfB", [128, D], F32)
        G = [[SBT(gs, f"G{k}{w}", [128, D], F32) for w in range(2)] for k in range(2)]
        kTp = [SBT(gs, f"kTp{a}", [128, 2, 128], BF16) for a in range(2)]
        Vpp = [SBT(gs, f"Vpp{a}", [128, 4, 128], BF16) for a in range(2)]

        wseq = weight_sequence()
        wstate = {"issued": 0, "next": 0}

        def w_issue(i):
            kind, l, idx = wseq[i]
            sl = slots[i % NSLOT]
            br = brow[i % NSLOT]
            dst = sl.t

            def q(dst_ap, src_ap, tile=sl):
                S.dma("pool", lambda e, d=dst_ap, s=src_ap: e.dma_start(out=d, in_=s), writes=[tile.b], hold=False)

            def tile(name, nk=8):
                q(dst[:, 0:nk, :], I[name][l][idx][:, 0:nk * 512].rearrange("p (k n) -> p k n", n=512))

            if kind == "ada":
                tile("wt_ada")
                q(br.t[:, :], I["b_ada"][l:l + 1, idx * 512:(idx + 1) * 512], br)
            elif kind == "qkv":
                tile("wt_qkv")
                q(br.t[:, :], I["b_qkv"][l:l + 1, idx * 512:(idx + 1) * 512], br)
            elif kind == "wo":
                tile("wt_o")
            elif kind == "win":
                tile("wt_in")
                q(br.t[:, :], I["b_in"][l:l + 1, idx * 512:(idx + 1) * 512], br)
            elif kind == "wout":
                tile("wt_out")
            elif kind == "wup":
                tile("wt_up")
            elif kind == "wdown":
                tile("wt_down", 8 if idx // 2 < 2 else 6)

        def w_next(kind, l, idx):
            i = wstate["next"]
            assert wseq[i] == (kind, l, idx), (wseq[i], kind, l, idx)
            while wstate["issued"] < min(len(wseq), i + PREFETCH + 1):
                w_issue(wstate["issued"])
                wstate["issued"] += 1
            wstate["next"] += 1
            return slots[i % NSLOT], brow[i % NSLOT]

        def mm(out_ap, lhsT, rhs, start, stop, reads, out_buf):
            return S.op("pe", lambda e: e.matmul(out=out_ap, lhsT=lhsT, rhs=rhs, start=start, stop=stop),
                        reads=reads, writes=[out_buf])

        def tr(out_ap, in_ap, ident_ap, reads, out_buf):
            return S.op("pe", lambda e: e.transpose(out=out_ap, in_=in_ap, identity=ident_ap), reads=reads, writes=[out_buf])

        def bfview(p):
            return p.t[:].bitcast(BF16).rearrange("p (j t) -> p j t", t=128)

        with ExitStack() as ph:
            def ld(tile, src, q="sp"):
                S.dma(q, lambda e: e.dma_start(out=tile.t[:], in_=src), writes=[tile.b])

            ld(identf, I["ident"])
            ld(maskP, I["maskP"])
            ld(maskP0, I["maskP0"])
            ld(maskS, I["maskS"])
            S.dma("sp", lambda e: e.dma_start(out=cs.t[:], in_=I["cossin"].rearrange("(b p) c -> p b c", p=128)), writes=[cs.b])
            S.dma("sp", lambda e: e.dma_start(out=bsp.t[:, 0, :], in_=I["bspP"]), writes=[bsp.b])
            S.dma("sp", lambda e: e.dma_start(out=bsp.t[:, 1, :], in_=I["bspS"]), writes=[bsp.b])
            S.dma("sp", lambda e: e.dma_start(out=sinkB.t[:].rearrange("p a h -> p (a h)"),
                                              in_=I["sinkp"].rearrange("a h -> (a h)").partition_broadcast(128)), writes=[sinkB.b])
            S.dma("sp", lambda e: e.dma_start(out=normfB.t[:], in_=I["normf"][0].partition_broadcast(128)), writes=[normfB.b])
            S.op("dve", lambda e: e.memset(ones.t[:], 1.0), writes=[ones.b])
            S.op("dve", lambda e: e.tensor_copy(out=identb.t[:], in_=identf.t[:]), reads=[identf.b], writes=[identb.b])
            S.op("dve", lambda e: e.memset(stP.t[:], 0.0), writes=stP_all)
            for a in range(2):
                S.op("dve", lambda e, a=a: e.memset(Vpp[a].t[:], 0.0), writes=[Vpp[a].b])
                S.op("dve", lambda e, a=a: e.memset(kTp[a].t[:], 0.0), writes=[kTp[a].b])
            tril = SBT(ph, "tril", [128, 128], F32)
            ld(tril, I["trilT"])
            wsp_f = SBT(ph, "wsp_f", [128, 2, 2, 4, 128], F32)
            for kind, nm in enumerate(("wspT", "wspST")):
                S.dma("sp", lambda e, kind=kind, nm=nm: e.dma_start(out=wsp_f.t[:, kind], in_=I[nm].rearrange("l g s t -> s l g t")),
                      writes=[wsp_f.b])
            S.op("dve", lambda e: e.tensor_tensor(out=wm.t[:].rearrange("p a l g t -> p (a l g) t"),
                                                  in0=wsp_f.t[:].rearrange("p a l g t -> p (a l g) t"),
                                                  in1=tril.t[:].unsqueeze(1).to_broadcast([128, 16, 128]), op=ALU.mult),
                 reads=[wsp_f.b, tril.b], writes=[wm.b])
            if dbg == "s1":
                dump(wm.t[:].rearrange("p a l g t -> p (a l g t)"), 2048, wm.b, bf=True)
                dump(cs.t[:].rearrange("p b c -> p (b c)"), 336, cs.b)
                dump(sinkB.t[:].rearrange("p a h -> p (a h)"), 32, sinkB.b)
                stop()
            rows8 = SBT(ph, "rows8", [8, D], F32)
            ld(rows8, I["rows8"])
            nrmT = SBT(ph, "nrmT", [128, 8, 8], F32)
            for j in range(8):
                mm(pb[0].t[:, j * 8:(j + 1) * 8], rows8.t[:, j * 128:(j + 1) * 128], identf.t[0:8, 0:8], True, True, [rows8.b, identf.b], pb[0].b)
            S.op("act", lambda e: e.activation(out=nrmT.t[:].rearrange("p j r -> p (j r)"), in_=pb[0].t[:, 0:64], func=AF.Copy),
                 reads=[pb[0].b], writes=[nrmT.b])
            crow = SBT(ph, "crow", [16, 2 * DFF], F32)
            ld(crow, I["convrows"])
            for c0 in range(0, NFC, 22):
                for j in range(22):
                    mm(pb[1].t[:, j * 16:(j + 1) * 16], crow.t[:, (c0 + j) * 128:(c0 + j + 1) * 128], identf.t[0:16, 0:16], True, True,
                       [crow.b, identf.b], pb[1].b)
                S.op("act", lambda e, c0=c0: e.activation(out=cwT.t[:, c0:c0 + 22, :].rearrange("p j r -> p (j r)"),
                                                          in_=pb[1].t[:, 0:352], func=AF.Copy), reads=[pb[1].b], writes=[cwT.b])
            if dbg == "s2":
                dump(cwT.t[:].rearrange("p c r -> p (c r)"), 704, cwT.b)
                dump(nrmT.t[:].rearrange("p j r -> p (j r)"), 64, nrmT.b)
                stop()
            c17 = SBT(ph, "c17", [17, D], F32)
            ld(c17, I["c17"])
            c17b = SBT(ph, "c17b", [17, D], BF16)
            S.op("act", lambda e: e.activation(out=c17b.t[:], in_=c17.t[:], func=AF.Silu), reads=[c17.b], writes=[c17b.b])
            sT = SBT(ph, "sT", [128, 8, 17], BF16)
            pv = pb[2].t[:]
            for k in range(8):
                mm(pv[:, k * 32:k * 32 + 17], c17b.t[:, k * 128:(k + 1) * 128], identb.t[0:17, 0:17], True, True, [c17b.b, identb.b], pb[2].b)
            S.op("act", lambda e: e.activation(out=sT.t[:], in_=pv[:, 0:256].rearrange("p (k c) -> p k c", c=32)[:, :, 0:17], func=AF.Copy),
                 reads=[pb[2].b], writes=[sT.b])
            if dbg == "s3":
                dump(sT.t[:].rearrange("p k c -> p (k c)"), 136, sT.b, bf=True)
                stop()
            for l in range(4):
                for nt in range(12):
                    sl, br = w_next("ada", l, nt)
                    bank = pb[3 + (nt % 2)]
                    for fc in range(4):
                        o = bank.t[:, fc * 17:(fc + 1) * 17]
                        for k in range(8):
                            mm(o, sl.t[:, k, fc * 128:(fc + 1) * 128], sT.t[:, k, :], k == 0, False, [sl.b, sT.b], bank.b)
                        mm(o, br.t[:, fc * 128:(fc + 1) * 128], ones.t[:, 0:17], False, True, [br.b, ones.b], bank.b)
                    S.op("dve", lambda e, l=l, nt=nt, bank=bank: e.tensor_copy(
                        out=modT.t[:, l, nt * 4:(nt + 1) * 4, :].rearrange("p c s -> p (c s)"), in_=bank.t[:, 0:68]),
                        reads=[bank.b], writes=[modT.b])
            for l in range(4):
                for w in range(2):
                    S.op("dve", lambda e, l=l, w=w: e.tensor_scalar(out=Amod.t[:, l, w], in0=modT.t[:, l, 8 + 24 * w:16 + 24 * w, :],
                                                                    scalar1=1.0, scalar2=None, op0=ALU.add),
                         reads=[modT.b], writes=[Amod.b])
                    S.op("dve", lambda e, l=l, w=w: e.tensor_tensor(out=Amod.t[:, l, w], in0=Amod.t[:, l, w],
                                                                    in1=nrmT.t[:, :, 4 * w + l:4 * w + l + 1].to_broadcast([128, 8, 17]),
                                                                    op=ALU.mult),
                         reads=[nrmT.b, Amod.b], writes=[Amod.b])
            if dbg == "setup":
                dump(modT.t[:].rearrange("p l j s -> p (l j s)"), 3264, modT.b)
                dump(Amod.t[:].rearrange("p l w j s -> p (l w j s)"), 1088, Amod.b)
                dump(cwT.t[:].rearrange("p c r -> p (c r)"), 704, cwT.b)
                dump(wm.t[:].rearrange("p a l g t -> p (a l g t)"), 2048, wm.b, bf=True)
                stop()
            S.flush()

        def build_gates(ph, l, kinds):
            gbs = [SBT(ph, f"gb{i}", [128, 128], F32) for i in range(4)]
            gbc = [0]
            for kind in kinds:
                for w in range(2):
                    for half in range(2):
                        bank = pb[half]
                        for jj in range(4):
                            j = half * 4 + jj
                            gb = gbs[gbc[0] % 4]
                            gbc[0] += 1
                            col = modT.t[:, l, 16 + 24 * w + j, :]
                            if kind == 0:
                                src = col[:, 0:1].to_broadcast([128, 128])
                                dstv = gb.t[:]
                            else:
                                src = col[:, 1:17].unsqueeze(2).to_broadcast([128, 16, 8])
                                dstv = gb.t[:].rearrange("p (b t) -> p b t", t=8)
                            S.op("dve", lambda e, d=dstv, s=src: e.tensor_copy(out=d, in_=s), reads=[modT.b], writes=[gb.b])
                            mm(bank.t[:, jj * 128:(jj + 1) * 128], gb.t[:], identf.t[:], True, True, [gb.b, identf.b], bank.b)
                        S.op("act", lambda e, kind=kind, w=w, half=half, bank=bank: e.activation(
                            out=G[kind][w].t[:, half * 512:(half + 1) * 512], in_=bank.t[:], func=AF.Copy),
                            reads=[bank.b], writes=[G[kind][w].b])

        def norm_phase(ph, blocks, l, w):
            ssl = [SBT(ph, f"ss{i}", [128, 4], F32) for i in range(NBMAX)]
            junk = SBT(ph, "junk", [128, D], BF16)
            xn = [SBT(ph, f"xn{i}", [128, D], BF16) for i in range(2)]
            tmpf = [SBT(ph, f"tmpf{i}", [128, 8, 128], F32) for i in range(2)]
            for bi, kind in enumerate(blocks):
                S.op("act", lambda e, bi=bi: e.activation(out=junk.t[:], in_=x[bi].t[:], func=AF.Square, accum_out=ssl[bi].t[:, 0:1]),
                     reads=[x[bi].b], writes=[junk.b, ssl[bi].b])
                S.op("act", lambda e, bi=bi: e.activation(out=ssl[bi].t[:, 1:2], in_=ssl[bi].t[:, 0:1], func=AF.Sqrt,
                                                          scale=1.0 / D, bias=EPS), reads=[ssl[bi].b], writes=[ssl[bi].b])
                S.op("dve", lambda e, bi=bi: e.reciprocal(out=ssl[bi].t[:, 2:3], in_=ssl[bi].t[:, 1:2]),
                     reads=[ssl[bi].b], writes=[ssl[bi].b])
                xt = xn[bi % 2]
                S.op("act", lambda e, bi=bi, xt=xt: e.activation(out=xt.t[:], in_=x[bi].t[:], func=AF.Copy, scale=ssl[bi].t[:, 2:3]),
                     reads=[x[bi].b, ssl[bi].b], writes=[xt.b])
                bank = pb[6 + bi % 2]
                bv = bfview(bank)
                for j in range(8):
                    tr(bv[:, j, :], xt.t[:, j * 128:(j + 1) * 128], identb.t[:], [xt.b, identb.b], bank.b)
                tf = tmpf[bi % 2]
                hv = hT[:, :, bi * 128:(bi + 1) * 128]
                if kind == 0:
                    a_ap = Amod.t[:, l, w, :, 0:1].to_broadcast([128, 8, 128])
                    s_ap = modT.t[:, l, 24 * w:24 * w + 8, 0:1].to_broadcast([128, 8, 128])
                    S.op("dve", lambda e, bv=bv, tf=tf, a_ap=a_ap: e.tensor_tensor(out=tf.t[:], in0=bv, in1=a_ap, op=ALU.mult),
                         reads=[bank.b, Amod.b], writes=[tf.b])
                    S.op("pool", lambda e, hv=hv, tf=tf, s_ap=s_ap: e.tensor_tensor(out=hv, in0=tf.t[:], in1=s_ap, op=ALU.add),
                         reads=[tf.b, modT.b], writes=[hTb[bi]])
                else:
                    for j in range(8):
                        a_ap = Amod.t[:, l, w, j, 1:17].unsqueeze(2).to_broadcast([128, 16, 8])
                        s_ap = modT.t[:, l, 24 * w + j, 1:17].unsqueeze(2).to_broadcast([128, 16, 8])
                        S.op("dve", lambda e, j=j, bv=bv, tf=tf, a_ap=a_ap: e.tensor_tensor(
                            out=tf.t[:, j, :].rearrange("p (b t) -> p b t", t=8), in0=bv[:, j, :].rearrange("p (b t) -> p b t", t=8),
                            in1=a_ap, op=ALU.mult), reads=[bank.b, Amod.b], writes=[tf.b])
                        S.op("dve", lambda e, j=j, hv=hv, tf=tf, s_ap=s_ap: e.tensor_tensor(
                            out=hv[:, j, :].rearrange("p (b t) -> p b t", t=8), in0=tf.t[:, j, :].rearrange("p (b t) -> p b t", t=8),
                            in1=s_ap, op=ALU.add), reads=[tf.b, modT.b], writes=[hTb[bi]])

        def resid_add(bi, kind, w, nt, bank, eng2="dve"):
            xs_ = x[bi].t[:, nt * 512:(nt + 1) * 512]
            g_ = G[kind][w].t[:, nt * 512:(nt + 1) * 512]
            tmp = resid_tmp[resid_ctr[0] % 3]
            resid_ctr[0] += 1
            S.op("dve", lambda e: e.tensor_tensor(out=tmp.t[:], in0=bank.t[:], in1=g_, op=ALU.mult),
                 reads=[bank.b, G[kind][w].b], writes=[tmp.b])
            S.op(eng2, lambda e: e.tensor_tensor(out=xs_, in0=xs_, in1=tmp.t[:], op=ALU.add), reads=[tmp.b, x[bi].b], writes=[x[bi].b])

        resid_tmp = [SBT(gs, f"rtmp{i}", [128, 512], F32) for i in range(3)]
        resid_ctr = [0]
        mmctr = [0]

        def dense_tm(blocks, sl, br, kchunks, act_chunk, evac, banks=(0, 1, 2)):
            pend = []
            for bi, kind in enumerate(blocks):
                bank = pb[banks[mmctr[0] % len(banks)]]
                mmctr[0] += 1
                n = len(kchunks)
                for i, kc in enumerate(kchunks):
                    lhsT, rb = act_chunk(bi, kc)
                    mm(bank.t[:], lhsT, sl.t[:, i, :], i == 0, (i == n - 1) and br is None, [sl.b, rb], bank.b)
                if br is not None:
                    mm(bank.t[:], ones.t[:], br.t[:], False, True, [ones.b, br.b], bank.b)
                r = evac(bi, kind, bank)
                if callable(r):
                    pend.append(r)
                    if len(pend) > 2:
                        pend.pop(0)()
            while pend:
                pend.pop(0)()

        def h_chunk(bi, kc):
            return hT[:, kc, bi * 128:(bi + 1) * 128], hTb[bi]

        def attn_layer(st, blocks, a, l):
            gblk0 = st * SB
            nb = len(blocks)
            has_s = 1 in blocks
            npb = sum(1 for k in blocks if k == 0)
            if has_s:
                with ExitStack() as ph:
                    build_gates(ph, l, sorted(set(blocks)))
                    norm_phase(ph, blocks, l, 0)
                    S.flush()
            with ExitStack() as pst:
                OT = [SBT(pst, f"OT{i}", [128, 8, 128], BF16) for i in range(nb)]
                qkb = [None] * nb
                Vp = [None] * nb
                kT = [None] * nb
                if has_s:
                    qkb[npb] = SBT(pst, "qkbS", [128, 1280], BF16)
                    Vp[npb] = SBT(pst, "VpS", [128, 4, 128], BF16)
                    kT[npb] = SBT(pst, "kTS", [128, 2, 128], BF16)
                smp = {}

                def temps(ph, n, nq=None):
                    return dict(
                        qT=[SBT(ph, f"qT{i}", [128, 8, 128], BF16) for i in range(nq or n)],
                        sm=[SBT(ph, f"sm{i}", [128, 4, 256], F32) for i in range(min(n, 2))],
                        pp=[SBT(ph, f"pp{i}", [128, 4, 256], BF16) for i in range(n)],
                        pT=[SBT(ph, f"pT{i}", [128, 8, 128], BF16) for i in range(min(n, 2))],
                        stt=[SBT(ph, f"stt{i}", [128, 32], F32) for i in range(n)])

                SCORE_SETS = ((pb[5], pb[6]), (pb[1], pb[2]), (pb[3], pb[4]))
                pv_half = [Buf("pv0"), Buf("pv1")]
                nsets = [2]

                def attn_pre(bi, kind, T):
                    n = len(T["qT"])
                    bq = pb[7]
                    bqv = bfview(bq)
                    for c in range(8):
                        tr(bqv[:, c, :], qkb[bi].t[:, c * 128:(c + 1) * 128], identb.t[:], [qkb[bi].b, identb.b], bq.b)
                    qt = T["qT"][bi % n]
                    S.op("act", lambda e: e.activation(out=qt.t[:], in_=bqv, func=AF.Copy), reads=[bq.b], writes=[qt.b])
                    bk = pb[7]
                    bkv = bfview(bk)
                    for kc in range(2):
                        tr(bkv[:, kc, :], qkb[bi].t[:, 1024 + kc * 128:1024 + (kc + 1) * 128], identb.t[:], [qkb[bi].b, identb.b], bk.b)
                    S.op("act", lambda e: e.activation(out=kT[bi].t[:], in_=bkv[:, 0:2, :], func=AF.Copy), reads=[bk.b], writes=[kT[bi].b])

                def attn_sa(bi, kind, gi, k, T, part):
                    qt = T["qT"][bi % len(T["qT"])]
                    n = len(T["sm"])
                    first = (kind == 0 and gblk0 + bi == 0)
                    if kind == 0:
                        kprev = kT[bi - 1] if bi > 0 else kTp[a]
                        msk = maskP0 if first else maskP
                    else:
                        msk = maskS
                        QmT, KTs, bmk = smp["QmT"], smp["KTs"], smp["bmk"]
                    kc = gi // 2
                    bA, bB = SCORE_SETS[k % nsets[0]]
                    if kind == 1 and part == 0:
                        for ci in range(2):
                            c = 2 * gi + ci
                            S.op("dve", lambda e, ci=ci, c=c: e.tensor_tensor(
                                out=QmT[ci].t[:], in0=qt.t[:, c, :].unsqueeze(1).to_broadcast([128, 16, 128]), in1=bmk.t[:], op=ALU.mult),
                                reads=[qt.b, bmk.b], writes=[QmT[ci].b])
                    for hf, bank in ((0, bA), (1, bB)):
                        if part != 0:
                            break
                        ps = slice(64 * hf, 64 * hf + 64)
                        for ci in range(2):
                            c = 2 * gi + ci
                            o_prev = bank.t[:, ci * 256:ci * 256 + 128]
                            o_own = bank.t[:, ci * 256 + 128:ci * 256 + 256]
                            if kind == 0:
                                mm(o_prev, qt.t[ps, c, :], kprev.t[ps, kc, :], True, True, [qt.b, kprev.b], bank.b)
                            else:
                                for b in range(16):
                                    mm(o_prev, QmT[ci].t[ps, b, :], KTs.t[ps, kc, b, :], b == 0, b == 15, [QmT[ci].b, KTs.b], bank.b)
                            mm(o_own, qt.t[ps, c, :], kT[bi].t[ps, kc, :], True, True, [qt.b, kT[bi].b], bank.b)
                    if part == 0:
                        return
                    s_ = T["sm"][k % len(T["sm"])]
                    p_ = T["pp"][k % len(T["pp"])]
                    t8 = T["stt"][k % len(T["stt"])]
                    for hf, bank in ((0, bA), (1, bB)):
                        S.op("dve", lambda e, hf=hf, bank=bank: e.scalar_tensor_tensor(
                            out=s_.t[:, 2 * hf:2 * hf + 2, :], in0=bank.t[:].rearrange("p (s k) -> p s k", k=256), scalar=0.125,
                            in1=msk.t[:].unsqueeze(1).to_broadcast([128, 2, 256]), op0=ALU.mult, op1=ALU.add),
                            reads=[bank.b, msk.b], writes=[s_.b])
                    sk = sinkB.t[:, a, 4 * gi:4 * gi + 4]
                    S.op("dve", lambda e: e.tensor_reduce(out=t8.t[:, 0:4], in_=s_.t[:], axis=AX.X, op=ALU.max), reads=[s_.b], writes=[t8.b])
                    S.op("dve", lambda e: e.tensor_tensor(out=t8.t[:, 0:4], in0=t8.t[:, 0:4], in1=sk, op=ALU.max),
                         reads=[t8.b, sinkB.b], writes=[t8.b])
                    S.op("dve", lambda e: e.tensor_scalar(out=t8.t[:, 4:8], in0=t8.t[:, 0:4], scalar1=-1.0, scalar2=None, op0=ALU.mult),
                         reads=[t8.b], writes=[t8.b])
                    S.op("dve", lambda e: e.tensor_tensor(out=t8.t[:, 12:16], in0=t8.t[:, 4:8], in1=sk, op=ALU.add),
                         reads=[t8.b, sinkB.b], writes=[t8.b])
                    for h4 in range(4):
                        S.op("act", lambda e, h4=h4: e.activation(
                            out=p_.t[:, h4, :], in_=s_.t[:, h4, :], func=AF.Exp, bias=t8.t[:, 4 + h4:5 + h4], scale=1.0,
                            accum_out=t8.t[:, 8 + h4:9 + h4]), reads=[s_.b, t8.b], writes=[p_.b, t8.b])
                    S.op("act", lambda e: e.activation(out=t8.t[:, 16:20], in_=t8.t[:, 12:16], func=AF.Exp), reads=[t8.b], writes=[t8.b])

                def attn_bpv(bi, kind, gi, k, T, part):
                    p_ = T["pp"][k % len(T["pp"])]
                    t8 = T["stt"][k % len(T["stt"])]
                    kc = gi // 2
                    vprev = None
                    if kind == 0:
                        vprev = Vp[bi - 1] if bi > 0 else Vpp[a]
                    else:
                        Vc = smp["Vc"]
                    if part == 0:
                        S.op("dve", lambda e: e.tensor_tensor(out=t8.t[:, 20:24], in0=t8.t[:, 8:12], in1=t8.t[:, 16:20], op=ALU.add),
                             reads=[t8.b], writes=[t8.b])
                        S.op("dve", lambda e: e.reciprocal(out=t8.t[:, 24:28], in_=t8.t[:, 20:24]), reads=[t8.b], writes=[t8.b])
                        S.op("dve", lambda e: e.tensor_tensor(out=p_.t[:], in0=p_.t[:],
                                                              in1=t8.t[:, 24:28].unsqueeze(2).to_broadcast([128, 4, 256]), op=ALU.mult),
                             reads=[t8.b, p_.b], writes=[p_.b])
                        return
                    bt = pb[7]
                    btv = bfview(bt)
                    for h4 in range(4):
                        for part in range(2):
                            tr(btv[:, h4 * 2 + part, :], p_.t[:, h4, part * 128:(part + 1) * 128], identb.t[:], [p_.b, identb.b], bt.b)
                    pt = T["pT"][k % len(T["pT"])]
                    S.op("act", lambda e: e.activation(out=pt.t[:], in_=btv, func=AF.Copy), reads=[bt.b], writes=[pt.b])
                    bo = Tile(pb[0].t, pv_half[k % 2])
                    c0 = (k % 2) * 256
                    for ci in range(2):
                        o = bo.t[:, c0 + ci * 128:c0 + (ci + 1) * 128]
                        seqm = []
                        for hf in range(2):
                            seqm.append((Vp[bi], 2 * kc + hf, (2 * hf + ci) * 2 + 1))
                        if kind == 0:
                            for hf in range(2):
                                seqm.append((vprev, 2 * kc + hf, (2 * hf + ci) * 2))
                        nn = len(seqm)
                        for i, (vt, g, pidx) in enumerate(seqm):
                            mm(o, vt.t[:, g, :], pt.t[:, pidx, :], i == 0, (i == nn - 1) and kind == 0, [vt.b, pt.b], bo.b)
                        if kind == 1:
                            for hf in range(2):
                                g = 2 * kc + hf
                                pidx = (2 * hf + ci) * 2
                                for b in range(16):
                                    mm(o[:, 8 * b:8 * b + 8], Vc.t[:, b, g, :], pt.t[:, pidx, 8 * b:8 * b + 8], False,
                                       (hf == 1 and b == 15), [Vc.b, pt.b], bo.b)
                    S.op("act", lambda e: e.activation(
                        out=OT[bi].t[:, 2 * gi:2 * gi + 2, :], in_=bo.t[:, c0:c0 + 256].rearrange("p (c t) -> p c t", t=128), func=AF.Copy),
                        reads=[bo.b], writes=[OT[bi].b])

                def attn_pipeline(bis, kind, T):
                    items = [(bi, gi) for bi in bis for gi in range(4)]
                    for hb in pv_half:
                        hb.w = pb[0].b.w
                        hb.r = list(pb[0].b.r)

                    def front(k, part=None):
                        bi, gi = items[k]
                        if part in (None, 0):
                            if gi == 0:
                                attn_pre(bi, kind, T)
                            attn_sa(bi, kind, gi, k, T, 0)
                        if part in (None, 1):
                            attn_sa(bi, kind, gi, k, T, 1)

                    skew = 2 if kind == 0 else 1
                    nsets[0] = skew + 1
                    for k0 in range(min(skew, len(items))):
                        front(k0)
                    for k, (bi, gi) in enumerate(items):
                        if k + skew < len(items):
                            front(k + skew, 0)
                        attn_bpv(bi, kind, gi, k, T, 0)
                        attn_bpv(bi, kind, gi, k, T, 1)
                        if k + skew < len(items):
                            front(k + skew, 1)
                        if kind == 0 and bi == SB - 1 and gi == 3:
                            S.op("dve", lambda e, bi=bi: e.tensor_copy(out=kTp[a].t[:], in_=kT[bi].t[:]), reads=[kT[bi].b], writes=[kTp[a].b])
                            S.op("dve", lambda e, bi=bi: e.tensor_copy(out=Vpp[a].t[:], in_=Vp[bi].t[:]), reads=[Vp[bi].b], writes=[Vpp[a].b])
                    pb[0].b.r = list(pb[0].b.r) + [ev for hb in pv_half for ev in ([hb.w] if hb.w else []) + hb.r]

                def phase_c():
                    for nt in range(2):
                        sl, _ = w_next("wo", a, nt)
                        dense_tm(blocks, sl, None, list(range(8)), lambda bi, kc: (OT[bi].t[:, kc, :], OT[bi].b),
                                 lambda bi, kind, bank, nt=nt: resid_add(bi, kind, 0, nt, bank), banks=(2, 3, 4))
                    S.flush()

                with ExitStack() as ph:
                    if not has_s:
                        build_gates(ph, l, sorted(set(blocks)))
                        norm_phase(ph, blocks, l, 0)
                    for i in range(npb):
                        qkb[i] = SBT(ph, f"qkb{i}", [128, 1280], BF16)
                        Vp[i] = SBT(ph, f"Vp{i}", [128, 4, 128], BF16)
                        kT[i] = SBT(ph, f"kT{i}", [128, 2, 128], BF16)
                    kvf = [SBT(ph, f"kvf{i}", [128, 512], F32) for i in range(2)]
                    rot = [SBT(ph, f"rot{i}", [128, 8, 16], F32) for i in range(2)]
                    rtm = [SBT(ph, f"rtm{i}", [128, 8, 8], F32) for i in range(2)]
                    TA = temps(ph, 3, 2)
                    qfs = [SBT(ph, f"qf{i}", [128, 512], F32) for i in range(2)]
                    qfc = [0]
                    for i in range(nb):
                        S.op("dve", lambda e, i=i: e.memset(Vp[i].t[:], 0.0), writes=[Vp[i].b])

                    def evac_qkv(nt):
                        def f(bi, kind, bank):
                            import os
                            ksub = int(os.environ.get("KSUB", "9"))
                            if ksub == 0:
                                S.op("act", lambda e: e.activation(out=qkb[bi].t[:, 0:512], in_=bank.t[:], func=AF.Copy),
                                     reads=[bank.b], writes=[qkb[bi].b])
                                return
                            gb = gblk0 + bi if kind == 0 else NPB
                            is_out = (kind == 1) or (gb == NPB - 1)
                            HF = 2 if nt < 2 else 1
                            W_ = HF * 256
                            pv3 = bank.t[:, 0:W_].rearrange("p (hf cl d) -> p hf cl d", hf=HF, d=64)
                            if nt < 2:
                                qv3 = qkb[bi].t[:, nt * 512:(nt + 1) * 512].rearrange("p (cl hf d) -> p hf cl d", hf=2, d=64)
                            else:
                                qv3 = qkb[bi].t[:, 1024:1280].rearrange("p (hf cl d) -> p hf cl d", hf=1, d=64)
                            S.op("act", lambda e: e.activation(out=qv3[:, :, :, 16:64], in_=pv3[:, :, :, 16:64], func=AF.Copy),
                                 reads=[bank.b], writes=[qkb[bi].b])
                            if ksub == 1:
                                return
                            nh = 4 * HF
                            r = rot[(bi + nt) % 2]
                            t_ = rtm[(bi + nt) % 2]
                            qf = qfs[qfc[0] % 2]
                            qfc[0] += 1
                            S.op("act", lambda e: e.activation(out=qf.t[:], in_=bank.t[:], func=AF.Copy), reads=[bank.b], writes=[qf.b])
                            src3 = qf.t[:, 0:nh * 64].rearrange("p (h d) -> p h d", d=64)
                            cosb = cs.t[:, gb, 0:8].unsqueeze(1).to_broadcast([128, nh, 8])
                            sinb = cs.t[:, gb, 8:16].unsqueeze(1).to_broadcast([128, nh, 8])
                            x1 = src3[:, :, 0:8]
                            x2 = src3[:, :, 8:16]
                            r1 = r.t[:, 0:nh, 0:8]
                            r2 = r.t[:, 0:nh, 8:16]
                            tv = t_.t[:, 0:nh, :]
                            rd = [qf.b, cs.b]
                            S.op("dve", lambda e: e.tensor_tensor(out=r1, in0=x1, in1=cosb, op=ALU.mult), reads=rd, writes=[r.b])
                            if ksub == 2:
                                return
                            S.op("dve", lambda e: e.tensor_tensor(out=tv, in0=x2, in1=sinb, op=ALU.mult), reads=rd, writes=[t_.b])
                            S.op("dve", lambda e: e.tensor_tensor(out=r1, in0=r1, in1=tv, op=ALU.subtract), reads=[t_.b, r.b], writes=[r.b])
                            S.op("dve", lambda e: e.tensor_tensor(out=r2, in0=x2, in1=cosb, op=ALU.mult), reads=rd + [r.b], writes=[r.b])
                            S.op("dve", lambda e: e.tensor_tensor(out=tv, in0=x1, in1=sinb, op=ALU.mult), reads=rd + [r.b], writes=[t_.b])
                            S.op("dve", lambda e: e.tensor_tensor(out=r2, in0=r2, in1=tv, op=ALU.add), reads=[t_.b, r.b], writes=[r.b])
                            if ksub == 3:
                                return
                            if nt < 2:
                                for hf in range(2):
                                    S.op("dve", lambda e, hf=hf: e.tensor_copy(out=qv3[:, hf, :, 0:16], in_=r.t[:, 4 * hf:4 * hf + 4, :]),
                                         reads=[r.b], writes=[qkb[bi].b])
                            else:
                                S.op("dve", lambda e: e.tensor_copy(out=qv3[:, 0, :, 0:16], in_=r.t[:, 0:4, :]), reads=[r.b], writes=[qkb[bi].b])
                            if ksub == 4:
                                return
                            if nt == 2:
                                vv = bank.t[:, 256:512].rearrange("p (g2 gp d) -> p g2 gp d", gp=2, d=64)
                                vd = Vp[bi].t[:].rearrange("p (g2 gp) c -> p g2 gp c", gp=2)
                                for gp in range(2):
                                    S.op("act", lambda e, gp=gp: e.activation(out=vd[:, :, gp, gp * 64:gp * 64 + 64], in_=vv[:, :, gp, :], func=AF.Copy),
                                         reads=[bank.b], writes=[Vp[bi].b])
                                if is_out:
                                    kf = kvf[kind]
                                    S.op("act", lambda e: e.activation(out=kf.t[:], in_=bank.t[:], func=AF.Copy), reads=[bank.b], writes=[kf.b])
                                    S.op("dve", lambda e: e.tensor_copy(out=kf.t[:, 0:256].rearrange("p (h d) -> p h d", d=64)[:, :, 0:16],
                                                                        in_=r.t[:, 0:4, :]), reads=[r.b, kf.b], writes=[kf.b])
                                    if kind == 0:
                                        S.dma("sp", lambda e: e.dma_start(out=O["kp"][a], in_=kf.t[:, 0:256]), reads=[kf.b], out=True)
                                        S.dma("sp", lambda e: e.dma_start(out=O["vp"][a], in_=kf.t[:, 256:512]), reads=[kf.b], out=True)
                                    else:
                                        for b in range(16):
                                            S.dma("sp", lambda e, b=b: e.dma_start(out=O["ks"][a][b, 120:128, :], in_=kf.t[8 * b:8 * b + 8, 0:256]),
                                                  reads=[kf.b], out=True)
                                            S.dma("sp", lambda e, b=b: e.dma_start(out=O["vs"][a][b, 120:128, :], in_=kf.t[8 * b:8 * b + 8, 256:512]),
                                                  reads=[kf.b], out=True)
                        return f

                    for nt in range(3):
                        sl, br = w_next("qkv", a, nt)
                        dense_tm(blocks, sl, br, list(range(8)), h_chunk, evac_qkv(nt))
                    if dbg == "attnA1":
                        dump(qkb[1].t[:], 1280, qkb[1].b, bf=True)
                        dump(Vp[1].t[:].rearrange("p g c -> p (g c)"), 512, Vp[1].b, bf=True)
                        stop()
                    attn_pipeline(list(range(npb)), 0, TA)
                    if has_s:
                        S.flush()
                    else:
                        phase_c()
                    if dbg == "attnA":
                        dump(qkb[1].t[:], 1280, qkb[1].b, bf=True)
                        dump(Vp[1].t[:].rearrange("p g c -> p (g c)"), 512, Vp[1].b, bf=True)
                        dump(kT[1].t[:].rearrange("p g c -> p (g c)"), 256, kT[1].b, bf=True)
                        for i in range(2):
                            dump(OT[i].t[:].rearrange("p c t -> p (c t)"), 1024, OT[i].b, bf=True)
                        stop()

                if has_s:
                    with ExitStack() as ph:
                        TB = temps(ph, 2, 1)
                        ckb = SBT(ph, "ckb", [128, 8, 256], BF16)
                        KTs = SBT(ph, "KTs", [128, 2, 16, 128], BF16)
                        Vc = SBT(ph, "Vc", [128, 16, 4, 128], BF16)
                        QmT = [SBT(ph, f"QmT{i}", [128, 16, 128], BF16) for i in range(2)]
                        bmk = SBT(ph, "bmk", [128, 16, 128], BF16)
                        smp.update(QmT=QmT, KTs=KTs, Vc=Vc, bmk=bmk)
                        S.dma("pool", lambda e: e.dma_start(out=bmk.t[:].rearrange("p b t -> p (b t)"), in_=I["bmask"]), writes=[bmk.b])
                        S.op("dve", lambda e: e.memset(Vc.t[:], 0.0), writes=[Vc.b])
                        cvv = I["cv"][a].rearrange("b k (g2 gp d) -> gp b k g2 d", gp=2, d=64)
                        for gp in range(2):
                            for b in range(16):
                                S.dma("pool", lambda e, gp=gp, b=b: e.dma_start(
                                    out=Vc.t[:, b].rearrange("p (g2 gp) c -> p g2 gp c", gp=2)[:, :, gp, gp * 64:gp * 64 + 64], in_=cvv[gp][b]),
                                    writes=[Vc.b])
                        for half in range(2):
                            S.dma("pool", lambda e, half=half: e.dma_start(
                                out=ckb.t[:], in_=I["ck"][a][8 * half:8 * half + 8].rearrange("b k c -> k b c")), writes=[ckb.b])
                            for b8 in range(8):
                                b = 8 * half + b8
                                bank = pb[3 + b % 2]
                                bv = bfview(bank)
                                for kc in range(2):
                                    tr(bv[:, kc, :], ckb.t[:, b8, kc * 128:(kc + 1) * 128], identb.t[:], [ckb.b, identb.b], bank.b)
                                S.op("act", lambda e, b=b, bv=bv: e.activation(out=KTs.t[:, :, b, :], in_=bv[:, 0:2, :], func=AF.Copy),
                                     reads=[bank.b], writes=[KTs.b])
                        for nm_i, nm_o in (("ck", "ks"), ("cv", "vs")):
                            S.dma("sp", lambda e, nm_i=nm_i, nm_o=nm_o: e.dma_start(out=O[nm_o][a][:, 0:120, :], in_=I[nm_i][a][:, 8:128, :]),
                                  out=True, hold=False)
                        attn_pipeline([npb], 1, TB)
                        S.flush()

                if has_s:
                    phase_c()

        def sgu_layer(st, blocks, a, l):
            if 1 in blocks:
                with ExitStack() as ph:
                    build_gates(ph, l, sorted(set(blocks)))
                    norm_phase(ph, blocks, l, 0)
                    S.flush()
            with ExitStack() as ph:
                if 1 not in blocks:
                    build_gates(ph, l, sorted(set(blocks)))
                    norm_phase(ph, blocks, l, 0)
                nb = len(blocks)
                lng = SBT(ph, "lng", [128, 2048], F32)
                lnb = SBT(ph, "lnb", [128, 2048], F32)
                S.dma("sp", lambda e: e.dma_start(out=lng.t[:], in_=I["lnrows"][a].partition_broadcast(128)), writes=[lng.b])
                S.dma("sp", lambda e: e.dma_start(out=lnb.t[:], in_=I["lnrows"][2 + a].partition_broadcast(128)), writes=[lnb.b])
                pTs = [SBT(ph, f"pTs{i}", [128, 16, 128], BF16) for i in range(nb)]
                ub = [SBT(ph, f"ub{i}", [128, 512], BF16) for i in range(2 * nb)]
                vtmp = [SBT(ph, f"vtmp{i}", [128, 512], F32) for i in range(3)]
                vnb = [SBT(ph, f"vnb{i}", [128, 512], BF16) for i in range(4)]
                pg = [SBT(ph, f"pg{i}", [128, 512], BF16) for i in range(4)]
                statsl = [SBT(ph, f"stats{i}", [128, 4, 6], F32) for i in range(nb)]
                mvl = [SBT(ph, f"mv{i}", [128, 4], F32) for i in range(nb)]
                vout = SBT(ph, "vout", [128, 2048], F32) if 1 in blocks else None
                vc = [0]

                def evac_stats(t4):
                    def f(bi, kind, bank):
                        vt = vtmp[vc[0] % 3]
                        vc[0] += 1
                        S.op("act", lambda e: e.activation(out=vt.t[:], in_=bank.t[:], func=AF.Gelu), reads=[bank.b], writes=[vt.b])
                        S.op("dve", lambda e: e.bn_stats(out=statsl[bi].t[:, t4, :], in_=vt.t[:]), reads=[vt.b], writes=[statsl[bi].b])
                        if t4 == 3:
                            S.op("dve", lambda e: e.bn_aggr(out=mvl[bi].t[:, 0:2], in_=statsl[bi].t[:]), reads=[statsl[bi].b], writes=[mvl[bi].b])
                            S.op("act", lambda e: e.activation(out=mvl[bi].t[:, 2:3], in_=mvl[bi].t[:, 1:2], func=AF.Sqrt, scale=1.0, bias=EPS),
                                 reads=[mvl[bi].b], writes=[mvl[bi].b])
                            S.op("dve", lambda e: e.reciprocal(out=mvl[bi].t[:, 3:4], in_=mvl[bi].t[:, 2:3]), reads=[mvl[bi].b], writes=[mvl[bi].b])
                    return f

                for t4 in range(4):
                    sl, br = w_next("win", a, 4 + t4)
                    dense_tm(blocks, sl, br, list(range(8)), h_chunk, evac_stats(t4))

                for g in range(4):
                    def evac_u(bi, kind, bank, g=g):
                        u_ = ub[(g % 2) * nb + bi]
                        S.op("act", lambda e: e.activation(out=u_.t[:], in_=bank.t[:], func=AF.Gelu), reads=[bank.b], writes=[u_.b])

                    def evac_v(bi, kind, bank, g=g):
                        vt = vtmp[vc[0] % 3]
                        vn = vnb[vc[0] % 4]
                        p_ = pg[vc[0] % 4]
                        vc[0] += 1
                        u_ = ub[(g % 2) * nb + bi]
                        S.op("act", lambda e: e.activation(out=vt.t[:], in_=bank.t[:], func=AF.Gelu), reads=[bank.b], writes=[vt.b])
                        S.op("dve", lambda e: e.tensor_scalar(out=vt.t[:], in0=vt.t[:], scalar1=mvl[bi].t[:, 0:1], scalar2=mvl[bi].t[:, 3:4],
                                                              op0=ALU.subtract, op1=ALU.mult), reads=[vt.b, mvl[bi].b], writes=[vt.b])
                        S.op("dve", lambda e: e.tensor_tensor(out=vt.t[:], in0=vt.t[:], in1=lng.t[:, g * 512:(g + 1) * 512], op=ALU.mult),
                             reads=[vt.b, lng.b], writes=[vt.b])
                        if kind == 1:
                            S.op("dve", lambda e: e.tensor_tensor(out=vout.t[:, g * 512:(g + 1) * 512], in0=vt.t[:],
                                                                  in1=lnb.t[:, g * 512:(g + 1) * 512], op=ALU.add),
                                 reads=[vt.b, lnb.b], writes=[vout.b])
                            S.op("dve", lambda e: e.tensor_copy(out=vn.t[:], in_=vout.t[:, g * 512:(g + 1) * 512]), reads=[vout.b], writes=[vn.b])
                        else:
                            S.op("dve", lambda e: e.tensor_tensor(out=vn.t[:], in0=vt.t[:], in1=lnb.t[:, g * 512:(g + 1) * 512], op=ALU.add),
                                 reads=[vt.b, lnb.b], writes=[vn.b])
                        vcv = vc[0]

                        def tail():
                            evac_v_tail(bi, kind, g, vn, p_, u_, vcv)
                        return tail

                    def evac_v_tail(bi, kind, g, vn, p_, u_, vcv):
                        bm = pb[3 + vcv % 2]
                        mm(bm.t[:], wm.t[:, kind, a, g, :], vn.t[:], True, True, [wm.b, vn.b], bm.b)
                        S.op("dve", lambda e: e.scalar_tensor_tensor(out=p_.t[:], in0=bm.t[:], scalar=bsp.t[:, kind, 4 * a + g:4 * a + g + 1],
                                                                     in1=u_.t[:], op0=ALU.add, op1=ALU.mult),
                             reads=[bm.b, bsp.b, u_.b], writes=[p_.b])
                        bt = pb[5 + vcv % 2]
                        btv = bfview(bt)
                        for j in range(4):
                            tr(btv[:, j, :], p_.t[:, j * 128:(j + 1) * 128], identb.t[:], [p_.b, identb.b], bt.b)
                        S.op("act", lambda e: e.activation(out=pTs[bi].t[:, 4 * g:4 * g + 4, :], in_=btv[:, 0:4, :], func=AF.Copy),
                             reads=[bt.b], writes=[pTs[bi].b])

                    sl, br = w_next("win", a, g)
                    dense_tm(blocks, sl, br, list(range(8)), h_chunk, evac_u)
                    sl, br = w_next("win", a, 4 + g)
                    dense_tm(blocks, sl, br, list(range(8)), h_chunk, evac_v)
                if vout is not None:
                    S.dma("sp", lambda e: e.dma_start(out=O["sguv"][a], in_=vout.t[:]), reads=[vout.b], out=True)
                for kh in range(2):
                    for nt in range(2):
                        sl, _ = w_next("wout", a, kh * 2 + nt)
                        dense_tm(blocks, sl, None, list(range(8)), lambda bi, kc, kh=kh: (pTs[bi].t[:, kh * 8 + kc, :], pTs[bi].b),
                                 lambda bi, kind, bank, nt=nt: resid_add(bi, kind, 0, nt, bank))
                S.flush()

        def ffn_layer(st, blocks, l):
            if 1 in blocks:
                with ExitStack() as ph:
                    norm_phase(ph, blocks, l, 1)
                    S.flush()
            with ExitStack() as ph:
                if 1 not in blocks:
                    norm_phase(ph, blocks, l, 1)
                nb = len(blocks)
                npb = sum(1 for k in blocks if k == 0)
                N = npb * 128
                has_s = 1 in blocks
                gT = ph.enter_context(nc.sbuf_tensor(uname("gT"), [128, 22, nb * 128], BF16))
                gTb = Buf("gT")
                nset = 2 if has_s else 3
                ab = [[SBT(ph, f"ab{i}{h}", [128, 2 + SB * 128], F32) for h in range(2)] for i in range(nset)]
                tt = [[SBT(ph, f"tt{i}{h}", [128, SB * 128], F32) for h in range(2)] for i in range(nset)]
                qq = [[SBT(ph, f"qq{i}{h}", [128, SB * 128], F32) for h in range(2)] for i in range(nset)]
                if has_s:
                    abS = [[SBT(ph, f"abS{i}{h}", [128, 16, 10], F32) for h in range(2)] for i in range(2)]
                    ttS = [[SBT(ph, f"ttS{i}{h}", [128, 16, 8], F32) for h in range(2)] for i in range(2)]
                    stS = SBT(ph, "stS", [128, NFC, 32], F32)
                    aLs = SBT(ph, "aLs", [128, NFC, 32], F32)
                    stg = [SBT(ph, "stg0", [32, 1408], F32)] * 2
                    for q4 in range(4):
                        sg = stg[q4 % 2]
                        S.dma("sp", lambda e, q4=q4, sg=sg: e.dma_start(out=sg.t[:], in_=I["sconv"][l][:, q4 * 1408:(q4 + 1) * 1408]),
                              writes=[sg.b])
                        bank = pb[6 + q4 % 2]
                        for j in range(11):
                            mm(bank.t[:, j * 32:(j + 1) * 32], sg.t[:, j * 128:(j + 1) * 128], identf.t[0:32, 0:32], True, True, [sg.b, identf.b], bank.b)
                        S.op("act", lambda e, q4=q4, bank=bank: e.activation(
                            out=stS.t[:, q4 * 11:(q4 + 1) * 11, :].rearrange("p j r -> p (j r)"), in_=bank.t[:, 0:352], func=AF.Copy),
                            reads=[bank.b], writes=[stS.b])
                cnt = [0]
                ffn_pend = []
                for jj in range(11):
                    sl, _ = w_next("wup", l, jj)
                    def chunk(jx):
                        j = 2 * jj + jx
                        par = cnt[0] % nset
                        cnt[0] += 1
                        banks = (pb[0 + 2 * par], pb[1 + 2 * par])
                        bS = pb[4 + par]
                        for h in range(2):
                            wcol = sl.t[:, :, h * 256 + jx * 128:h * 256 + jx * 128 + 128]
                            if npb:
                                for k in range(8):
                                    mm(banks[h].t[:, 0:N], wcol[:, k, :], hT[:, k, 0:N], k == 0, k == 7,
                                       [sl.b] + hTb[0:npb], banks[h].b)
                            if has_s:
                                for k in range(8):
                                    mm(bS.t[:, h * 128:(h + 1) * 128], wcol[:, k, :], hT[:, k, N:N + 128], k == 0, k == 7,
                                       [sl.b, hTb[npb]], bS.b)
                        for h in range(2):
                            fc = j + 22 * h
                            w0 = cwT.t[:, fc, 4 * l + 0:4 * l + 1]
                            w1 = cwT.t[:, fc, 4 * l + 1:4 * l + 2]
                            w2 = cwT.t[:, fc, 4 * l + 2:4 * l + 3]
                            cb = cwT.t[:, fc, 4 * l + 3:4 * l + 4]
                            if npb:
                                a_ = ab[par][h]
                                t_ = tt[par][h]
                                S.op("act", lambda e, a_=a_, h=h: e.activation(out=a_.t[:, 2:2 + N], in_=banks[h].t[:, 0:N], func=AF.Copy),
                                     reads=[banks[h].b], writes=[a_.b])
                                if h == 0:
                                    S.op("dve", lambda e, a_=a_, fc=fc: e.tensor_copy(out=a_.t[:, 0:2], in_=stP.t[:, l, fc, :]), reads=[stPb[l][fc]], writes=[a_.b])
                                    S.op("dve", lambda e, a_=a_, t_=t_, w2=w2, cb=cb: e.tensor_scalar(
                                        out=t_.t[:, 0:N], in0=a_.t[:, 2:2 + N], scalar1=w2, scalar2=cb, op0=ALU.mult, op1=ALU.add),
                                        reads=[a_.b, cwT.b], writes=[t_.b])
                                    S.op("dve", lambda e, a_=a_, t_=t_, w1=w1: e.scalar_tensor_tensor(
                                        out=t_.t[:, 0:N], in0=a_.t[:, 1:1 + N], scalar=w1, in1=t_.t[:, 0:N], op0=ALU.mult, op1=ALU.add),
                                        reads=[a_.b, cwT.b, t_.b], writes=[t_.b])
                                    S.op("dve", lambda e, a_=a_, t_=t_, w0=w0: e.scalar_tensor_tensor(
                                        out=t_.t[:, 0:N], in0=a_.t[:, 0:N], scalar=w0, in1=t_.t[:, 0:N], op0=ALU.mult, op1=ALU.add),
                                        reads=[a_.b, cwT.b, t_.b], writes=[t_.b])
                                    S.op("dve", lambda e, a_=a_, fc=fc: e.tensor_copy(out=stP.t[:, l, fc, :], in_=a_.t[:, N:N + 2]), reads=[a_.b], writes=[stPb[l][fc]])
                                else:
                                    q1, q0 = qq[par]
                                    S.op("pool", lambda e, a_=a_, fc=fc: e.tensor_copy(out=a_.t[:, 0:2], in_=stP.t[:, l, fc, :]), reads=[stPb[l][fc]], writes=[a_.b])
                                    S.op("act", lambda e, a_=a_, q1=q1, w1=w1: e.activation(out=q1.t[:, 0:N], in_=a_.t[:, 1:1 + N], func=AF.Copy, scale=w1),
                                         reads=[a_.b, cwT.b], writes=[q1.b])
                                    S.op("act", lambda e, a_=a_, q0=q0, w0=w0: e.activation(out=q0.t[:, 0:N], in_=a_.t[:, 0:N], func=AF.Copy, scale=w0),
                                         reads=[a_.b, cwT.b], writes=[q0.b])
                                    S.op("act", lambda e, a_=a_, t_=t_, w2=w2, cb=cb: e.activation(
                                        out=t_.t[:, 0:N], in_=a_.t[:, 2:2 + N], func=AF.Identity, scale=w2, bias=cb),
                                        reads=[a_.b, cwT.b], writes=[t_.b])
                                    S.op("dve", lambda e, t_=t_, q1=q1: e.tensor_tensor(out=t_.t[:, 0:N], in0=t_.t[:, 0:N], in1=q1.t[:, 0:N], op=ALU.add),
                                         reads=[q1.b, t_.b], writes=[t_.b])
                                    S.op("dve", lambda e, t_=t_, q0=q0: e.tensor_tensor(out=t_.t[:, 0:N], in0=t_.t[:, 0:N], in1=q0.t[:, 0:N], op=ALU.add),
                                         reads=[q0.b, t_.b], writes=[t_.b])
                                    S.op("pool", lambda e, a_=a_, fc=fc: e.tensor_copy(out=stP.t[:, l, fc, :], in_=a_.t[:, N:N + 2]), reads=[a_.b], writes=[stPb[l][fc]])
                            if has_s:
                                a_ = abS[par][h]
                                t_ = ttS[par][h]
                                S.op("act", lambda e, a_=a_, h=h: e.activation(
                                    out=a_.t[:, :, 2:10], in_=bS.t[:, h * 128:(h + 1) * 128].rearrange("p (b t) -> p b t", t=8), func=AF.Copy),
                                    reads=[bS.b], writes=[a_.b])
                                S.op("dve", lambda e, a_=a_, fc=fc: e.tensor_copy(
                                    out=a_.t[:, :, 0:2], in_=stS.t[:, fc, :].rearrange("p (b t) -> p b t", t=2)), reads=[stS.b], writes=[a_.b])
                                S.op("dve", lambda e, a_=a_, t_=t_, w2=w2, cb=cb: e.tensor_scalar(
                                    out=t_.t[:], in0=a_.t[:, :, 2:10], scalar1=w2, scalar2=cb, op0=ALU.mult, op1=ALU.add),
                                    reads=[a_.b, cwT.b], writes=[t_.b])
                                S.op("dve", lambda e, a_=a_, t_=t_, w1=w1: e.scalar_tensor_tensor(
                                    out=t_.t[:], in0=a_.t[:, :, 1:9], scalar=w1, in1=t_.t[:], op0=ALU.mult, op1=ALU.add),
                                    reads=[a_.b, cwT.b, t_.b], writes=[t_.b])
                                S.op("dve", lambda e, a_=a_, t_=t_, w0=w0: e.scalar_tensor_tensor(
                                    out=t_.t[:], in0=a_.t[:, :, 0:8], scalar=w0, in1=t_.t[:], op0=ALU.mult, op1=ALU.add),
                                    reads=[a_.b, cwT.b, t_.b], writes=[t_.b])
                                S.op("dve", lambda e, a_=a_, fc=fc: e.tensor_copy(
                                    out=aLs.t[:, fc, :].rearrange("p (b t) -> p b t", t=2), in_=a_.t[:, :, 8:10]), reads=[a_.b], writes=[aLs.b])
                        def tail():
                            if npb:
                                tg, tu = tt[par][0], tt[par][1]
                                S.op("act", lambda e: e.activation(out=tg.t[:, 0:N], in_=tg.t[:, 0:N], func=AF.Silu), reads=[tg.b], writes=[tg.b])
                                S.op("dve", lambda e: e.tensor_tensor(out=gT[:, j, 0:N], in0=tg.t[:, 0:N], in1=tu.t[:, 0:N], op=ALU.mult),
                                     reads=[tg.b, tu.b], writes=[gTb])
                            if has_s:
                                tgs, tus = ttS[par][0], ttS[par][1]
                                S.op("act", lambda e: e.activation(out=tgs.t[:], in_=tgs.t[:], func=AF.Silu), reads=[tgs.b], writes=[tgs.b])
                                S.op("dve", lambda e: e.tensor_tensor(
                                    out=gT[:, j, N:N + 128].rearrange("p (b t) -> p b t", t=8), in0=tgs.t[:], in1=tus.t[:], op=ALU.mult),
                                    reads=[tgs.b, tus.b], writes=[gTb])
                        return tail

                    for jx in range(2):
                        tl = chunk(jx)
                        if has_s:
                            tl()
                        else:
                            if ffn_pend:
                                ffn_pend.pop(0)()
                            ffn_pend.append(tl)
                while ffn_pend:
                    ffn_pend.pop(0)()
                for kg in range(3):
                    nk = 8 if kg < 2 else 6
                    for nt in range(2):
                        sl, _ = w_next("wdown", l, kg * 2 + nt)
                        dense_tm(blocks, sl, None, list(range(nk)),
                                 lambda bi, kc, kg=kg: (gT[:, kg * 8 + kc, bi * 128:(bi + 1) * 128], gTb),
                                 lambda bi, kind, bank, nt=nt: resid_add(bi, kind, 1, nt, bank, "dve"), banks=(6, 7))
                if has_s:
                    for c0 in range(0, NFC, 4):
                        bank = pb[(c0 // 4) % 2]
                        sg = stg[(c0 // 4) % 2]
                        for j in range(4):
                            mm(bank.t[0:32, j * 128:(j + 1) * 128], aLs.t[:, c0 + j, :], identf.t[:], True, True, [aLs.b, identf.b], bank.b)
                        S.op("act", lambda e, bank=bank, sg=sg: e.activation(out=sg.t[:, 0:512], in_=bank.t[0:32, :], func=AF.Copy),
                             reads=[bank.b], writes=[sg.b])
                        S.dma("sp", lambda e, sg=sg, c0=c0: e.dma_start(out=O["convs"][l][:, c0 * 128:(c0 + 4) * 128], in_=sg.t[:, 0:512]),
                              reads=[sg.b], out=True)
                S.flush()

        for st in range(NST):
            blocks = [0] * SB + ([1] if st == NST - 1 else [])
            with ExitStack() as ph:
                for bi, kind in enumerate(blocks):
                    src = I["xp"][(st * SB + bi) * 128:(st * SB + bi + 1) * 128, :] if kind == 0 else I["xs"]
                    S.dma("sp", lambda e, bi=bi, src=src: e.dma_start(out=x[bi].t[:], in_=src), writes=[x[bi].b])
                S.flush()
            for l in range(4):
                if l % 2 == 0:
                    attn_layer(st, blocks, l // 2, l)
                else:
                    sgu_layer(st, blocks, l // 2, l)
                if dbg == f"mix{l}" and st == 0:
                    for i in range(4):
                        dump(x[i].t[:], 1024, x[i].b)
                    stop()
                ffn_layer(st, blocks, l)
                if dbg == f"ffn{l}" and st == 0:
                    for i in range(4):
                        dump(x[i].t[:], 1024, x[i].b)
                    S.op("dve", lambda e: e.engine_nop(), reads=stP_all, writes=[stP.b])
                    dump(stP.t[:].rearrange("p l c t -> p (l c t)"), 352, stP.b)
                    stop()
            with ExitStack() as ph:
                ssl = [SBT(ph, f"fss{i}", [128, 4], F32) for i in range(NBMAX)]
                junk = SBT(ph, "fjunk", [128, D], BF16)
                yo = [SBT(ph, f"yo{i}", [128, D], F32) for i in range(2)]
                for bi, kind in enumerate(blocks):
                    S.op("act", lambda e, bi=bi: e.activation(out=junk.t[:], in_=x[bi].t[:], func=AF.Square, accum_out=ssl[bi].t[:, 0:1]),
                         reads=[x[bi].b], writes=[junk.b, ssl[bi].b])
                    S.op("act", lambda e, bi=bi: e.activation(out=ssl[bi].t[:, 1:2], in_=ssl[bi].t[:, 0:1], func=AF.Sqrt,
                                                              scale=1.0 / D, bias=EPS), reads=[ssl[bi].b], writes=[ssl[bi].b])
                    S.op("dve", lambda e, bi=bi: e.reciprocal(out=ssl[bi].t[:, 2:3], in_=ssl[bi].t[:, 1:2]),
                         reads=[ssl[bi].b], writes=[ssl[bi].b])
                    y_ = yo[bi % 2]
                    S.op("dve", lambda e, bi=bi, y_=y_: e.scalar_tensor_tensor(out=y_.t[:], in0=x[bi].t[:], scalar=ssl[bi].t[:, 2:3],
                                                                              in1=normfB.t[:], op0=ALU.mult, op1=ALU.mult),
                         reads=[x[bi].b, ssl[bi].b, normfB.b], writes=[y_.b])
                    dst = O["yp"][(st * SB + bi) * 128:(st * SB + bi + 1) * 128, :] if kind == 0 else O["ys"]
                    S.dma("sp", lambda e, y_=y_, dst=dst: e.dma_start(out=dst, in_=y_.t[:]), reads=[y_.b], out=True)
                S.flush()

        with ExitStack() as ph:
            stg2 = [SBT(ph, f"stg2{i}", [2, 512], F32) for i in range(2)]
            for l in range(4):
                for c0 in range(0, NFC, 4):
                    i = (l * 11 + c0 // 4) % 2
                    bank = pb[i]
                    for j in range(4):
                        mm(bank.t[0:2, j * 128:(j + 1) * 128], stP.t[:, l, c0 + j, :], identf.t[:], True, True, [stPb[l][c0 + j], identf.b], bank.b)
                    S.op("act", lambda e, bank=bank, i=i: e.activation(out=stg2[i].t[:], in_=bank.t[0:2, :], func=AF.Copy),
                         reads=[bank.b], writes=[stg2[i].b])
                    S.dma("sp", lambda e, i=i, l=l, c0=c0: e.dma_start(out=O["convp"][l][:, c0 * 128:(c0 + 4) * 128], in_=stg2[i].t[:]),
                          reads=[stg2[i].b], out=True)
            S.flush(final=True)
        assert wstate["next"] == len(wseq)
    return nc


_CACHE = {}


def _consts():
    i = np.arange(128)[:, None]
    j = np.arange(128)[None, :]
    ninf = np.float32(NEG)
    prev = np.where(j >= i, 0.0, ninf)
    own = np.where(j <= i, 0.0, ninf)
    maskP = np.concatenate([prev, own], 1).astype(np.float32)
    maskP0 = np.concatenate([np.full((128, 128), ninf), own], 1).astype(np.float32)
    t = i % 8
    cachem = np.where(j >= t, 0.0, ninf)
    newm = np.where((j // 8 == i // 8) & (j % 8 <= t), 0.0, ninf)
    maskS = np.concatenate([cachem, newm], 1).astype(np.float32)
    trilT = (i <= j).astype(np.float32)
    bm = (np.arange(128)[None, :] // 8 == np.arange(16)[:, None]).astype(np.float32).reshape(1, 16 * 128)
    bmask = np.ascontiguousarray(np.broadcast_to(bm, (128, 16 * 128)))
    return dict(ident=np.eye(128, dtype=np.float32), maskP=maskP, maskP0=maskP0, maskS=maskS, trilT=trilT, bmask=bmask)


def _cossin(pos):
    half = 8
    inv = (np.float32(500000.0) ** (-np.arange(0, 16, 2, dtype=np.float32) / np.float32(16))).astype(np.float32)
    ang = pos.astype(np.float32)[:, None] * inv[None, :]
    return np.concatenate([np.cos(ang), np.sin(ang)], 1).astype(np.float32)


def _prep(x_prompt, x_sample, c_prompt, c_sample, cache_k, cache_v, state_conv,
          w_ada, b_ada, norm_mix, norm_ffn, w_qkv, b_qkv, attn_sink, w_o,
          w_sgu_in, b_sgu_in, sgu_ln_g, sgu_ln_b, w_spatial, b_spatial, w_sgu_out,
          w_up, conv_w, conv_b, w_down, norm_final):
    f = lambda a: np.ascontiguousarray(np.asarray(a, dtype=np.float32))
    x_prompt, x_sample, c_prompt, c_sample = f(x_prompt), f(x_sample), f(c_prompt), f(c_sample)
    cache_k, cache_v, state_conv = f(cache_k), f(cache_v), f(state_conv)
    consts = _consts()
    w_spatial = f(w_spatial)
    b_spatial = f(b_spatial)
    wspT = np.ascontiguousarray(w_spatial.transpose(0, 1, 3, 2))
    wspST = np.zeros((2, 4, 128, 128), np.float32)
    for b in range(16):
        wspST[:, :, 8 * b:8 * b + 8, 8 * b:8 * b + 8] = wspT[:, :, 0:8, 0:8]
    bspP = np.ascontiguousarray(b_spatial.transpose(2, 0, 1).reshape(128, 8))
    bspS = np.ascontiguousarray(bspP[np.arange(128) % 8])
    perm = [0, 1, 4, 5, 2, 3, 6, 7, 8, 9, 12, 13, 10, 11, 14, 15]
    def tiles_k(W):
        L, K, N = W.shape
        t = W.reshape(L, K // 1024, 8, 128, N // 512, 512).transpose(0, 1, 4, 3, 2, 5)
        return np.ascontiguousarray(t.reshape(L, (K // 1024) * (N // 512), 128, 4096))

    w_o_ = f(w_o).reshape(2, 2, 2, 4, 64, D).transpose(0, 2, 4, 1, 3, 5).reshape(2, 128, 8, 2, 512)
    wt_o = np.ascontiguousarray(w_o_.transpose(0, 3, 1, 2, 4).reshape(2, 2, 128, 4096))
    w_up_ = f(w_up)
    wu = np.concatenate([w_up_[:, :, :DFF].reshape(4, D, 11, 256), w_up_[:, :, DFF:].reshape(4, D, 11, 256)], 3)
    wt_up = np.ascontiguousarray(wu.reshape(4, 8, 128, 11, 512).transpose(0, 3, 2, 1, 4).reshape(4, 11, 128, 4096))
    w_dn = np.zeros((4, 3072, D), np.float32)
    w_dn[:, :DFF] = f(w_down)
    shared = dict(
        wt_ada=tiles_k(f(w_ada)), b_ada=f(b_ada), rows8=np.concatenate([f(norm_mix), f(norm_ffn)], 0), normf=f(norm_final).reshape(1, D),
        wt_qkv=tiles_k(f(w_qkv)), b_qkv=f(b_qkv), sinkp=np.ascontiguousarray(f(attn_sink)[:, perm]), wt_o=wt_o,
        wt_in=tiles_k(f(w_sgu_in)), b_in=f(b_sgu_in), lnrows=np.concatenate([f(sgu_ln_g), f(sgu_ln_b)], 0),
        wspT=wspT, wspST=wspST, bspP=bspP, bspS=bspS, wt_out=tiles_k(f(w_sgu_out)), wt_up=wt_up,
        convrows=np.ascontiguousarray(np.concatenate([f(conv_w), f(conv_b)[:, None, :]], 1).reshape(16, 2 * DFF)),
        wt_down=tiles_k(w_dn), **consts)
    in_maps = []
    for c in range(8):
        seq, r = c // 4, c % 4
        p0 = PROC_START[r]
        pos = np.concatenate([np.arange(p0, p0 + NPB * 128), 8192 + (np.arange(128) % 8)])
        m = dict(shared)
        m.update(
            xp=np.ascontiguousarray(x_prompt[seq, p0:p0 + NPB * 128]),
            xs=np.ascontiguousarray(x_sample[16 * c:16 * c + 16].reshape(128, D)),
            c17=np.ascontiguousarray(np.concatenate([c_prompt[seq:seq + 1], c_sample[16 * c:16 * c + 16]], 0)),
            ck=np.ascontiguousarray(cache_k[:, 16 * c:16 * c + 16].reshape(2, 16, 128, 256)),
            cv=np.ascontiguousarray(cache_v[:, 16 * c:16 * c + 16].reshape(2, 16, 128, 256)),
            sconv=np.ascontiguousarray(state_conv[:, 16 * c:16 * c + 16].reshape(4, 32, 2 * DFF)),
            cossin=_cossin(pos),
        )
        in_maps.append(m)
    return in_maps


def kernel(**inputs):
    in_maps = _prep(**inputs)
    if "nc" not in _CACHE:
        _CACHE["nc"] = build_program()
    nc = _CACHE["nc"]
    res = run_bass_kernel_spmd(nc, in_maps, core_ids=list(range(8)))
    R = res.results
    y_prompt = np.zeros((2, 8192, D), np.float32)
    k_p = np.zeros((2, 2, 128, 4, 64), np.float32)
    v_p = np.zeros((2, 2, 128, 4, 64), np.float32)
    conv_p = np.zeros((4, 2, 2, 2 * DFF), np.float32)
    for c in range(8):
        seq, r = c // 4, c % 4
        p0 = PROC_START[r]
        y_prompt[seq, OWN_START[r]:OWN_END[r]] = R[c]["yp"][OWN_START[r] - p0:OWN_END[r] - p0]
        if r == 3:
            k_p[:, seq] = R[c]["kp"].reshape(2, 128, 4, 64)
            v_p[:, seq] = R[c]["vp"].reshape(2, 128, 4, 64)
            conv_p[:, seq] = R[c]["convp"]
    y_sample = np.concatenate([R[c]["ys"].reshape(16, 8, D) for c in range(8)], 0)
    k_s = np.concatenate([R[c]["ks"].reshape(2, 16, 128, 4, 64) for c in range(8)], 1)
    v_s = np.concatenate([R[c]["vs"].reshape(2, 16, 128, 4, 64) for c in range(8)], 1)
    conv_s = np.concatenate([R[c]["convs"].reshape(4, 16, 2, 2 * DFF) for c in range(8)], 1)
    sgu_v = np.concatenate([R[c]["sguv"].reshape(2, 16, 8, 2048) for c in range(8)], 1)
    return (y_prompt, y_sample, k_p, v_p, conv_p, k_s, v_s, conv_s, sgu_v)
```
